# Optimizing a Trainium2 kernel written in Bass

```python
import jax, jax.numpy as jnp
from jax import lax
import numpy as np

D_MODEL = 1024
BATCH = 8
SEQ = 8192
DEPTH = 1

HEAD_DIM = 64
N_FOX_HEADS = 8
N_SB_HEADS = 8
FOX_WIDTH = N_FOX_HEADS * HEAD_DIM
SB_WIDTH = N_SB_HEADS * HEAD_DIM
IN_COLS = 3 * FOX_WIDTH + N_FOX_HEADS + 3 * SB_WIDTH + 2 * D_MODEL
D_FF = 2816
CONV_WIDTH = 3
Q_BLOCK = 128
LN_EPS = 1e-5
N_MOD = 6
DEEPNORM_ALPHA = (2.0 * DEPTH) ** 0.25
DEEPNORM_BETA = (8.0 * DEPTH) ** -0.25
ATTN_SCALE = HEAD_DIM ** -0.5

kernel_name = "fox_stickbreak_gated_hybrid_deepnorm_adaln"


def _split_points():
    cols = [FOX_WIDTH] * 3 + [N_FOX_HEADS] + [SB_WIDTH] * 3 + [D_MODEL] * 2
    return [int(v) for v in np.cumsum(cols)[:-1]]


def _layer_norm(x, g, b):
    xf = x.astype(jnp.float32)
    mu = jnp.mean(xf, axis=-1, keepdims=True)
    var = jnp.mean(jnp.square(xf - mu), axis=-1, keepdims=True)
    y = (xf - mu) * lax.rsqrt(var + LN_EPS)
    return (y * g.astype(jnp.float32) + b.astype(jnp.float32)).astype(x.dtype)


def _to_heads(t, n_heads):
    b, s, _ = t.shape
    return t.reshape(b, s, n_heads, HEAD_DIM).transpose(0, 2, 1, 3)


def _merge_heads(t):
    b, h, s, d = t.shape
    return t.transpose(0, 2, 1, 3).reshape(b, s, h * d)


def _blocks(t):
    b, h, s = t.shape[:3]
    nb = s // Q_BLOCK
    t = t.reshape((b, h, nb, Q_BLOCK) + t.shape[3:])
    return jnp.moveaxis(t, 2, 0)


def _unblocks(t):
    nb, b, h, qb, d = t.shape
    return jnp.moveaxis(t, 0, 2).reshape(b, h, nb * qb, d)


def _forgetting_attention(q, k, v, log_f):
    s_len = q.shape[2]
    cum = jnp.cumsum(log_f, axis=-1)
    kpos = jnp.arange(s_len)

    def block(args):
        qi, cqi, bi = args
        qpos = bi * Q_BLOCK + jnp.arange(Q_BLOCK)
        logits = (jnp.einsum('bhqd,bhkd->bhqk', qi, k).astype(jnp.float32) * ATTN_SCALE
                  + cqi[..., None] - cum[:, :, None, :])
        causal = kpos[None, :] <= qpos[:, None]
        logits = jnp.where(causal, logits, -jnp.inf)
        p = jax.nn.softmax(logits, axis=-1)
        return jnp.einsum('bhqk,bhkd->bhqd', p.astype(v.dtype), v)

    nb = s_len // Q_BLOCK
    out = lax.map(block, (_blocks(q), _blocks(cum), jnp.arange(nb, dtype=jnp.int32)))
    return _unblocks(out)


def _stick_breaking_attention(q, k, v):
    s_len = q.shape[2]
    kpos = jnp.arange(s_len)

    def block(args):
        qi, bi = args
        qpos = bi * Q_BLOCK + jnp.arange(Q_BLOCK)
        z = jnp.einsum('bhqd,bhkd->bhqk', qi, k).astype(jnp.float32) * ATTN_SCALE
        strict = kpos[None, :] < qpos[:, None]
        log_beta = jax.nn.log_sigmoid(z)
        log_one_minus = jnp.where(strict, jax.nn.log_sigmoid(-z), 0.0)
        rest = lax.cumsum(log_one_minus, axis=3, reverse=True) - log_one_minus
        a = jnp.where(strict, jnp.exp(log_beta + rest), 0.0)
        return jnp.einsum('bhqk,bhkd->bhqd', a.astype(v.dtype), v)

    nb = s_len // Q_BLOCK
    out = lax.map(block, (_blocks(q), jnp.arange(nb, dtype=jnp.int32)))
    return _unblocks(out)


def _causal_depthwise_conv(h, w, b):
    s_len = h.shape[1]
    hp = jnp.pad(h, ((0, 0), (CONV_WIDTH - 1, 0), (0, 0)))
    out = b
    for tap in range(CONV_WIDTH):
        out = out + hp[:, tap:tap + s_len, :] * w[tap]
    return out


def setup_inputs(seed: int = 0) -> dict:
    key = jax.random.key(seed)
    ks = jax.random.split(key, 17)
    f32 = jnp.float32
    d = D_MODEL
    nrm = lambda k, shape, s: jax.random.normal(k, shape, f32) * s
    return {
        "x": nrm(ks[0], (BATCH, SEQ, d), 1.0),
        "c": nrm(ks[1], (BATCH, d), 1.0),
        "w_ada": nrm(ks[2], (d, N_MOD * d), 0.1 * d ** -0.5),
        "b_ada": nrm(ks[3], (N_MOD * d,), 0.01),
        "w_in": nrm(ks[4], (d, IN_COLS), d ** -0.5),
        "b_forget": jnp.linspace(1.0, 5.0, N_FOX_HEADS, dtype=f32) + nrm(ks[5], (N_FOX_HEADS,), 0.1),
        "w_fox_proj": nrm(ks[6], (FOX_WIDTH, d), FOX_WIDTH ** -0.5),
        "w_sb_proj": nrm(ks[7], (SB_WIDTH, d), SB_WIDTH ** -0.5),
        "w_o": nrm(ks[8], (d, d), DEEPNORM_BETA * d ** -0.5),
        "ln1_g": 1.0 + nrm(ks[9], (d,), 0.02),
        "ln1_b": nrm(ks[10], (d,), 0.02),
        "w_up": nrm(ks[11], (d, 2 * D_FF), d ** -0.5),
        "conv_w": nrm(ks[12], (CONV_WIDTH, 2 * D_FF), CONV_WIDTH ** -0.5),
        "conv_b": nrm(ks[13], (2 * D_FF,), 0.01),
        "w_down": nrm(ks[14], (D_FF, d), DEEPNORM_BETA * D_FF ** -0.5),
        "ln2_g": 1.0 + nrm(ks[15], (d,), 0.02),
        "ln2_b": nrm(ks[16], (d,), 0.02),
    }


def reference(x, c, w_ada, b_ada, w_in, b_forget, w_fox_proj, w_sb_proj, w_o,
              ln1_g, ln1_b, w_up, conv_w, conv_b, w_down, ln2_g, ln2_b):
    for _ in range(DEPTH):
        mod = c @ w_ada + b_ada
        sh1, sc1, gt1, sh2, sc2, gt2 = [m[:, None, :] for m in jnp.split(mod, N_MOD, axis=-1)]

        u = x * (1.0 + sc1) + sh1
        proj = u @ w_in
        q_a, k_a, v_a, f_a, q_b, k_b, v_b, g_a, g_b = jnp.split(proj, _split_points(), axis=-1)

        log_f = jax.nn.log_sigmoid((f_a + b_forget).astype(jnp.float32)).transpose(0, 2, 1)
        y_fox = _merge_heads(_forgetting_attention(
            _to_heads(q_a, N_FOX_HEADS), _to_heads(k_a, N_FOX_HEADS), _to_heads(v_a, N_FOX_HEADS), log_f))
        y_sb = _merge_heads(_stick_breaking_attention(
            _to_heads(q_b, N_SB_HEADS), _to_heads(k_b, N_SB_HEADS), _to_heads(v_b, N_SB_HEADS)))

        merged = jax.nn.sigmoid(g_a) * (y_fox @ w_fox_proj) + jax.nn.sigmoid(g_b) * (y_sb @ w_sb_proj)
        attn_out = merged @ w_o
        x = _layer_norm(DEEPNORM_ALPHA * x + (1.0 + gt1) * attn_out, ln1_g, ln1_b)

        u2 = x * (1.0 + sc2) + sh2
        h = _causal_depthwise_conv(u2 @ w_up, conv_w, conv_b)
        h_gate, h_val = jnp.split(h, 2, axis=-1)
        ffn_out = (jax.nn.silu(h_gate) * h_val) @ w_down
        x = _layer_norm(DEEPNORM_ALPHA * x + (1.0 + gt2) * ffn_out, ln2_g, ln2_b)
    return x
```

```python
import numpy as np
from contextlib import ExitStack
import concourse.bass as bass
import concourse.mybir as mybir
from concourse.bass_utils import run_bass_kernel_spmd

F32 = mybir.dt.float32
BF16 = mybir.dt.bfloat16
AF = mybir.ActivationFunctionType
ALU = mybir.AluOpType
AX = mybir.AxisListType

D = 1024
DFF = 2816
NH = 8
DH = 64
NFC = 2 * DFF // 128
NAC = DFF // 128
ALPHA = 2.0 ** 0.25
EPS = 1e-5


class Sync:
    def __init__(self, nc, st):
        self.nc = nc
        self.st = st
        self.nsem = 0
        self.engs = {"sp": nc.sync, "act": nc.scalar, "dve": nc.vector, "pool": nc.gpsimd, "pe": nc.tensor}
        self.cs = {e: [self._newsem(), 0] for e in self.engs}
        self.waited = {e: {} for e in self.engs}
        self.lastw = {}
        self.readers = {}
        self.NDS = {"sp": 8, "pool": 40}
        self.dsem = {e: [self._newsem() for _ in range(self.NDS[e])] for e in ("sp", "pool")}
        self.dcnt = {e: [0] * self.NDS[e] for e in ("sp", "pool")}
        self.dn = {e: 0 for e in ("sp", "pool")}

    def _newsem(self):
        self.nsem += 1
        return self.st.enter_context(self.nc.semaphore(f"sy{self.nsem}"))

    def _wait(self, eng, sem, val):
        w = self.waited[eng]
        if w.get(sem, 0) >= val:
            return
        self.engs[eng].wait_ge(sem, val)
        w[sem] = val

    def op(self, eng, fn, reads=(), writes=(), dma=False):
        deps = []
        for k in reads:
            lw = self.lastw.get(k)
            if lw is not None:
                if lw[2] != eng or lw[3] or eng in ("act", "dve", "pool"):
                    deps.append(lw)
        for k in writes:
            lw = self.lastw.get(k)
            if lw is not None and (lw[2] != eng or lw[3] or eng in ("act", "dve", "pool")):
                deps.append(lw)
            for r in self.readers.get(k, {}).values():
                if r[2] != eng or r[3] or eng in ("act", "dve", "pool"):
                    deps.append(r)
        for d in deps:
            self._wait(eng, d[0], d[1])
        e = self.engs[eng]
        if dma:
            i = self.dn[eng] % self.NDS[eng]
            self.dn[eng] += 1
            sem = self.dsem[eng][i]
            if self.dcnt[eng][i] > 0:
                self._wait(eng, sem, self.dcnt[eng][i])
            ins = fn(e)
            self.dcnt[eng][i] += 16
            ins.then_inc(sem, 16)
            ref = (sem, self.dcnt[eng][i], eng, True)
            rkey = ("dma", eng, i)
        else:
            c = self.cs[eng]
            if c[1] >= 30000:
                c[0] = self._newsem()
                c[1] = 0
            ins = fn(e)
            c[1] += 1
            ins.then_inc(c[0], 1)
            ref = (c[0], c[1], eng, False)
            rkey = eng
        for k in reads:
            self.readers.setdefault(k, {})[rkey] = ref
        for k in writes:
            self.lastw[k] = ref
            self.readers[k] = {}
        return ref

    def barrier(self):
        refs = [(c[0], c[1]) for c in self.cs.values() if c[1] > 0]
        for e in self.dsem:
            for i in range(self.NDS[e]):
                if self.dcnt[e][i] > 0:
                    refs.append((self.dsem[e][i], self.dcnt[e][i]))
        for eng in self.engs:
            for (s, v) in refs:
                if s is self.cs[eng][0]:
                    continue
                self._wait(eng, s, v)
        self.lastw.clear()
        self.readers.clear()


def build_nc(S, dbg=False):
    assert S % 512 == 0
    NT = S // 512
    NB = S // 128
    nc = bass.Bass("TRN2", target_bir_lowering=False)
    okind = "ExternalOutput" if dbg else "Internal"

    def din(name, shape, dt=F32):
        return nc.dram_tensor(name, list(shape), dt, kind="ExternalInput").ap()

    x_d = din("x", [S, D])
    cT_d = din("cT", [128, 8])
    wada_d = din("w_ada", [D, 6 * D])
    bada_d = din("b_ada", [12, 1, 512])
    win_d = din("w_in", [D, 5128])
    bft_d = din("bft", [128, 32])
    wfox_d = din("w_fox", [512, D])
    wsb_d = din("w_sb", [512, D])
    wo_d = din("w_o", [D, D])
    lnt_d = din("lnt", [4, 128, D])
    wup_d = din("w_up", [D, 2 * DFF])
    convp_d = din("convp", [128, NFC, 4])
    wdown_d = din("w_down", [DFF, D])
    consts_d = din("consts", [8, 128, 128])
    out_d = nc.dram_tensor("out", [S, D], F32, kind="ExternalOutput").ap()

    fm_d = nc.dram_tensor("fm_s", [32, 128, S], BF16, kind=okind).ap()
    v_d = nc.dram_tensor("v_s", [2, S, 512], BF16, kind=okind).ap()
    caug_d = nc.dram_tensor("caug_s", [8, 3, S], BF16, kind=okind).ap()
    yT_d = nc.dram_tensor("yT_s", [2, 512, S], BF16, kind=okind).ap()
    x1_d = nc.dram_tensor("x1_s", [S, D], F32, kind=okind).ap()
    dv_d = nc.dram_tensor("dv_s", [S, 512], BF16, kind=okind).ap()
    gt_d = nc.dram_tensor("gt_s", [2, 128, D], F32).ap()

    with ExitStack() as st:
        E = st.enter_context
        sy = Sync(nc, st)
        op = sy.op
        ps = [E(nc.psum_tensor(f"psb{i}", [128, 512], F32)) for i in range(8)]

        identf = E(nc.sbuf_tensor("identf", [128, 128], F32))
        mhalf = E(nc.sbuf_tensor("mhalf", [128, 1], F32))
        modp = E(nc.sbuf_tensor("modp", [128, 4, 8], F32))
        p012 = ExitStack()
        E2 = p012.enter_context
        cf = E2(nc.sbuf_tensor("cf", [128, 8, 128], F32))
        cb = E2(nc.sbuf_tensor("cb", [128, 8, 128], BF16))
        ctok = E2(nc.sbuf_tensor("ctok", [128, NB, 8], F32))
        cql = E2(nc.sbuf_tensor("cql", [128, NT + 1, 8], F32))
        IDENT, UINC, LINC, LSTR, USTR, ONES, DM, EB = range(8)

        op("sp", lambda e: e.dma_start(out=identf[:], in_=consts_d[0]), writes=["identf"], dma=True)
        op("dve", lambda e: e.memset(mhalf[:], -0.5), writes=["mhalf"])
        op("sp", lambda e: e.dma_start(out=cf[:], in_=consts_d.rearrange("c p n -> p c n")), writes=["cf"], dma=True)
        op("pool", lambda e: e.dma_start(out=cb[:], in_=consts_d.rearrange("c p n -> p c n")), writes=["cb"], dma=True)

        with ExitStack() as p0:
            P = p0.enter_context
            cts = P(nc.sbuf_tensor("cts", [128, 8], F32))
            modtab = P(nc.sbuf_tensor("modtab", [128, 6 * D], F32))
            crep = P(nc.sbuf_tensor("crep", [128, 8, 128], F32))
            wa = [P(nc.sbuf_tensor(f"wa{i}", [128, 8, 512], F32)) for i in range(2)]
            bad = [P(nc.sbuf_tensor(f"bad{i}", [1, 512], F32)) for i in range(2)]
            dtmp = P(nc.sbuf_tensor("dtmp", [128, 8, 128], F32))
            op("sp", lambda e: e.dma_start(out=cts[:], in_=cT_d), writes=["cts"], dma=True)
            for kc in range(8):
                op("dve", lambda e, kc=kc: e.tensor_copy(out=crep[:, kc, :], in_=cts[:, kc:kc + 1].to_broadcast([128, 128])),
                   reads=["cts"], writes=[f"crep{kc}"])
            for g in range(12):
                sl = g % 2
                op("sp", lambda e, g=g, sl=sl: e.dma_start(
                    out=wa[sl][:], in_=wada_d[:, g * 512:(g + 1) * 512].rearrange("(kc p) n -> p kc n", p=128)),
                   writes=[f"wa{sl}"], dma=True)
                op("sp", lambda e, g=g, sl=sl: e.dma_start(out=bad[sl][:], in_=bada_d[g]), writes=[f"bad{sl}"], dma=True)
                bank = ps[g % 2]

                def mm(e, g=g, sl=sl, bank=bank):
                    for kc in range(8):
                        e.matmul(bank[:], lhsT=crep[:, kc, :], rhs=wa[sl][:, kc, :], start=(kc == 0), stop=False)
                    return e.matmul(bank[:], lhsT=cf[0:1, ONES, :], rhs=bad[sl][0:1, :], start=False, stop=True)
                op("pe", mm, reads=[f"wa{sl}", f"bad{sl}", "cf"] + [f"crep{k}" for k in range(8)], writes=[f"ps{g % 2}"])
                addone = g in (2, 3, 8, 9)
                if g in (4, 5, 10, 11):
                    op("dve", lambda e, g=g, bank=bank: e.tensor_scalar(out=modtab[:, g * 512:(g + 1) * 512], in0=bank[:], scalar1=1.0,
                                                                        scalar2=1.0 / ALPHA, op0=ALU.add, op1=ALU.mult),
                       reads=[f"ps{g % 2}"], writes=[f"modtab{g}"])
                elif addone:
                    op("dve", lambda e, g=g, bank=bank: e.tensor_scalar_add(out=modtab[:, g * 512:(g + 1) * 512], in0=bank[:], scalar1=1.0),
                       reads=[f"ps{g % 2}"], writes=[f"modtab{g}"])
                else:
                    op("act", lambda e, g=g, bank=bank: e.activation(out=modtab[:, g * 512:(g + 1) * 512], in_=bank[:], func=AF.Identity),
                       reads=[f"ps{g % 2}"], writes=[f"modtab{g}"])
            for vi, base in enumerate((0, 1024, 3072, 4096)):
                for k in range(8):
                    op("dve", lambda e, base=base, k=k: e.tensor_tensor(
                        out=dtmp[:, k, :], in0=modtab[:, base + k * 128: base + (k + 1) * 128], in1=cf[:, IDENT, :], op=ALU.mult),
                       reads=[f"modtab{(base + k * 128) // 512}", "cf"], writes=["dtmp"])
                op("dve", lambda e, vi=vi: e.tensor_reduce(out=modp[:, vi, :], in_=dtmp[:], axis=AX.X, op=ALU.add),
                   reads=["dtmp"], writes=["modp"])
            op("sp", lambda e: e.dma_start(out=gt_d[0], in_=modtab[:, 2048:3072]), reads=["modtab4", "modtab5"], writes=["gt_d0"], dma=True)
            op("sp", lambda e: e.dma_start(out=gt_d[1], in_=modtab[:, 5120:6144]), reads=["modtab10", "modtab11"], writes=["gt_d1"], dma=True)
            sy.barrier()

        with ExitStack() as p1:
            P = p1.enter_context
            winb = P(nc.sbuf_tensor("winb", [128, 8, 5128], BF16))
            xt = [P(nc.sbuf_tensor(f"xt{i}", [128, 4, D], F32)) for i in range(2)]
            uT = [P(nc.sbuf_tensor(f"uT{i}", [128, 8, 512], BF16)) for i in range(2)]
            fst = [P(nc.sbuf_tensor(f"fst{i}", [128, 4, 512], BF16)) for i in range(3)]
            vst = [P(nc.sbuf_tensor(f"vst{i}", [128, 4, 1024], BF16)) for i in range(2)]
            dvst = [P(nc.sbuf_tensor(f"dvst{i}", [128, 4, 512], BF16)) for i in range(2)]
            bft = P(nc.sbuf_tensor("bft_sb", [128, 32], F32))
            fb = P(nc.sbuf_tensor("fb", [128, 32], F32))
            ef = P(nc.sbuf_tensor("ef", [128, 32], F32))
            lf = [P(nc.sbuf_tensor(f"lf{i}", [128, 32], F32)) for i in range(2)]
            d32 = P(nc.sbuf_tensor("d32", [8, 512], F32))
            r1 = P(nc.sbuf_tensor("r1", [8, 512], F32))
            r2 = P(nc.sbuf_tensor("r2", [8, 512], F32))
            cst = [P(nc.sbuf_tensor(f"cst{i}", [8, 3, 512], BF16)) for i in range(2)]

            for kc in range(8):
                op("pool", lambda e, kc=kc: e.dma_start(out=winb[:, kc, :], in_=win_d[kc * 128:(kc + 1) * 128, :]),
                   writes=[f"winb{kc}"], dma=True)
            WIN = [f"winb{k}" for k in range(8)]
            op("sp", lambda e: e.dma_start(out=bft[:], in_=bft_d), writes=["bft"], dma=True)
            op("dve", lambda e: e.memset(cql[:, 0, :], 0.0), writes=["cql"])

            def load_x(tt):
                op("sp", lambda e: e.dma_start(out=xt[tt % 2][:], in_=x_d[tt * 512:(tt + 1) * 512, :].rearrange("(j p) d -> p j d", p=128)),
                   writes=[f"xt{tt % 2}"], dma=True)
            load_x(0)
            pcnt = [0]

            def nbank():
                pcnt[0] += 1
                return pcnt[0] % 6
            fcnt = [0]
            for tt in range(NT):
                if tt + 1 < NT:
                    load_x(tt + 1)
                X = xt[tt % 2]
                U = uT[tt % 2]
                ukeys = [f"uT{tt % 2}_{k}" for k in range(8)]
                for kc in range(8):
                    b = nbank()

                    def tr(e, kc=kc, b=b, X=X):
                        for j in range(4):
                            ins = e.transpose(out=ps[b][:, j * 128:(j + 1) * 128], in_=X[:, j, kc * 128:(kc + 1) * 128], identity=cf[:, IDENT, :])
                        return ins
                    op("pe", tr, reads=[f"xt{tt % 2}", "cf"], writes=[f"ps{b}"])
                    if kc % 2 == 0:
                        op("act", lambda e, kc=kc, b=b, U=U: e.activation(out=U[:, kc, :], in_=ps[b][:], func=AF.Identity,
                                                                          bias=modp[:, 0, kc:kc + 1], scale=modp[:, 1, kc:kc + 1]),
                           reads=[f"ps{b}", "modp"], writes=[ukeys[kc]])
                    else:
                        op("dve", lambda e, kc=kc, b=b, U=U: e.tensor_scalar(out=U[:, kc, :], in0=ps[b][:], scalar1=modp[:, 1, kc:kc + 1],
                                                                             scalar2=modp[:, 0, kc:kc + 1], op0=ALU.mult, op1=ALU.add),
                           reads=[f"ps{b}", "modp"], writes=[ukeys[kc]])
                for oc in range(32):
                    b = nbank()

                    def mm(e, oc=oc, b=b, U=U):
                        for kc in range(8):
                            ins = e.matmul(ps[b][:], lhsT=winb[:, kc, oc * 128:(oc + 1) * 128], rhs=U[:, kc, :], start=(kc == 0), stop=(kc == 7))
                        return ins
                    op("pe", mm, reads=WIN + ukeys, writes=[f"ps{b}"])
                    fs = fcnt[0] % 3
                    dst = fst[fs][:, oc % 4, :]
                    if oc < 16 and (oc // 4) % 2 == 0:
                        op("dve", lambda e, b=b, dst=dst: e.tensor_scalar_mul(out=dst, in0=ps[b][:], scalar1=0.125),
                           reads=[f"ps{b}"], writes=[f"fst{fs}"])
                    elif oc < 16:
                        op("dve", lambda e, b=b, dst=dst: e.tensor_copy(out=dst, in_=ps[b][:]), reads=[f"ps{b}"], writes=[f"fst{fs}"])
                    else:
                        op("act", lambda e, b=b, dst=dst: e.activation(out=dst, in_=ps[b][:], func=AF.Sigmoid),
                           reads=[f"ps{b}"], writes=[f"fst{fs}"])
                    if oc % 4 == 3:
                        oc0 = oc - 3
                        op("sp", lambda e, oc0=oc0, fs=fs, tt=tt: e.dma_start(
                            out=fm_d[oc0:oc0 + 4, :, tt * 512:(tt + 1) * 512].rearrange("c p t -> p c t"), in_=fst[fs][:]),
                           reads=[f"fst{fs}"], writes=[f"fm_d{oc0}_{tt}"], dma=True)
                        fcnt[0] += 1
                VS = vst[tt % 2]
                for j in range(4):
                    for ab in range(2):
                        b = nbank()

                        def mmv(e, j=j, ab=ab, b=b, U=U):
                            for kc in range(8):
                                ins = e.matmul(ps[b][:], lhsT=U[:, kc, j * 128:(j + 1) * 128],
                                               rhs=winb[:, kc, 4096 + ab * 512: 4096 + (ab + 1) * 512], start=(kc == 0), stop=(kc == 7))
                            return ins
                        op("pe", mmv, reads=WIN + ukeys, writes=[f"ps{b}"])
                        dst = VS[:, j, ab * 512:(ab + 1) * 512]
                        if ab == 0:
                            op("act", lambda e, b=b, dst=dst: e.activation(out=dst, in_=ps[b][:], func=AF.Identity),
                               reads=[f"ps{b}"], writes=[f"vst{tt % 2}"])
                        else:
                            op("dve", lambda e, b=b, dst=dst: e.tensor_copy(out=dst, in_=ps[b][:]), reads=[f"ps{b}"], writes=[f"vst{tt % 2}"])
                for ab in range(2):
                    op("sp", lambda e, ab=ab, tt=tt, VS=VS: e.dma_start(
                        out=v_d[ab, tt * 512:(tt + 1) * 512, :].rearrange("(j p) d -> p j d", p=128), in_=VS[:, :, ab * 512:(ab + 1) * 512]),
                       reads=[f"vst{tt % 2}"], writes=[f"v_d{ab}_{tt}"], dma=True)
                DVS = dvst[tt % 2]
                for j in range(4):
                    b = nbank()
                    blk = tt * 4 + j
                    prev = VS[:, j - 1, 512:1024] if j > 0 else vst[(tt - 1) % 2][:, 3, 512:1024]

                    def mmdv(e, j=j, b=b, blk=blk, prev=prev, VS=VS):
                        ins = e.matmul(ps[b][:], lhsT=cb[:, DM, :], rhs=VS[:, j, 512:1024], start=True, stop=(blk == 0))
                        if blk > 0:
                            ins = e.matmul(ps[b][:], lhsT=cb[:, EB, :], rhs=prev, start=False, stop=True)
                        return ins
                    op("pe", mmdv, reads=[f"vst{tt % 2}", f"vst{(tt - 1) % 2}", "cb"], writes=[f"ps{b}"])
                    op("act", lambda e, j=j, b=b, DVS=DVS: e.activation(out=DVS[:, j, :], in_=ps[b][:], func=AF.Identity),
                       reads=[f"ps{b}"], writes=[f"dvst{tt % 2}"])
                op("sp", lambda e, tt=tt, DVS=DVS: e.dma_start(out=dv_d[tt * 512:(tt + 1) * 512, :].rearrange("(j p) d -> p j d", p=128), in_=DVS[:]),
                   reads=[f"dvst{tt % 2}"], writes=[f"dv_d{tt}"], dma=True)
                LF = lf[tt % 2]

                def mmf(e, U=U):
                    for j in range(4):
                        for kc in range(8):
                            ins = e.matmul(ps[6][:, j * 8:(j + 1) * 8], lhsT=U[:, kc, j * 128:(j + 1) * 128], rhs=winb[:, kc, 5120:5128],
                                           start=(kc == 0), stop=(kc == 7))
                    return ins
                op("pe", mmf, reads=WIN + ukeys, writes=["ps6"])
                op("dve", lambda e: e.tensor_tensor(out=fb[:], in0=ps[6][:, 0:32], in1=bft[:], op=ALU.add), reads=["ps6", "bft"], writes=["fb"])
                op("act", lambda e: e.activation(out=ef[:], in_=fb[:], func=AF.Exp, scale=-1.0), reads=["fb"], writes=["ef"])
                op("act", lambda e, LF=LF: e.activation(out=LF[:], in_=ef[:], func=AF.Ln, bias=1.0, scale=1.0), reads=["ef"], writes=[f"lf{tt % 2}"])

                def mmc(e, LF=LF):
                    for j in range(4):
                        ins = e.matmul(ps[6][:, 64 + j * 8: 64 + (j + 1) * 8], lhsT=cf[:, UINC, :], rhs=LF[:, j * 8:(j + 1) * 8], start=True, stop=(j == 0))
                        for jp in range(j):
                            ins = e.matmul(ps[6][:, 64 + j * 8: 64 + (j + 1) * 8], lhsT=cf[:, ONES, :], rhs=LF[:, jp * 8:(jp + 1) * 8],
                                           start=False, stop=(jp == j - 1))
                    for jp in range(4):
                        ins = e.matmul(ps[6][:, 96:104], lhsT=cf[:, ONES, :], rhs=LF[:, jp * 8:(jp + 1) * 8], start=(jp == 0), stop=(jp == 3))
                    return ins
                op("pe", mmc, reads=[f"lf{tt % 2}", "cf"], writes=["ps6"])
                for j in range(4):
                    op("dve", lambda e, j=j, tt=tt: e.tensor_tensor(out=ctok[:, tt * 4 + j, :], in0=ps[6][:, 64 + j * 8: 64 + (j + 1) * 8],
                                                                    in1=cql[:, tt, :], op=ALU.add),
                       reads=["ps6", "cql"], writes=["ctok"])
                op("dve", lambda e, tt=tt: e.tensor_tensor(out=cql[:, tt + 1, :], in0=ps[6][:, 96:104], in1=cql[:, tt, :], op=ALU.add),
                   reads=["ps6", "cql"], writes=["cql"])

                def mmd(e, LF=LF):
                    for j in range(4):
                        ins = e.matmul(ps[7][0:8, j * 128:(j + 1) * 128], lhsT=LF[:, j * 8:(j + 1) * 8], rhs=cf[:, LSTR, :], start=True, stop=(j == 3))
                        for jp in range(j + 1, 4):
                            ins = e.matmul(ps[7][0:8, j * 128:(j + 1) * 128], lhsT=LF[:, jp * 8:(jp + 1) * 8], rhs=cf[:, ONES, :],
                                           start=False, stop=(jp == 3))
                    return ins
                op("pe", mmd, reads=[f"lf{tt % 2}", "cf"], writes=["ps7"])
                CS = cst[tt % 2]
                ck = f"cst{tt % 2}"
                op("dve", lambda e: e.tensor_copy(out=d32[:], in_=ps[7][0:8, :]), reads=["ps7"], writes=["d32"])
                op("dve", lambda e, CS=CS: e.tensor_copy(out=CS[:, 0, :], in_=d32[:]), reads=["d32"], writes=[ck])
                op("dve", lambda e, CS=CS: e.tensor_tensor(out=r1[:], in0=d32[:], in1=CS[:, 0, :], op=ALU.subtract), reads=["d32", ck], writes=["r1"])
                op("dve", lambda e, CS=CS: e.tensor_copy(out=CS[:, 1, :], in_=r1[:]), reads=["r1"], writes=[ck])
                op("dve", lambda e, CS=CS: e.tensor_tensor(out=r2[:], in0=r1[:], in1=CS[:, 1, :], op=ALU.subtract), reads=["r1", ck], writes=["r2"])
                op("dve", lambda e, CS=CS: e.tensor_copy(out=CS[:, 2, :], in_=r2[:]), reads=["r2"], writes=[ck])
                op("sp", lambda e, CS=CS, tt=tt: e.dma_start(out=caug_d[:, :, tt * 512:(tt + 1) * 512], in_=CS[:]),
                   reads=[ck], writes=[f"caug_d{tt}"], dma=True)
            sy.barrier()

        p3w = ExitStack()
        wfb = p3w.enter_context(nc.sbuf_tensor("wfb", [128, 4, D], BF16))
        wsbb = p3w.enter_context(nc.sbuf_tensor("wsbb", [128, 4, D], BF16))
        wob = p3w.enter_context(nc.sbuf_tensor("wob", [128, 8, D], BF16))
        for c in range(4):
            op("pool", lambda e, c=c: e.dma_start(out=wfb[:, c, :], in_=wfox_d[c * 128:(c + 1) * 128, :]), writes=["wfb"], dma=True)
            op("pool", lambda e, c=c: e.dma_start(out=wsbb[:, c, :], in_=wsb_d[c * 128:(c + 1) * 128, :]), writes=["wsbb"], dma=True)
        for c in range(8):
            op("pool", lambda e, c=c: e.dma_start(out=wob[:, c, :], in_=wo_d[c * 128:(c + 1) * 128, :]), writes=["wob"], dma=True)

        with ExitStack() as p2:
            P = p2.enter_context
            KT = [P(nc.sbuf_tensor(f"KT{i}", [128, S], BF16)) for i in range(2)]
            QT = [P(nc.sbuf_tensor(f"QT{i}", [128, S], BF16)) for i in range(2)]
            VV = [P(nc.sbuf_tensor(f"VV{i}", [128, NB, 65], BF16)) for i in range(2)]
            bias_t = [P(nc.sbuf_tensor(f"bias{i}", [128, NT, NB], F32)) for i in range(2)]
            zeros = P(nc.sbuf_tensor("zeros", [128, 64], BF16))
            Pb = [P(nc.sbuf_tensor(f"Pb{i}", [128, 512], BF16)) for i in range(4)]
            dVV = [P(nc.sbuf_tensor(f"dVV{i}", [128, NB, 64], BF16)) for i in range(2)]
            wbuf = [P(nc.sbuf_tensor(f"wbuf{i}", [128, 512], F32)) for i in range(3)]
            Gs = [P(nc.sbuf_tensor(f"Gs{i}", [128, 512], BF16)) for i in range(4)]
            GT = [P(nc.sbuf_tensor(f"GT{i}", [128, 512], BF16)) for i in range(3)]
            onesf = P(nc.sbuf_tensor("onesf", [128, 512], F32))
            rs = [P(nc.sbuf_tensor(f"rs{i}", [65, 512], F32)) for i in range(2)]
            ysb = [P(nc.sbuf_tensor(f"ysb{i}", [64, 512], F32)) for i in range(2)]
            yo = [P(nc.sbuf_tensor(f"yo{i}", [64, 512], BF16)) for i in range(2)]

            op("dve", lambda e: e.memset(zeros[:], 0.0), writes=["zeros"])
            for i in range(2):
                op("dve", lambda e, i=i: e.memset(KT[i][64:67, :], 1.0), writes=[f"KTa{i}"])
                op("dve", lambda e, i=i: e.memset(VV[i][:, :, 64:65], 1.0), writes=[f"VVa{i}"])

            def fox_loads(h):
                sl = h % 2
                hp, hf = h // 2, h % 2
                op("sp", lambda e: e.dma_start(out=KT[sl][0:64, :], in_=fm_d[4 + hp, hf * 64:(hf + 1) * 64, :]), writes=[f"KT{sl}"], dma=True)
                op("sp", lambda e: e.dma_start(out=QT[sl][0:64, :], in_=fm_d[0 + hp, hf * 64:(hf + 1) * 64, :]), writes=[f"QT{sl}"], dma=True)
                op("sp", lambda e: e.dma_start(out=QT[sl][64:67, :], in_=caug_d[h]), writes=[f"QTa{sl}"], dma=True)
                for k0 in range(0, NB, 8):
                    op("sp", lambda e, k0=k0: e.dma_start(out=VV[sl][:, k0:k0 + 8, 0:64],
                                                          in_=v_d[0, k0 * 128:(k0 + 8) * 128, h * 64:(h + 1) * 64].rearrange("(kb p) d -> p kb d", p=128)),
                       writes=[f"VV{sl}"], dma=True)
                for qt in range(NT):
                    op("dve", lambda e, qt=qt: e.tensor_scalar(out=bias_t[sl][:, qt, :], in0=ctok[:, :, h], scalar1=cql[:, qt + 1, h:h + 1],
                                                               scalar2=None, op0=ALU.subtract),
                       reads=["ctok", "cql"], writes=[f"bias{sl}"])

            items = []
            for h in range(NH):
                for qt in range(NT):
                    n = 4 * qt + 4
                    for kb in range(n):
                        items.append(dict(h=h, qt=qt, kb=kb, first=(kb == 0), last=(kb == n - 1), hfirst=(qt == 0 and kb == 0)))
            for i, it in enumerate(items):
                it["i"] = i
                it["tile"] = it["h"] * NT + it["qt"]
                j = it["kb"] - 4 * it["qt"]
                it["j"] = j
                it["c0"] = 128 * j if j > 0 else 0
            nper = len(items) // NH
            assert nper >= 10
            fox_loads(0)

            def fx_z(it):
                i, h, qt, kb, c0 = it["i"], it["h"], it["qt"], it["kb"], it["c0"]
                sl = h % 2
                if it["hfirst"] and h + 1 < NH:
                    pass
                b = i % 3
                op("pe", lambda e: e.matmul(ps[b][:, c0:512], lhsT=KT[sl][0:67, kb * 128:(kb + 1) * 128],
                                            rhs=QT[sl][0:67, qt * 512 + c0:(qt + 1) * 512], start=True, stop=True),
                   reads=[f"KT{sl}", f"KTa{sl}", f"QT{sl}", f"QTa{sl}"], writes=[f"ps{b}"])

            def fx_p(it):
                i, h, qt, kb, c0 = it["i"], it["h"], it["qt"], it["kb"], it["c0"]
                sl = h % 2
                b = i % 3
                pb = i % 4
                op("act", lambda e: e.activation(out=Pb[pb][:, c0:512], in_=ps[b][:, c0:512], func=AF.Exp,
                                                 bias=bias_t[sl][:, qt, kb:kb + 1], scale=1.0),
                   reads=[f"ps{b}", f"bias{sl}"], writes=[f"Pb{pb}"])
                if it["j"] >= 0:
                    op("pool", lambda e: e.tensor_tensor(out=Pb[pb][:, c0:c0 + 128], in0=Pb[pb][:, c0:c0 + 128], in1=cb[:, UINC, :], op=ALU.mult),
                       reads=[f"Pb{pb}", "cb"], writes=[f"Pb{pb}"])

            def fx_pv(it):
                i, h, qt, kb, c0 = it["i"], it["h"], it["qt"], it["kb"], it["c0"]
                sl = h % 2
                pb = i % 4
                yb = 4 + it["tile"] % 2
                op("pe", lambda e: e.matmul(ps[yb][0:65, c0:512], lhsT=VV[sl][:, kb, 0:65], rhs=Pb[pb][:, c0:512],
                                            start=it["first"], stop=it["last"]),
                   reads=[f"VV{sl}", f"VVa{sl}", f"Pb{pb}"], writes=[f"ps{yb}"])

            def fx_fin1(it):
                if not it["last"]:
                    return
                t2 = it["tile"] % 2
                yb = 4 + t2
                op("dve", lambda e: e.reciprocal(out=rs[t2][64:65, :], in_=ps[yb][64:65, :]), reads=[f"ps{yb}"], writes=[f"rs{t2}"])
                op("act", lambda e: e.activation(out=ysb[t2][:], in_=ps[yb][0:64, :], func=AF.Identity), reads=[f"ps{yb}"], writes=[f"ysb{t2}"])

            def fx_fin2(it):
                if not it["last"]:
                    return
                h, qt = it["h"], it["qt"]
                t2 = it["tile"] % 2
                op("pe", lambda e: e.matmul(ps[6][0:64, :], lhsT=cf[64:65, ONES, 0:64], rhs=rs[t2][64:65, :], start=True, stop=True),
                   reads=[f"rs{t2}", "cf"], writes=["ps6"])
                op("dve", lambda e: e.tensor_tensor(out=yo[t2][:], in0=ysb[t2][:], in1=ps[6][0:64, :], op=ALU.mult),
                   reads=[f"ysb{t2}", "ps6"], writes=[f"yo{t2}"])
                op("sp", lambda e: e.dma_start(out=yT_d[0, h * 64:(h + 1) * 64, qt * 512:(qt + 1) * 512], in_=yo[t2][:]),
                   reads=[f"yo{t2}"], writes=[f"yT_d0_{h}_{qt}"], dma=True)

            stages = [(fx_z, 0), (fx_p, 1), (fx_pv, 2), (fx_fin1, 3), (fx_fin2, 9)]
            n = len(items)
            for s in range(n + 10):
                for fn, sk in stages:
                    i = s - sk
                    if 0 <= i < n:
                        fn(items[i])
                if s < n and (s % nper) == 6:
                    h = items[s]["h"]
                    if h + 1 < NH:
                        fox_loads(h + 1)
            sy.barrier()

            op("dve", lambda e: e.memset(onesf[:], 1.0), writes=["onesf"])

            def sb_loads(hp):
                sl = hp % 2
                op("sp", lambda e: e.dma_start(out=KT[sl][:, :], in_=fm_d[12 + hp]), writes=[f"KT{sl}"], dma=True)
                op("sp", lambda e: e.dma_start(out=QT[sl][:, :], in_=fm_d[8 + hp]), writes=[f"QT{sl}"], dma=True)

            def sb_vload(h):
                sl = h % 2
                for k0 in range(0, NB, 8):
                    op("sp", lambda e, k0=k0: e.dma_start(out=VV[sl][:, k0:k0 + 8, 0:64],
                                                          in_=v_d[1, k0 * 128:(k0 + 8) * 128, h * 64:(h + 1) * 64].rearrange("(kb p) d -> p kb d", p=128)),
                       writes=[f"VV{sl}"], dma=True)
                    op("sp", lambda e, k0=k0: e.dma_start(out=dVV[sl][:, k0:k0 + 8, :],
                                                          in_=dv_d[k0 * 128:(k0 + 8) * 128, h * 64:(h + 1) * 64].rearrange("(kb p) d -> p kb d", p=128)),
                       writes=[f"dVV{sl}"], dma=True)

            items = []
            for h in range(NH):
                for qb in range(NB):
                    a, jq = qb // 4, qb % 4
                    for kt in range(a, -1, -1):
                        W = 128 * (jq + 1) if kt == a else 512
                        items.append(dict(h=h, qb=qb, a=a, jq=jq, kt=kt, W=W, first=(kt == a), last=(kt == 0)))
            for i, it in enumerate(items):
                it["i"] = i
            nper = len(items) // NH
            assert nper >= 12
            sb_loads(0)
            sb_vload(0)
            GTV = [ps[2][:].bitcast(BF16), ps[3][:].bitcast(BF16), ps[4][:].bitcast(BF16)]

            def sb_z(it):
                i, h, qb, kt, W = it["i"], it["h"], it["qb"], it["kt"], it["W"]
                sl = (h // 2) % 2
                p0_ = 64 * (h % 2)
                b = i % 2
                op("pe", lambda e: e.matmul(ps[b][:, 0:W], lhsT=QT[sl][p0_:p0_ + 64, qb * 128:(qb + 1) * 128],
                                            rhs=KT[sl][p0_:p0_ + 64, kt * 512: kt * 512 + W], start=True, stop=True),
                   reads=[f"KT{sl}", f"QT{sl}"], writes=[f"ps{b}"])

            def sb_w(it):
                i, W, jq = it["i"], it["W"], it["jq"]
                b = i % 2
                wi = i % 3
                op("act", lambda e: e.activation(out=wbuf[wi][:, 0:W], in_=ps[b][:, 0:W], func=AF.Sigmoid, scale=-1.0),
                   reads=[f"ps{b}"], writes=[f"wbuf{wi}"])
                if it["first"]:
                    c0 = 128 * jq
                    op("pool", lambda e: e.tensor_tensor(out=wbuf[wi][:, c0:c0 + 128], in0=wbuf[wi][:, c0:c0 + 128], in1=cf[:, LSTR, :], op=ALU.mult),
                       reads=[f"wbuf{wi}", "cf"], writes=[f"wbuf{wi}"])
                    op("pool", lambda e: e.tensor_tensor(out=wbuf[wi][:, c0:c0 + 128], in0=wbuf[wi][:, c0:c0 + 128], in1=cf[:, UINC, :], op=ALU.add),
                       reads=[f"wbuf{wi}", "cf"], writes=[f"wbuf{wi}"])

            def rev(ap_, n):
                return bass.AP(ap_.tensor, ap_.offset + (n - 1), [[ap_.ap[0][0], 128], [-1, n]])

            def sb_scan(it):
                i, W, jq = it["i"], it["W"], it["jq"]
                wi = i % 3
                gi = i % 4
                if it["first"]:
                    init = 1.0
                    rd = [f"wbuf{wi}", "onesf"]
                else:
                    init = Gs[(i - 1) % 4][:, 0:1]
                    rd = [f"wbuf{wi}", "onesf", f"Gs{(i - 1) % 4}"]
                op("dve", lambda e: e.tensor_tensor_scan(out=rev(Gs[gi][:, 0:W], W), data0=rev(wbuf[wi][:, 0:W], W), data1=onesf[:, 0:W],
                                                         initial=init, op0=ALU.mult, op1=ALU.mult),
                   reads=rd, writes=[f"Gs{gi}"])
                if it["first"]:
                    c0 = 128 * jq
                    op("pool", lambda e: e.tensor_tensor(out=Gs[gi][:, c0:c0 + 128], in0=Gs[gi][:, c0:c0 + 128], in1=cb[:, LINC, :], op=ALU.mult),
                       reads=[f"Gs{gi}", "cb"], writes=[f"Gs{gi}"])

            def sb_tr(it):
                i, W = it["i"], it["W"]
                gi = i % 4
                ti = i % 3
                tv = GTV[ti]
                o0 = 0

                def tr(e):
                    for c in range(W // 128):
                        ins = e.matmul(ps[2 + ti][:, c * 128:(c + 1) * 128], lhsT=Gs[gi][:, c * 128:(c + 1) * 128], rhs=cb[:, IDENT, :],
                                       start=True, stop=True)
                    return ins
                op("pe", tr, reads=[f"Gs{gi}", "cb"], writes=[f"pst{ti}"])

            def sb_ev(it):
                i, W = it["i"], it["W"]
                ti = i % 3
                tv = GTV[ti]
                o0 = 0
                g3 = i % 3
                op("act", lambda e: e.activation(out=GT[g3][:, 0:W], in_=ps[2 + ti][:, 0:W], func=AF.Identity), reads=[f"pst{ti}"], writes=[f"GT{g3}"])

            def sb_pv(it):
                i, h, qb, kt, W, jq, a = it["i"], it["h"], it["qb"], it["kt"], it["W"], it["jq"], it["a"]
                sl = h % 2
                g3 = i % 3
                yb = 5 + (h * NT + a) % 3
                ycol = slice(jq * 128, (jq + 1) * 128)

                def mm(e):
                    if it["first"]:
                        e.matmul(ps[yb][0:64, ycol], lhsT=VV[sl][:, qb, 0:64], rhs=cb[:, IDENT, :], start=True, stop=False)
                    nsub = W // 128
                    for c in range(nsub):
                        ins = e.matmul(ps[yb][0:64, ycol], lhsT=dVV[sl][:, kt * 4 + c, :], rhs=GT[g3][:, c * 128:(c + 1) * 128],
                                       start=False, stop=(it["last"] and c == nsub - 1))
                    return ins
                op("pe", mm, reads=[f"VV{sl}", f"dVV{sl}", f"GT{g3}", "cb"], writes=[f"ps{yb}"])

            def sb_fin(it):
                if not (it["last"] and it["jq"] == 3):
                    return
                h, a = it["h"], it["a"]
                t2 = (h * NT + a) % 2
                yb = 5 + (h * NT + a) % 3
                op("act", lambda e: e.activation(out=yo[t2][:], in_=ps[yb][0:64, :], func=AF.Identity), reads=[f"ps{yb}"], writes=[f"yo{t2}"])
                op("sp", lambda e: e.dma_start(out=yT_d[1, h * 64:(h + 1) * 64, a * 512:(a + 1) * 512], in_=yo[t2][:]),
                   reads=[f"yo{t2}"], writes=[f"yT_d1_{h}_{a}"], dma=True)

            stages = [(sb_z, 0), (sb_w, 1), (sb_scan, 2), (sb_tr, 3), (sb_ev, 4), (sb_pv, 5), (sb_fin, 6)]
            n = len(items)
            for s in range(n + 7):
                for fn, sk in stages:
                    i = s - sk
                    if 0 <= i < n:
                        fn(items[i])
                if s < n and (s % nper) == 9:
                    h = items[s]["h"]
                    if h + 1 < NH:
                        sb_vload(h + 1)
                        if (h + 1) % 2 == 0:
                            sb_loads((h + 1) // 2)
            sy.barrier()

        def ln_tail(R, rk, g_t, b_t, st6, mv, sc2, g_eng="pool", b_eng="pool", sfx=""):
            ops = []
            for hh in range(2):
                ops.append(lambda hh=hh: op("dve", lambda e: e.bn_stats(out=st6[:, hh * 6:(hh + 1) * 6], in_=R[:, hh * 512:(hh + 1) * 512]),
                                            reads=[rk], writes=["st6" + sfx]))
            ops.append(lambda: op("dve", lambda e: e.bn_aggr(out=mv[:], in_=st6[:]), reads=["st6" + sfx], writes=["mv" + sfx]))
            ops.append(lambda: op("dve", lambda e: e.tensor_scalar_add(out=sc2[:, 0:1], in0=mv[:, 1:2], scalar1=EPS / (ALPHA * ALPHA)), reads=["mv" + sfx], writes=["sc2" + sfx]))
            ops.append(lambda: op("pool", lambda e: e.tensor_tensor(out=sc2[:, 0:1], in0=sc2[:, 0:1], in1=mhalf[:, 0:1], op=ALU.pow),
                                  reads=["sc2" + sfx, "mhalf"], writes=["sc2" + sfx]))
            ops.append(lambda: op("dve", lambda e: e.scalar_tensor_tensor(out=sc2[:, 1:2], in0=mv[:, 0:1], scalar=-1.0, in1=sc2[:, 0:1],
                                                                          op0=ALU.mult, op1=ALU.mult), reads=["mv" + sfx, "sc2" + sfx], writes=["sc2b" + sfx]))
            ops.append(lambda: op("act", lambda e: e.activation(out=R[:], in_=R[:], func=AF.Identity, bias=sc2[:, 1:2], scale=sc2[:, 0:1]),
                                  reads=[rk, "sc2" + sfx, "sc2b" + sfx], writes=[rk]))
            for q in range(4):
                qs = slice(q * 256, (q + 1) * 256)
                ops.append(lambda qs=qs: op(g_eng, lambda e: e.tensor_tensor(out=R[:, qs], in0=R[:, qs], in1=g_t[:, qs], op=ALU.mult),
                                            reads=[rk, "lnt"], writes=[rk]))
            for q in range(4):
                qs = slice(q * 256, (q + 1) * 256)
                ops.append(lambda qs=qs: op(b_eng, lambda e: e.tensor_tensor(out=R[:, qs], in0=R[:, qs], in1=b_t[:, qs], op=ALU.add),
                                            reads=[rk, "lnt"], writes=[rk]))
            return ops

        deferred = []

        def pop_deferred(n):
            for _ in range(n):
                if deferred:
                    deferred.pop(0)()

        with ExitStack() as p3:
            P = p3.enter_context
            lnt = P(nc.sbuf_tensor("lnt_sb", [128, 2, D], F32))
            gt1 = P(nc.sbuf_tensor("gtab1", [128, D], F32))
            yf = [P(nc.sbuf_tensor(f"yf{i}", [128, 4, 512], BF16)) for i in range(2)]
            ysl = [P(nc.sbuf_tensor(f"ysl{i}", [128, 4, 512], BF16)) for i in range(2)]
            gt = [P(nc.sbuf_tensor(f"gt{i}", [128, 16, 512], BF16)) for i in range(2)]
            xt = [P(nc.sbuf_tensor(f"xt3_{i}", [128, 4, D], F32)) for i in range(2)]
            m1 = [P(nc.sbuf_tensor(f"m1_{i}", [128, 512], F32)) for i in range(2)]
            m2 = [P(nc.sbuf_tensor(f"m2_{i}", [128, 512], F32)) for i in range(2)]
            mg = [P(nc.sbuf_tensor(f"mg{i}", [128, 8, 512], BF16)) for i in range(2)]
            Rb = [P(nc.sbuf_tensor(f"Rb{i}", [128, D], F32)) for i in range(4)]
            st6 = [P(nc.sbuf_tensor(f"st6_{i}", [128, 12], F32)) for i in range(4)]
            mv = [P(nc.sbuf_tensor(f"mv_{i}", [128, 2], F32)) for i in range(4)]
            sc2 = [P(nc.sbuf_tensor(f"sc2_{i}", [128, 2], F32)) for i in range(4)]
            op("sp", lambda e: e.dma_start(out=lnt[:], in_=lnt_d[0:2].rearrange("c p n -> p c n")), writes=["lnt"], dma=True)
            op("sp", lambda e: e.dma_start(out=gt1[:], in_=gt_d[0]), writes=["gt1"], dma=True)

            def loadA(tt):
                sl = tt % 2
                tsl = slice(tt * 512, (tt + 1) * 512)
                op("sp", lambda e: e.dma_start(out=yf[sl][:], in_=yT_d[0, :, tsl].rearrange("(c p) t -> p c t", p=128)), writes=[f"yf{sl}"], dma=True)
                op("sp", lambda e: e.dma_start(out=ysl[sl][:], in_=yT_d[1, :, tsl].rearrange("(c p) t -> p c t", p=128)), writes=[f"ysl{sl}"], dma=True)
                op("sp", lambda e: e.dma_start(out=gt[sl][:], in_=fm_d[16:32, :, tsl].rearrange("c p t -> p c t")), writes=[f"gt{sl}"], dma=True)

            def loadX(tt):
                sl = tt % 2
                tsl = slice(tt * 512, (tt + 1) * 512)
                op("sp", lambda e: e.dma_start(out=xt[sl][:], in_=x_d[tsl, :].rearrange("(j p) d -> p j d", p=128)), writes=[f"xt3_{sl}"], dma=True)

            c3 = {"cnt": 0}

            def OC(tt):
                sl = tt % 2
                MG = mg[sl]
                for oc in range(8):
                    cnt = c3["cnt"]
                    ba, bb = (2 * cnt) % 4, (2 * cnt + 1) % 4
                    mi = cnt % 2
                    c3["cnt"] += 1

                    def mmab(e, oc=oc, ba=ba, bb=bb, sl=sl):
                        for c in range(4):
                            e.matmul(ps[ba][:], lhsT=wfb[:, c, oc * 128:(oc + 1) * 128], rhs=yf[sl][:, c, :], start=(c == 0), stop=(c == 3))
                        for c in range(4):
                            ins = e.matmul(ps[bb][:], lhsT=wsbb[:, c, oc * 128:(oc + 1) * 128], rhs=ysl[sl][:, c, :], start=(c == 0), stop=(c == 3))
                        return ins
                    op("pe", mmab, reads=["wfb", "wsbb", f"yf{sl}", f"ysl{sl}"], writes=[f"ps{ba}", f"ps{bb}"])
                    op("dve", lambda e, oc=oc, ba=ba, mi=mi, sl=sl: e.tensor_tensor(out=m1[mi][:], in0=ps[ba][:], in1=gt[sl][:, oc, :], op=ALU.mult),
                       reads=[f"ps{ba}", f"gt{sl}"], writes=[f"m1_{mi}"])
                    op("dve", lambda e, oc=oc, bb=bb, mi=mi, sl=sl: e.tensor_tensor(out=m2[mi][:], in0=ps[bb][:], in1=gt[sl][:, 8 + oc, :], op=ALU.mult),
                       reads=[f"ps{bb}", f"gt{sl}"], writes=[f"m2_{mi}"])
                    op("pool", lambda e, oc=oc, mi=mi, MG=MG: e.tensor_tensor(out=MG[:, oc, :], in0=m1[mi][:], in1=m2[mi][:], op=ALU.add),
                       reads=[f"m1_{mi}", f"m2_{mi}"], writes=[f"mg{sl}"])
                    pop_deferred(9)

            def MO(tt):
                sl = tt % 2
                MG = mg[sl]
                chains = []
                for j in range(4):
                    blk = tt * 4 + j
                    R = Rb[j]
                    rk = f"Rb{j}"
                    for hh in range(2):
                        b = 4 + (2 * blk + hh) % 4

                        def mmo(e, j=j, hh=hh, b=b, MG=MG):
                            for oc in range(8):
                                ins = e.matmul(ps[b][:], lhsT=MG[:, oc, j * 128:(j + 1) * 128], rhs=wob[:, oc, hh * 512:(hh + 1) * 512],
                                               start=(oc == 0), stop=(oc == 7))
                            return ins
                        op("pe", mmo, reads=[f"mg{sl}", "wob"], writes=[f"ps{b}"])
                        op("dve", lambda e, hh=hh, b=b, R=R: e.tensor_tensor(out=R[:, hh * 512:(hh + 1) * 512], in0=ps[b][:],
                                                                              in1=gt1[:, hh * 512:(hh + 1) * 512], op=ALU.mult),
                           reads=[f"ps{b}", "gt1"], writes=[rk])
                    op("pool", lambda e, j=j, R=R, sl=sl: e.tensor_tensor(out=R[:], in0=R[:], in1=xt[sl][:, j, :], op=ALU.add),
                       reads=[f"xt3_{sl}", rk], writes=[rk])
                    ch = ln_tail(R, rk, lnt[:, 0, :], lnt[:, 1, :], st6[j], mv[j], sc2[j], g_eng="dve", sfx=f"_{j}")
                    ch.append(lambda blk=blk, R=R, rk=rk: op("sp", lambda e: e.dma_start(out=x1_d[blk * 128:(blk + 1) * 128, :], in_=R[:]),
                                                             reads=[rk], writes=[f"x1_d{blk}"], dma=True))
                    chains.append(ch)
                for k in range(max(len(c) for c in chains)):
                    for c in chains:
                        if k < len(c):
                            deferred.append(c[k])

            loadA(0)
            loadX(0)
            if NT > 1:
                loadA(1)
            OC(0)
            for tt in range(NT):
                if tt + 2 < NT:
                    loadA(tt + 2)
                if tt + 1 < NT:
                    loadX(tt + 1)
                    OC(tt + 1)
                else:
                    pop_deferred(len(deferred))
                MO(tt)
            pop_deferred(len(deferred))
            sy.barrier()

        p3w.close()
        p012.close()

        TW = 256
        NT2 = S // TW
        JB = TW // 128
        with ExitStack() as p4:
            P = p4.enter_context
            wupb = P(nc.sbuf_tensor("wupb", [128, 8, 2 * DFF], BF16))
            wdb = P(nc.sbuf_tensor("wdb", [128, NAC, D], BF16))
            lnt = P(nc.sbuf_tensor("lnt2", [128, 2, D], F32))
            gt2 = P(nc.sbuf_tensor("gtab2", [128, D], F32))
            convp = P(nc.sbuf_tensor("convp_sb", [128, NFC, 4], F32))
            halo = [P(nc.sbuf_tensor(f"halo{i}", [128, NFC, 2], F32)) for i in range(2)]
            xt = [P(nc.sbuf_tensor(f"xt4_{i}", [128, JB, D], F32)) for i in range(2)]
            uT = [P(nc.sbuf_tensor(f"u2T{i}", [128, 8, TW], BF16)) for i in range(2)]
            hb = [P(nc.sbuf_tensor(f"hb{i}", [128, TW + 2], F32)) for i in range(2)]
            cv = [P(nc.sbuf_tensor(f"cv{i}", [128, TW], F32)) for i in range(3)]
            sg = [P(nc.sbuf_tensor(f"sg{i}", [128, TW], F32)) for i in range(2)]
            act = [P(nc.sbuf_tensor(f"act{i}", [128, NAC, TW], BF16)) for i in range(2)]
            Rb = [P(nc.sbuf_tensor(f"Rb4_{i}", [128, D], F32)) for i in range(2)]
            st6 = P(nc.sbuf_tensor("st6b", [128, 12], F32))
            mv = P(nc.sbuf_tensor("mvb", [128, 2], F32))
            sc2 = P(nc.sbuf_tensor("sc2b_", [128, 2], F32))
            HF = (NAC // 2) * 128
            for hv, key in ((0, "wupbA"), (1, "wupbB")):
                for c in range(8):
                    for base in (0, DFF):
                        c0 = base + hv * HF
                        op("pool", lambda e, c=c, c0=c0: e.dma_start(out=wupb[:, c, c0:c0 + HF], in_=wup_d[c * 128:(c + 1) * 128, c0:c0 + HF]),
                           writes=[key], dma=True)
            for c in range(NAC):
                op("pool", lambda e, c=c: e.dma_start(out=wdb[:, c, :], in_=wdown_d[c * 128:(c + 1) * 128, :]), writes=["wdb"], dma=True)
            op("sp", lambda e: e.dma_start(out=lnt[:], in_=lnt_d[2:4].rearrange("c p n -> p c n")), writes=["lnt"], dma=True)
            op("sp", lambda e: e.dma_start(out=gt2[:], in_=gt_d[1]), writes=["gt2"], dma=True)
            op("sp", lambda e: e.dma_start(out=convp[:], in_=convp_d), writes=["convp"], dma=True)
            op("dve", lambda e: e.memset(halo[1][:], 0.0), writes=["halo1"])

            def loads4(tt):
                sl = tt % 2
                op("sp", lambda e: e.dma_start(out=xt[sl][:], in_=x1_d[tt * TW:(tt + 1) * TW, :].rearrange("(j p) d -> p j d", p=128)),
                   writes=[f"xt4_{sl}"], dma=True)
            cnts = {"pc": 0, "hc": 0}

            def T4(tt):
                sl = tt % 2
                X, U = xt[sl], uT[sl]
                for kc in range(8):
                    b = cnts["pc"] % 4
                    cnts["pc"] += 1

                    def tr(e, kc=kc, b=b, X=X):
                        for j in range(JB):
                            ins = e.transpose(out=ps[b][:, j * 128:(j + 1) * 128], in_=X[:, j, kc * 128:(kc + 1) * 128], identity=identf[:])
                        return ins
                    op("pe", tr, reads=[f"xt4_{sl}", "identf"], writes=[f"ps{b}"])
                    op("act", lambda e, kc=kc, b=b, U=U: e.activation(out=U[:, kc, :], in_=ps[b][:, 0:TW], func=AF.Identity,
                                                                      bias=modp[:, 2, kc:kc + 1], scale=modp[:, 3, kc:kc + 1]),
                       reads=[f"ps{b}", "modp"], writes=[f"u2T{sl}"])

            def U4(tt):
                sl = tt % 2
                U, A = uT[sl], act[sl]
                uk = f"u2T{sl}"
                base = cnts["hc"]
                cnts["hc"] += 2 * NAC

                def stA(c):
                    i, half = c // 2, c % 2
                    fc = i + half * NAC
                    b = cnts["pc"] % 4
                    cnts["pc"] += 1
                    hi = (base + c) % 2
                    ci = (base + c) % 3
                    H, CV = hb[hi], cv[ci]

                    def mmu(e):
                        for kc in range(8):
                            ins = e.matmul(ps[b][:, 0:TW], lhsT=wupb[:, kc, fc * 128:(fc + 1) * 128], rhs=U[:, kc, :], start=(kc == 0), stop=(kc == 7))
                        return ins
                    op("pe", mmu, reads=["wupbA" if i < NAC // 2 else "wupbB", uk], writes=[f"ps{b}"])
                    op("act", lambda e: e.activation(out=H[:, 2:TW + 2], in_=ps[b][:, 0:TW], func=AF.Identity), reads=[f"ps{b}"], writes=[f"hb{hi}"])
                    op("act", lambda e: e.activation(out=CV[:], in_=ps[b][:, 0:TW], func=AF.Identity, bias=convp[:, fc, 3:4], scale=convp[:, fc, 2:3]),
                       reads=[f"ps{b}", "convp"], writes=[f"cv{ci}"])
                    op("act", lambda e: e.activation(out=halo[sl][:, fc, :], in_=ps[b][:, TW - 2:TW], func=AF.Identity),
                       reads=[f"ps{b}"], writes=[f"halo{sl}"])
                    op("pool", lambda e: e.tensor_copy(out=H[:, 0:2], in_=halo[1 - sl][:, fc, :]), reads=[f"halo{1 - sl}"], writes=[f"hbh{hi}"])

                def stB(c):
                    i, half = c // 2, c % 2
                    fc = i + half * NAC
                    hi = (base + c) % 2
                    ci = (base + c) % 3
                    H, CV = hb[hi], cv[ci]
                    op("dve", lambda e: e.scalar_tensor_tensor(out=CV[:], in0=H[:, 1:TW + 1], scalar=convp[:, fc, 1:2], in1=CV[:],
                                                               op0=ALU.mult, op1=ALU.add),
                       reads=[f"hb{hi}", f"hbh{hi}", "convp", f"cv{ci}"], writes=[f"cv{ci}"])
                    op("dve", lambda e: e.scalar_tensor_tensor(out=CV[:], in0=H[:, 0:TW], scalar=convp[:, fc, 0:1], in1=CV[:],
                                                               op0=ALU.mult, op1=ALU.add),
                       reads=[f"hb{hi}", f"hbh{hi}", "convp", f"cv{ci}"], writes=[f"cv{ci}"])

                def stC(c):
                    i, half = c // 2, c % 2
                    ci = (base + c) % 3
                    CV = cv[ci]
                    SG = sg[i % 2]
                    if half == 0:
                        op("act", lambda e: e.activation(out=SG[:], in_=CV[:], func=AF.Silu), reads=[f"cv{ci}"], writes=[f"sg{i % 2}"])
                    else:
                        op("pool", lambda e: e.tensor_tensor(out=A[:, i, :], in0=SG[:], in1=CV[:], op=ALU.mult),
                           reads=[f"cv{ci}", f"sg{i % 2}"], writes=[f"act{sl}"])

                nchunk = 2 * NAC
                for st_ in range(nchunk + 2):
                    if st_ < nchunk:
                        stA(st_)
                    if 0 <= st_ - 1 < nchunk:
                        stB(st_ - 1)
                    if 0 <= st_ - 2 < nchunk:
                        stC(st_ - 2)
                    pop_deferred(1)

            def D4(tt):
                sl = tt % 2
                X, A = xt[sl], act[sl]
                for j in range(JB):
                    blk = tt * JB + j
                    R = Rb[blk % 2]
                    rk = f"Rb{blk % 2}"
                    for hh in range(2):
                        b = 4 + (2 * blk + hh) % 4

                        def mmd2(e, j=j, hh=hh, b=b, A=A):
                            for i in range(NAC):
                                ins = e.matmul(ps[b][:], lhsT=A[:, i, j * 128:(j + 1) * 128], rhs=wdb[:, i, hh * 512:(hh + 1) * 512],
                                               start=(i == 0), stop=(i == NAC - 1))
                            return ins
                        op("pe", mmd2, reads=[f"act{sl}", "wdb"], writes=[f"ps{b}"])
                        op("dve", lambda e, hh=hh, b=b, R=R: e.tensor_tensor(out=R[:, hh * 512:(hh + 1) * 512], in0=ps[b][:],
                                                                              in1=gt2[:, hh * 512:(hh + 1) * 512], op=ALU.mult),
                           reads=[f"ps{b}", "gt2"], writes=[rk])
                    op("pool", lambda e, j=j, R=R, X=X: e.tensor_tensor(out=R[:], in0=R[:], in1=X[:, j, :], op=ALU.add),
                       reads=[f"xt4_{sl}", rk], writes=[rk])
                    deferred.extend(ln_tail(R, rk, lnt[:, 0, :], lnt[:, 1, :], st6, mv, sc2, g_eng="dve", b_eng="dve"))
                    deferred.append(lambda blk=blk, R=R, rk=rk: op("sp", lambda e: e.dma_start(out=out_d[blk * 128:(blk + 1) * 128, :], in_=R[:]),
                                                                   reads=[rk], writes=[f"out_d{blk}"], dma=True))

            loads4(0)
            T4(0)
            for tt in range(NT2):
                if tt + 1 < NT2:
                    loads4(tt + 1)
                U4(tt)
                if tt + 1 < NT2:
                    T4(tt + 1)
                D4(tt)
            pop_deferred(len(deferred))
            sy.barrier()
    return nc


def _consts():
    r = np.arange(128)[:, None]
    c = np.arange(128)[None, :]
    dm = (r == c - 1).astype(np.float32) - (r == c).astype(np.float32)
    eb = ((r == 127) & (c == 0)).astype(np.float32)
    six = np.stack([(r == c), (r <= c), (r >= c), (r > c), (r < c), np.ones((128, 128), bool)]).astype(np.float32)
    return np.concatenate([six, dm[None], eb[None]], axis=0)


def prep_shared(w_ada, b_ada, w_in, b_forget, w_fox_proj, w_sb_proj, w_o, ln1_g, ln1_b, w_up, conv_w, conv_b, w_down, ln2_g, ln2_b):
    f = lambda a: np.ascontiguousarray(np.asarray(a, dtype=np.float32))
    w_in = f(w_in)
    w_in_r = np.concatenate([w_in[:, 0:512], w_in[:, 512:1024], w_in[:, 1544:2056], w_in[:, 2056:2568], w_in[:, 3080:4104],
                             w_in[:, 4104:5128], w_in[:, 1024:1536], w_in[:, 2568:3080], w_in[:, 1536:1544]], axis=1)
    convp = np.stack([f(conv_w)[0], f(conv_w)[1], f(conv_w)[2], f(conv_b)], axis=1).reshape(NFC, 128, 4).transpose(1, 0, 2)
    lnt = np.stack([np.broadcast_to(f(v)[None, :], (128, D)) for v in (ln1_g, ln1_b, ln2_g, ln2_b)])
    return {
        "w_ada": f(w_ada), "b_ada": f(b_ada).reshape(12, 1, 512), "w_in": f(w_in_r),
        "bft": f(np.broadcast_to(np.tile(f(b_forget), 4)[None, :], (128, 32))),
        "w_fox": f(w_fox_proj), "w_sb": f(w_sb_proj), "w_o": f(w_o), "lnt": f(lnt), "w_up": f(w_up),
        "convp": f(convp), "w_down": f(w_down), "consts": _consts(),
    }


def kernel(x, c, w_ada, b_ada, w_in, b_forget, w_fox_proj, w_sb_proj, w_o, ln1_g, ln1_b, w_up, conv_w, conv_b, w_down, ln2_g, ln2_b):
    x = np.asarray(x, dtype=np.float32)
    c = np.asarray(c, dtype=np.float32)
    B, S, _ = x.shape
    shared = prep_shared(w_ada, b_ada, w_in, b_forget, w_fox_proj, w_sb_proj, w_o, ln1_g, ln1_b, w_up, conv_w, conv_b, w_down, ln2_g, ln2_b)
    nc = build_nc(S)
    in_maps = []
    for b in range(B):
        m = dict(shared)
        m["x"] = np.ascontiguousarray(x[b])
        m["cT"] = np.ascontiguousarray(c[b].reshape(8, 128).T)
        in_maps.append(m)
    res = run_bass_kernel_spmd(nc, in_maps, core_ids=list(range(B)))
    return np.stack([r["out"] for r in res.results], axis=0).astype(np.float32)
```

```python
import numpy as np
from contextlib import ExitStack
import concourse.bass as bass
import concourse.mybir as mybir
from concourse.bass_utils import run_bass_kernel_spmd

F32 = mybir.dt.float32
BF16 = mybir.dt.bfloat16
AF = mybir.ActivationFunctionType
ALU = mybir.AluOpType
AX = mybir.AxisListType

D = 1024
DFF = 2816
NH = 8
DH = 64
NFC = 2 * DFF // 128
NAC = DFF // 128
ALPHA = 2.0 ** 0.25
EPS = 1e-5


class Sync:
    def __init__(self, nc, st):
        self.nc = nc
        self.st = st
        self.nsem = 0
        self.engs = {"sp": nc.sync, "act": nc.scalar, "dve": nc.vector, "pool": nc.gpsimd, "pe": nc.tensor}
        self.cs = {e: [self._newsem(), 0] for e in self.engs}
        self.waited = {e: {} for e in self.engs}
        self.lastw = {}
        self.readers = {}
        self.NDS = 8
        self.dsem = {e: [self._newsem() for _ in range(self.NDS)] for e in ("sp", "pool")}
        self.dcnt = {e: [0] * self.NDS for e in ("sp", "pool")}
        self.dn = {e: 0 for e in ("sp", "pool")}

    def _newsem(self):
        self.nsem += 1
        return self.st.enter_context(self.nc.semaphore(f"sy{self.nsem}"))

    def _wait(self, eng, sem, val):
        w = self.waited[eng]
        if w.get(sem, 0) >= val:
            return
        self.engs[eng].wait_ge(sem, val)
        w[sem] = val

    def op(self, eng, fn, reads=(), writes=(), dma=False):
        deps = []
        for k in reads:
            lw = self.lastw.get(k)
            if lw is not None:
                if lw[2] != eng or lw[3] or eng in ("act", "dve", "pool"):
                    deps.append(lw)
        for k in writes:
            lw = self.lastw.get(k)
            if lw is not None and (lw[2] != eng or lw[3] or eng in ("act", "dve", "pool")):
                deps.append(lw)
            for r in self.readers.get(k, {}).values():
                if r[2] != eng or r[3] or eng in ("act", "dve", "pool"):
                    deps.append(r)
        for d in deps:
            self._wait(eng, d[0], d[1])
        e = self.engs[eng]
        if dma:
            i = self.dn[eng] % self.NDS
            self.dn[eng] += 1
            sem = self.dsem[eng][i]
            if self.dcnt[eng][i] > 0:
                self._wait(eng, sem, self.dcnt[eng][i])
            ins = fn(e)
            self.dcnt[eng][i] += 16
            ins.then_inc(sem, 16)
            ref = (sem, self.dcnt[eng][i], eng, True)
            rkey = ("dma", eng, i)
        else:
            c = self.cs[eng]
            if c[1] >= 30000:
                c[0] = self._newsem()
                c[1] = 0
            ins = fn(e)
            c[1] += 1
            ins.then_inc(c[0], 1)
            ref = (c[0], c[1], eng, False)
            rkey = eng
        for k in reads:
            self.readers.setdefault(k, {})[rkey] = ref
        for k in writes:
            self.lastw[k] = ref
            self.readers[k] = {}
        return ref

    def barrier(self):
        refs = [(c[0], c[1]) for c in self.cs.values() if c[1] > 0]
        for e in self.dsem:
            for i in range(self.NDS):
                if self.dcnt[e][i] > 0:
                    refs.append((self.dsem[e][i], self.dcnt[e][i]))
        for eng in self.engs:
            for (s, v) in refs:
                if s is self.cs[eng][0]:
                    continue
                self._wait(eng, s, v)
        self.lastw.clear()
        self.readers.clear()


def build_nc(S, dbg=False):
    assert S % 512 == 0
    NT = S // 512
    NB = S // 128
    nc = bass.Bass("TRN2", target_bir_lowering=False)
    okind = "ExternalOutput" if dbg else "Internal"

    def din(name, shape, dt=F32):
        return nc.dram_tensor(name, list(shape), dt, kind="ExternalInput").ap()

    x_d = din("x", [S, D])
    cT_d = din("cT", [128, 8])
    wada_d = din("w_ada", [D, 6 * D])
    bada_d = din("b_ada", [12, 1, 512])
    win_d = din("w_in", [D, 5128])
    bft_d = din("bft", [128, 32])
    wfox_d = din("w_fox", [512, D])
    wsb_d = din("w_sb", [512, D])
    wo_d = din("w_o", [D, D])
    lnt_d = din("lnt", [4, 128, D])
    wup_d = din("w_up", [D, 2 * DFF])
    convp_d = din("convp", [128, NFC, 4])
    wdown_d = din("w_down", [DFF, D])
    consts_d = din("consts", [8, 128, 128])
    out_d = nc.dram_tensor("out", [S, D], F32, kind="ExternalOutput").ap()

    fm_d = nc.dram_tensor("fm_s", [32, 128, S], BF16, kind=okind).ap()
    v_d = nc.dram_tensor("v_s", [2, S, 512], BF16, kind=okind).ap()
    caug_d = nc.dram_tensor("caug_s", [8, 3, S], BF16, kind=okind).ap()
    yT_d = nc.dram_tensor("yT_s", [2, 512, S], BF16, kind=okind).ap()
    x1_d = nc.dram_tensor("x1_s", [S, D], F32, kind=okind).ap()
    dv_d = nc.dram_tensor("dv_s", [S, 512], BF16, kind=okind).ap()
    gt_d = nc.dram_tensor("gt_s", [2, 128, D], F32).ap()

    with ExitStack() as st:
        E = st.enter_context
        sy = Sync(nc, st)
        op = sy.op
        ps = [E(nc.psum_tensor(f"psb{i}", [128, 512], F32)) for i in range(8)]

        identf = E(nc.sbuf_tensor("identf", [128, 128], F32))
        mhalf = E(nc.sbuf_tensor("mhalf", [128, 1], F32))
        modp = E(nc.sbuf_tensor("modp", [128, 4, 8], F32))
        p012 = ExitStack()
        E2 = p012.enter_context
        cf = E2(nc.sbuf_tensor("cf", [128, 8, 128], F32))
        cb = E2(nc.sbuf_tensor("cb", [128, 8, 128], BF16))
        ctok = E2(nc.sbuf_tensor("ctok", [128, NB, 8], F32))
        cql = E2(nc.sbuf_tensor("cql", [128, NT + 1, 8], F32))
        IDENT, UINC, LINC, LSTR, USTR, ONES, DM, EB = range(8)

        op("sp", lambda e: e.dma_start(out=identf[:], in_=consts_d[0]), writes=["identf"], dma=True)
        op("dve", lambda e: e.memset(mhalf[:], -0.5), writes=["mhalf"])
        op("sp", lambda e: e.dma_start(out=cf[:], in_=consts_d.rearrange("c p n -> p c n")), writes=["cf"], dma=True)
        op("pool", lambda e: e.dma_start(out=cb[:], in_=consts_d.rearrange("c p n -> p c n")), writes=["cb"], dma=True)

        with ExitStack() as p0:
            P = p0.enter_context
            cts = P(nc.sbuf_tensor("cts", [128, 8], F32))
            modtab = P(nc.sbuf_tensor("modtab", [128, 6 * D], F32))
            crep = P(nc.sbuf_tensor("crep", [128, 8, 128], F32))
            wa = [P(nc.sbuf_tensor(f"wa{i}", [128, 8, 512], F32)) for i in range(2)]
            bad = [P(nc.sbuf_tensor(f"bad{i}", [1, 512], F32)) for i in range(2)]
            dtmp = P(nc.sbuf_tensor("dtmp", [128, 8, 128], F32))
            op("sp", lambda e: e.dma_start(out=cts[:], in_=cT_d), writes=["cts"], dma=True)
            for kc in range(8):
                op("dve", lambda e, kc=kc: e.tensor_copy(out=crep[:, kc, :], in_=cts[:, kc:kc + 1].to_broadcast([128, 128])),
                   reads=["cts"], writes=[f"crep{kc}"])
            for g in range(12):
                sl = g % 2
                op("sp", lambda e, g=g, sl=sl: e.dma_start(
                    out=wa[sl][:], in_=wada_d[:, g * 512:(g + 1) * 512].rearrange("(kc p) n -> p kc n", p=128)),
                   writes=[f"wa{sl}"], dma=True)
                op("sp", lambda e, g=g, sl=sl: e.dma_start(out=bad[sl][:], in_=bada_d[g]), writes=[f"bad{sl}"], dma=True)
                bank = ps[g % 2]

                def mm(e, g=g, sl=sl, bank=bank):
                    for kc in range(8):
                        e.matmul(bank[:], lhsT=crep[:, kc, :], rhs=wa[sl][:, kc, :], start=(kc == 0), stop=False)
                    return e.matmul(bank[:], lhsT=cf[0:1, ONES, :], rhs=bad[sl][0:1, :], start=False, stop=True)
                op("pe", mm, reads=[f"wa{sl}", f"bad{sl}", "cf"] + [f"crep{k}" for k in range(8)], writes=[f"ps{g % 2}"])
                addone = g in (2, 3, 8, 9)
                if g in (4, 5, 10, 11):
                    op("dve", lambda e, g=g, bank=bank: e.tensor_scalar(out=modtab[:, g * 512:(g + 1) * 512], in0=bank[:], scalar1=1.0,
                                                                        scalar2=1.0 / ALPHA, op0=ALU.add, op1=ALU.mult),
                       reads=[f"ps{g % 2}"], writes=[f"modtab{g}"])
                elif addone:
                    op("dve", lambda e, g=g, bank=bank: e.tensor_scalar_add(out=modtab[:, g * 512:(g + 1) * 512], in0=bank[:], scalar1=1.0),
                       reads=[f"ps{g % 2}"], writes=[f"modtab{g}"])
                else:
                    op("act", lambda e, g=g, bank=bank: e.activation(out=modtab[:, g * 512:(g + 1) * 512], in_=bank[:], func=AF.Identity),
                       reads=[f"ps{g % 2}"], writes=[f"modtab{g}"])
            for vi, base in enumerate((0, 1024, 3072, 4096)):
                for k in range(8):
                    op("dve", lambda e, base=base, k=k: e.tensor_tensor(
                        out=dtmp[:, k, :], in0=modtab[:, base + k * 128: base + (k + 1) * 128], in1=cf[:, IDENT, :], op=ALU.mult),
                       reads=[f"modtab{(base + k * 128) // 512}", "cf"], writes=["dtmp"])
                op("dve", lambda e, vi=vi: e.tensor_reduce(out=modp[:, vi, :], in_=dtmp[:], axis=AX.X, op=ALU.add),
                   reads=["dtmp"], writes=["modp"])
            op("sp", lambda e: e.dma_start(out=gt_d[0], in_=modtab[:, 2048:3072]), reads=["modtab4", "modtab5"], writes=["gt_d0"], dma=True)
            op("sp", lambda e: e.dma_start(out=gt_d[1], in_=modtab[:, 5120:6144]), reads=["modtab10", "modtab11"], writes=["gt_d1"], dma=True)
            sy.barrier()

        with ExitStack() as p1:
            P = p1.enter_context
            winb = P(nc.sbuf_tensor("winb", [128, 8, 5128], BF16))
            xt = [P(nc.sbuf_tensor(f"xt{i}", [128, 4, D], F32)) for i in range(2)]
            uT = [P(nc.sbuf_tensor(f"uT{i}", [128, 8, 512], BF16)) for i in range(2)]
            fst = [P(nc.sbuf_tensor(f"fst{i}", [128, 4, 512], BF16)) for i in range(3)]
            vst = [P(nc.sbuf_tensor(f"vst{i}", [128, 4, 1024], BF16)) for i in range(2)]
            dvst = [P(nc.sbuf_tensor(f"dvst{i}", [128, 4, 512], BF16)) for i in range(2)]
            bft = P(nc.sbuf_tensor("bft_sb", [128, 32], F32))
            fb = P(nc.sbuf_tensor("fb", [128, 32], F32))
            ef = P(nc.sbuf_tensor("ef", [128, 32], F32))
            lf = [P(nc.sbuf_tensor(f"lf{i}", [128, 32], F32)) for i in range(2)]
            d32 = P(nc.sbuf_tensor("d32", [8, 512], F32))
            r1 = P(nc.sbuf_tensor("r1", [8, 512], F32))
            r2 = P(nc.sbuf_tensor("r2", [8, 512], F32))
            cst = [P(nc.sbuf_tensor(f"cst{i}", [8, 3, 512], BF16)) for i in range(2)]

            for kc in range(8):
                op("pool", lambda e, kc=kc: e.dma_start(out=winb[:, kc, :], in_=win_d[kc * 128:(kc + 1) * 128, :]),
                   writes=[f"winb{kc}"], dma=True)
            WIN = [f"winb{k}" for k in range(8)]
            op("sp", lambda e: e.dma_start(out=bft[:], in_=bft_d), writes=["bft"], dma=True)
            op("dve", lambda e: e.memset(cql[:, 0, :], 0.0), writes=["cql"])

            def load_x(tt):
                op("sp", lambda e: e.dma_start(out=xt[tt % 2][:], in_=x_d[tt * 512:(tt + 1) * 512, :].rearrange("(j p) d -> p j d", p=128)),
                   writes=[f"xt{tt % 2}"], dma=True)
            load_x(0)
            pcnt = [0]

            def nbank():
                pcnt[0] += 1
                return pcnt[0] % 6
            fcnt = [0]
            for tt in range(NT):
                if tt + 1 < NT:
                    load_x(tt + 1)
                X = xt[tt % 2]
                U = uT[tt % 2]
                ukeys = [f"uT{tt % 2}_{k}" for k in range(8)]
                for kc in range(8):
                    b = nbank()

                    def tr(e, kc=kc, b=b, X=X):
                        for j in range(4):
                            ins = e.transpose(out=ps[b][:, j * 128:(j + 1) * 128], in_=X[:, j, kc * 128:(kc + 1) * 128], identity=cf[:, IDENT, :])
                        return ins
                    op("pe", tr, reads=[f"xt{tt % 2}", "cf"], writes=[f"ps{b}"])
                    if kc % 2 == 0:
                        op("act", lambda e, kc=kc, b=b, U=U: e.activation(out=U[:, kc, :], in_=ps[b][:], func=AF.Identity,
                                                                          bias=modp[:, 0, kc:kc + 1], scale=modp[:, 1, kc:kc + 1]),
                           reads=[f"ps{b}", "modp"], writes=[ukeys[kc]])
                    else:
                        op("dve", lambda e, kc=kc, b=b, U=U: e.tensor_scalar(out=U[:, kc, :], in0=ps[b][:], scalar1=modp[:, 1, kc:kc + 1],
                                                                             scalar2=modp[:, 0, kc:kc + 1], op0=ALU.mult, op1=ALU.add),
                           reads=[f"ps{b}", "modp"], writes=[ukeys[kc]])
                for oc in range(32):
                    b = nbank()

                    def mm(e, oc=oc, b=b, U=U):
                        for kc in range(8):
                            ins = e.matmul(ps[b][:], lhsT=winb[:, kc, oc * 128:(oc + 1) * 128], rhs=U[:, kc, :], start=(kc == 0), stop=(kc == 7))
                        return ins
                    op("pe", mm, reads=WIN + ukeys, writes=[f"ps{b}"])
                    fs = fcnt[0] % 3
                    dst = fst[fs][:, oc % 4, :]
                    if oc < 16 and (oc // 4) % 2 == 0:
                        op("dve", lambda e, b=b, dst=dst: e.tensor_scalar_mul(out=dst, in0=ps[b][:], scalar1=0.125),
                           reads=[f"ps{b}"], writes=[f"fst{fs}"])
                    elif oc < 16:
                        op("dve", lambda e, b=b, dst=dst: e.tensor_copy(out=dst, in_=ps[b][:]), reads=[f"ps{b}"], writes=[f"fst{fs}"])
                    else:
                        op("act", lambda e, b=b, dst=dst: e.activation(out=dst, in_=ps[b][:], func=AF.Sigmoid),
                           reads=[f"ps{b}"], writes=[f"fst{fs}"])
                    if oc % 4 == 3:
                        oc0 = oc - 3
                        op("sp", lambda e, oc0=oc0, fs=fs, tt=tt: e.dma_start(
                            out=fm_d[oc0:oc0 + 4, :, tt * 512:(tt + 1) * 512].rearrange("c p t -> p c t"), in_=fst[fs][:]),
                           reads=[f"fst{fs}"], writes=[f"fm_d{oc0}_{tt}"], dma=True)
                        fcnt[0] += 1
                VS = vst[tt % 2]
                for j in range(4):
                    for ab in range(2):
                        b = nbank()

                        def mmv(e, j=j, ab=ab, b=b, U=U):
                            for kc in range(8):
                                ins = e.matmul(ps[b][:], lhsT=U[:, kc, j * 128:(j + 1) * 128],
                                               rhs=winb[:, kc, 4096 + ab * 512: 4096 + (ab + 1) * 512], start=(kc == 0), stop=(kc == 7))
                            return ins
                        op("pe", mmv, reads=WIN + ukeys, writes=[f"ps{b}"])
                        dst = VS[:, j, ab * 512:(ab + 1) * 512]
                        if ab == 0:
                            op("act", lambda e, b=b, dst=dst: e.activation(out=dst, in_=ps[b][:], func=AF.Identity),
                               reads=[f"ps{b}"], writes=[f"vst{tt % 2}"])
                        else:
                            op("dve", lambda e, b=b, dst=dst: e.tensor_copy(out=dst, in_=ps[b][:]), reads=[f"ps{b}"], writes=[f"vst{tt % 2}"])
                for ab in range(2):
                    op("sp", lambda e, ab=ab, tt=tt, VS=VS: e.dma_start(
                        out=v_d[ab, tt * 512:(tt + 1) * 512, :].rearrange("(j p) d -> p j d", p=128), in_=VS[:, :, ab * 512:(ab + 1) * 512]),
                       reads=[f"vst{tt % 2}"], writes=[f"v_d{ab}_{tt}"], dma=True)
                DVS = dvst[tt % 2]
                for j in range(4):
                    b = nbank()
                    blk = tt * 4 + j
                    prev = VS[:, j - 1, 512:1024] if j > 0 else vst[(tt - 1) % 2][:, 3, 512:1024]

                    def mmdv(e, j=j, b=b, blk=blk, prev=prev, VS=VS):
                        ins = e.matmul(ps[b][:], lhsT=cb[:, DM, :], rhs=VS[:, j, 512:1024], start=True, stop=(blk == 0))
                        if blk > 0:
                            ins = e.matmul(ps[b][:], lhsT=cb[:, EB, :], rhs=prev, start=False, stop=True)
                        return ins
                    op("pe", mmdv, reads=[f"vst{tt % 2}", f"vst{(tt - 1) % 2}", "cb"], writes=[f"ps{b}"])
                    op("act", lambda e, j=j, b=b, DVS=DVS: e.activation(out=DVS[:, j, :], in_=ps[b][:], func=AF.Identity),
                       reads=[f"ps{b}"], writes=[f"dvst{tt % 2}"])
                op("sp", lambda e, tt=tt, DVS=DVS: e.dma_start(out=dv_d[tt * 512:(tt + 1) * 512, :].rearrange("(j p) d -> p j d", p=128), in_=DVS[:]),
                   reads=[f"dvst{tt % 2}"], writes=[f"dv_d{tt}"], dma=True)
                LF = lf[tt % 2]

                def mmf(e, U=U):
                    for j in range(4):
                        for kc in range(8):
                            ins = e.matmul(ps[6][:, j * 8:(j + 1) * 8], lhsT=U[:, kc, j * 128:(j + 1) * 128], rhs=winb[:, kc, 5120:5128],
                                           start=(kc == 0), stop=(kc == 7))
                    return ins
                op("pe", mmf, reads=WIN + ukeys, writes=["ps6"])
                op("dve", lambda e: e.tensor_tensor(out=fb[:], in0=ps[6][:, 0:32], in1=bft[:], op=ALU.add), reads=["ps6", "bft"], writes=["fb"])
                op("act", lambda e: e.activation(out=ef[:], in_=fb[:], func=AF.Exp, scale=-1.0), reads=["fb"], writes=["ef"])
                op("act", lambda e, LF=LF: e.activation(out=LF[:], in_=ef[:], func=AF.Ln, bias=1.0, scale=1.0), reads=["ef"], writes=[f"lf{tt % 2}"])

                def mmc(e, LF=LF):
                    for j in range(4):
                        ins = e.matmul(ps[6][:, 64 + j * 8: 64 + (j + 1) * 8], lhsT=cf[:, UINC, :], rhs=LF[:, j * 8:(j + 1) * 8], start=True, stop=(j == 0))
                        for jp in range(j):
                            ins = e.matmul(ps[6][:, 64 + j * 8: 64 + (j + 1) * 8], lhsT=cf[:, ONES, :], rhs=LF[:, jp * 8:(jp + 1) * 8],
                                           start=False, stop=(jp == j - 1))
                    for jp in range(4):
                        ins = e.matmul(ps[6][:, 96:104], lhsT=cf[:, ONES, :], rhs=LF[:, jp * 8:(jp + 1) * 8], start=(jp == 0), stop=(jp == 3))
                    return ins
                op("pe", mmc, reads=[f"lf{tt % 2}", "cf"], writes=["ps6"])
                for j in range(4):
                    op("dve", lambda e, j=j, tt=tt: e.tensor_tensor(out=ctok[:, tt * 4 + j, :], in0=ps[6][:, 64 + j * 8: 64 + (j + 1) * 8],
                                                                    in1=cql[:, tt, :], op=ALU.add),
                       reads=["ps6", "cql"], writes=["ctok"])
                op("dve", lambda e, tt=tt: e.tensor_tensor(out=cql[:, tt + 1, :], in0=ps[6][:, 96:104], in1=cql[:, tt, :], op=ALU.add),
                   reads=["ps6", "cql"], writes=["cql"])

                def mmd(e, LF=LF):
                    for j in range(4):
                        ins = e.matmul(ps[7][0:8, j * 128:(j + 1) * 128], lhsT=LF[:, j * 8:(j + 1) * 8], rhs=cf[:, LSTR, :], start=True, stop=(j == 3))
                        for jp in range(j + 1, 4):
                            ins = e.matmul(ps[7][0:8, j * 128:(j + 1) * 128], lhsT=LF[:, jp * 8:(jp + 1) * 8], rhs=cf[:, ONES, :],
                                           start=False, stop=(jp == 3))
                    return ins
                op("pe", mmd, reads=[f"lf{tt % 2}", "cf"], writes=["ps7"])
                CS = cst[tt % 2]
                ck = f"cst{tt % 2}"
                op("dve", lambda e: e.tensor_copy(out=d32[:], in_=ps[7][0:8, :]), reads=["ps7"], writes=["d32"])
                op("dve", lambda e, CS=CS: e.tensor_copy(out=CS[:, 0, :], in_=d32[:]), reads=["d32"], writes=[ck])
                op("dve", lambda e, CS=CS: e.tensor_tensor(out=r1[:], in0=d32[:], in1=CS[:, 0, :], op=ALU.subtract), reads=["d32", ck], writes=["r1"])
                op("dve", lambda e, CS=CS: e.tensor_copy(out=CS[:, 1, :], in_=r1[:]), reads=["r1"], writes=[ck])
                op("dve", lambda e, CS=CS: e.tensor_tensor(out=r2[:], in0=r1[:], in1=CS[:, 1, :], op=ALU.subtract), reads=["r1", ck], writes=["r2"])
                op("dve", lambda e, CS=CS: e.tensor_copy(out=CS[:, 2, :], in_=r2[:]), reads=["r2"], writes=[ck])
                op("sp", lambda e, CS=CS, tt=tt: e.dma_start(out=caug_d[:, :, tt * 512:(tt + 1) * 512], in_=CS[:]),
                   reads=[ck], writes=[f"caug_d{tt}"], dma=True)
            sy.barrier()

        p3w = ExitStack()
        wfb = p3w.enter_context(nc.sbuf_tensor("wfb", [128, 4, D], BF16))
        wsbb = p3w.enter_context(nc.sbuf_tensor("wsbb", [128, 4, D], BF16))
        wob = p3w.enter_context(nc.sbuf_tensor("wob", [128, 8, D], BF16))
        pend_w = []
        for c in range(4):
            pend_w.append(lambda c=c: op("pool", lambda e: e.dma_start(out=wfb[:, c, :], in_=wfox_d[c * 128:(c + 1) * 128, :]), writes=["wfb"], dma=True))
            pend_w.append(lambda c=c: op("pool", lambda e: e.dma_start(out=wsbb[:, c, :], in_=wsb_d[c * 128:(c + 1) * 128, :]), writes=["wsbb"], dma=True))
        for c in range(8):
            pend_w.append(lambda c=c: op("pool", lambda e: e.dma_start(out=wob[:, c, :], in_=wo_d[c * 128:(c + 1) * 128, :]), writes=["wob"], dma=True))

        with ExitStack() as p2:
            P = p2.enter_context
            KT = [P(nc.sbuf_tensor(f"KT{i}", [128, S], BF16)) for i in range(2)]
            QT = [P(nc.sbuf_tensor(f"QT{i}", [128, S], BF16)) for i in range(2)]
            VV = [P(nc.sbuf_tensor(f"VV{i}", [128, NB, 65], BF16)) for i in range(2)]
            bias_t = [P(nc.sbuf_tensor(f"bias{i}", [128, NT, NB], F32)) for i in range(2)]
            zeros = P(nc.sbuf_tensor("zeros", [128, 64], BF16))
            Pb = [P(nc.sbuf_tensor(f"Pb{i}", [128, 512], BF16)) for i in range(4)]
            dVV = [P(nc.sbuf_tensor(f"dVV{i}", [128, NB, 64], BF16)) for i in range(2)]
            wbuf = [P(nc.sbuf_tensor(f"wbuf{i}", [128, 512], F32)) for i in range(3)]
            Gs = [P(nc.sbuf_tensor(f"Gs{i}", [128, 512], BF16)) for i in range(4)]
            GT = [P(nc.sbuf_tensor(f"GT{i}", [128, 512], BF16)) for i in range(3)]
            onesf = P(nc.sbuf_tensor("onesf", [128, 512], F32))
            rs = [P(nc.sbuf_tensor(f"rs{i}", [65, 512], F32)) for i in range(2)]
            ysb = [P(nc.sbuf_tensor(f"ysb{i}", [64, 512], F32)) for i in range(2)]
            yo = [P(nc.sbuf_tensor(f"yo{i}", [64, 512], BF16)) for i in range(2)]

            op("dve", lambda e: e.memset(zeros[:], 0.0), writes=["zeros"])
            for i in range(2):
                op("dve", lambda e, i=i: e.memset(KT[i][64:67, :], 1.0), writes=[f"KTa{i}"])
                op("dve", lambda e, i=i: e.memset(VV[i][:, :, 64:65], 1.0), writes=[f"VVa{i}"])

            def fox_loads(h):
                sl = h % 2
                hp, hf = h // 2, h % 2
                op("sp", lambda e: e.dma_start(out=KT[sl][0:64, :], in_=fm_d[4 + hp, hf * 64:(hf + 1) * 64, :]), writes=[f"KT{sl}"], dma=True)
                op("sp", lambda e: e.dma_start(out=QT[sl][0:64, :], in_=fm_d[0 + hp, hf * 64:(hf + 1) * 64, :]), writes=[f"QT{sl}"], dma=True)
                op("sp", lambda e: e.dma_start(out=QT[sl][64:67, :], in_=caug_d[h]), writes=[f"QTa{sl}"], dma=True)
                for k0 in range(0, NB, 8):
                    op("sp", lambda e, k0=k0: e.dma_start(out=VV[sl][:, k0:k0 + 8, 0:64],
                                                          in_=v_d[0, k0 * 128:(k0 + 8) * 128, h * 64:(h + 1) * 64].rearrange("(kb p) d -> p kb d", p=128)),
                       writes=[f"VV{sl}"], dma=True)
                for qt in range(NT):
                    op("dve", lambda e, qt=qt: e.tensor_scalar(out=bias_t[sl][:, qt, :], in0=ctok[:, :, h], scalar1=cql[:, qt + 1, h:h + 1],
                                                               scalar2=None, op0=ALU.subtract),
                       reads=["ctok", "cql"], writes=[f"bias{sl}"])

            items = []
            for h in range(NH):
                for qt in range(NT):
                    n = 4 * qt + 4
                    for kb in range(n):
                        items.append(dict(h=h, qt=qt, kb=kb, first=(kb == 0), last=(kb == n - 1), hfirst=(qt == 0 and kb == 0)))
            for i, it in enumerate(items):
                it["i"] = i
                it["tile"] = it["h"] * NT + it["qt"]
                j = it["kb"] - 4 * it["qt"]
                it["j"] = j
                it["c0"] = 128 * j if j > 0 else 0
            nper = len(items) // NH
            assert nper >= 10
            fox_loads(0)

            def fx_z(it):
                i, h, qt, kb, c0 = it["i"], it["h"], it["qt"], it["kb"], it["c0"]
                sl = h % 2
                if it["hfirst"] and h + 1 < NH:
                    pass
                b = i % 3
                op("pe", lambda e: e.matmul(ps[b][:, c0:512], lhsT=KT[sl][0:67, kb * 128:(kb + 1) * 128],
                                            rhs=QT[sl][0:67, qt * 512 + c0:(qt + 1) * 512], start=True, stop=True),
                   reads=[f"KT{sl}", f"KTa{sl}", f"QT{sl}", f"QTa{sl}"], writes=[f"ps{b}"])

            def fx_p(it):
                i, h, qt, kb, c0 = it["i"], it["h"], it["qt"], it["kb"], it["c0"]
                sl = h % 2
                b = i % 3
                pb = i % 4
                op("act", lambda e: e.activation(out=Pb[pb][:, c0:512], in_=ps[b][:, c0:512], func=AF.Exp,
                                                 bias=bias_t[sl][:, qt, kb:kb + 1], scale=1.0),
                   reads=[f"ps{b}", f"bias{sl}"], writes=[f"Pb{pb}"])
                if it["j"] >= 0:
                    op("pool", lambda e: e.tensor_tensor(out=Pb[pb][:, c0:c0 + 128], in0=Pb[pb][:, c0:c0 + 128], in1=cb[:, UINC, :], op=ALU.mult),
                       reads=[f"Pb{pb}", "cb"], writes=[f"Pb{pb}"])

            def fx_pv(it):
                i, h, qt, kb, c0 = it["i"], it["h"], it["qt"], it["kb"], it["c0"]
                sl = h % 2
                pb = i % 4
                yb = 4 + it["tile"] % 2
                op("pe", lambda e: e.matmul(ps[yb][0:65, c0:512], lhsT=VV[sl][:, kb, 0:65], rhs=Pb[pb][:, c0:512],
                                            start=it["first"], stop=it["last"]),
                   reads=[f"VV{sl}", f"VVa{sl}", f"Pb{pb}"], writes=[f"ps{yb}"])

            def fx_fin1(it):
                if not it["last"]:
                    return
                t2 = it["tile"] % 2
                yb = 4 + t2
                op("dve", lambda e: e.reciprocal(out=rs[t2][64:65, :], in_=ps[yb][64:65, :]), reads=[f"ps{yb}"], writes=[f"rs{t2}"])
                op("act", lambda e: e.activation(out=ysb[t2][:], in_=ps[yb][0:64, :], func=AF.Identity), reads=[f"ps{yb}"], writes=[f"ysb{t2}"])

            def fx_fin2(it):
                if not it["last"]:
                    return
                h, qt = it["h"], it["qt"]
                t2 = it["tile"] % 2
                op("pe", lambda e: e.matmul(ps[6][0:64, :], lhsT=cf[64:65, ONES, 0:64], rhs=rs[t2][64:65, :], start=True, stop=True),
                   reads=[f"rs{t2}", "cf"], writes=["ps6"])
                op("dve", lambda e: e.tensor_tensor(out=yo[t2][:], in0=ysb[t2][:], in1=ps[6][0:64, :], op=ALU.mult),
                   reads=[f"ysb{t2}", "ps6"], writes=[f"yo{t2}"])
                op("sp", lambda e: e.dma_start(out=yT_d[0, h * 64:(h + 1) * 64, qt * 512:(qt + 1) * 512], in_=yo[t2][:]),
                   reads=[f"yo{t2}"], writes=[f"yT_d0_{h}_{qt}"], dma=True)

            stages = [(fx_z, 0), (fx_p, 1), (fx_pv, 2), (fx_fin1, 3), (fx_fin2, 9)]
            n = len(items)
            for s in range(n + 10):
                for fn, sk in stages:
                    i = s - sk
                    if 0 <= i < n:
                        fn(items[i])
                if pend_w and s % 48 == 30:
                    pend_w.pop(0)()
                if s < n and (s % nper) == 6:
                    h = items[s]["h"]
                    if h + 1 < NH:
                        fox_loads(h + 1)
            while pend_w:
                pend_w.pop(0)()
            sy.barrier()

            op("dve", lambda e: e.memset(onesf[:], 1.0), writes=["onesf"])

            def sb_loads(hp):
                sl = hp % 2
                op("sp", lambda e: e.dma_start(out=KT[sl][:, :], in_=fm_d[12 + hp]), writes=[f"KT{sl}"], dma=True)
                op("sp", lambda e: e.dma_start(out=QT[sl][:, :], in_=fm_d[8 + hp]), writes=[f"QT{sl}"], dma=True)

            def sb_vload(h):
                sl = h % 2
                for k0 in range(0, NB, 8):
                    op("sp", lambda e, k0=k0: e.dma_start(out=VV[sl][:, k0:k0 + 8, 0:64],
                                                          in_=v_d[1, k0 * 128:(k0 + 8) * 128, h * 64:(h + 1) * 64].rearrange("(kb p) d -> p kb d", p=128)),
                       writes=[f"VV{sl}"], dma=True)
                    op("sp", lambda e, k0=k0: e.dma_start(out=dVV[sl][:, k0:k0 + 8, :],
                                                          in_=dv_d[k0 * 128:(k0 + 8) * 128, h * 64:(h + 1) * 64].rearrange("(kb p) d -> p kb d", p=128)),
                       writes=[f"dVV{sl}"], dma=True)

            items = []
            for h in range(NH):
                for qb in range(NB):
                    a, jq = qb // 4, qb % 4
                    for kt in range(a, -1, -1):
                        W = 128 * (jq + 1) if kt == a else 512
                        items.append(dict(h=h, qb=qb, a=a, jq=jq, kt=kt, W=W, first=(kt == a), last=(kt == 0)))
            for i, it in enumerate(items):
                it["i"] = i
            nper = len(items) // NH
            assert nper >= 12
            sb_loads(0)
            sb_vload(0)
            GTV = [ps[2][:].bitcast(BF16), ps[3][:].bitcast(BF16), ps[4][:].bitcast(BF16)]

            def sb_z(it):
                i, h, qb, kt, W = it["i"], it["h"], it["qb"], it["kt"], it["W"]
                sl = (h // 2) % 2
                p0_ = 64 * (h % 2)
                b = i % 2
                op("pe", lambda e: e.matmul(ps[b][:, 0:W], lhsT=QT[sl][p0_:p0_ + 64, qb * 128:(qb + 1) * 128],
                                            rhs=KT[sl][p0_:p0_ + 64, kt * 512: kt * 512 + W], start=True, stop=True),
                   reads=[f"KT{sl}", f"QT{sl}"], writes=[f"ps{b}"])

            def sb_w(it):
                i, W, jq = it["i"], it["W"], it["jq"]
                b = i % 2
                wi = i % 3
                op("act", lambda e: e.activation(out=wbuf[wi][:, 0:W], in_=ps[b][:, 0:W], func=AF.Sigmoid, scale=-1.0),
                   reads=[f"ps{b}"], writes=[f"wbuf{wi}"])
                if it["first"]:
                    c0 = 128 * jq
                    op("pool", lambda e: e.tensor_tensor(out=wbuf[wi][:, c0:c0 + 128], in0=wbuf[wi][:, c0:c0 + 128], in1=cf[:, LSTR, :], op=ALU.mult),
                       reads=[f"wbuf{wi}", "cf"], writes=[f"wbuf{wi}"])
                    op("pool", lambda e: e.tensor_tensor(out=wbuf[wi][:, c0:c0 + 128], in0=wbuf[wi][:, c0:c0 + 128], in1=cf[:, UINC, :], op=ALU.add),
                       reads=[f"wbuf{wi}", "cf"], writes=[f"wbuf{wi}"])

            def rev(ap_, n):
                return bass.AP(ap_.tensor, ap_.offset + (n - 1), [[ap_.ap[0][0], 128], [-1, n]])

            def sb_scan(it):
                i, W, jq = it["i"], it["W"], it["jq"]
                wi = i % 3
                gi = i % 4
                if it["first"]:
                    init = 1.0
                    rd = [f"wbuf{wi}", "onesf"]
                else:
                    init = Gs[(i - 1) % 4][:, 0:1]
                    rd = [f"wbuf{wi}", "onesf", f"Gs{(i - 1) % 4}"]
                op("dve", lambda e: e.tensor_tensor_scan(out=rev(Gs[gi][:, 0:W], W), data0=rev(wbuf[wi][:, 0:W], W), data1=onesf[:, 0:W],
                                                         initial=init, op0=ALU.mult, op1=ALU.mult),
                   reads=rd, writes=[f"Gs{gi}"])
                if it["first"]:
                    c0 = 128 * jq
                    op("pool", lambda e: e.tensor_tensor(out=Gs[gi][:, c0:c0 + 128], in0=Gs[gi][:, c0:c0 + 128], in1=cb[:, LINC, :], op=ALU.mult),
                       reads=[f"Gs{gi}", "cb"], writes=[f"Gs{gi}"])

            def sb_tr(it):
                i, W = it["i"], it["W"]
                gi = i % 4
                ti = i % 3
                tv = GTV[ti]
                o0 = 0

                def tr(e):
                    for c in range(W // 128):
                        ins = e.matmul(ps[2 + ti][:, c * 128:(c + 1) * 128], lhsT=Gs[gi][:, c * 128:(c + 1) * 128], rhs=cb[:, IDENT, :],
                                       start=True, stop=True)
                    return ins
                op("pe", tr, reads=[f"Gs{gi}", "cb"], writes=[f"pst{ti}"])

            def sb_ev(it):
                i, W = it["i"], it["W"]
                ti = i % 3
                tv = GTV[ti]
                o0 = 0
                g3 = i % 3
                op("act", lambda e: e.activation(out=GT[g3][:, 0:W], in_=ps[2 + ti][:, 0:W], func=AF.Identity), reads=[f"pst{ti}"], writes=[f"GT{g3}"])

            def sb_pv(it):
                i, h, qb, kt, W, jq, a = it["i"], it["h"], it["qb"], it["kt"], it["W"], it["jq"], it["a"]
                sl = h % 2
                g3 = i % 3
                yb = 5 + (h * NT + a) % 3
                ycol = slice(jq * 128, (jq + 1) * 128)

                def mm(e):
                    if it["first"]:
                        e.matmul(ps[yb][0:64, ycol], lhsT=VV[sl][:, qb, 0:64], rhs=cb[:, IDENT, :], start=True, stop=False)
                    nsub = W // 128
                    for c in range(nsub):
                        ins = e.matmul(ps[yb][0:64, ycol], lhsT=dVV[sl][:, kt * 4 + c, :], rhs=GT[g3][:, c * 128:(c + 1) * 128],
                                       start=False, stop=(it["last"] and c == nsub - 1))
                    return ins
                op("pe", mm, reads=[f"VV{sl}", f"dVV{sl}", f"GT{g3}", "cb"], writes=[f"ps{yb}"])

            def sb_fin(it):
                if not (it["last"] and it["jq"] == 3):
                    return
                h, a = it["h"], it["a"]
                t2 = (h * NT + a) % 2
                yb = 5 + (h * NT + a) % 3
                op("act", lambda e: e.activation(out=yo[t2][:], in_=ps[yb][0:64, :], func=AF.Identity), reads=[f"ps{yb}"], writes=[f"yo{t2}"])
                op("sp", lambda e: e.dma_start(out=yT_d[1, h * 64:(h + 1) * 64, a * 512:(a + 1) * 512], in_=yo[t2][:]),
                   reads=[f"yo{t2}"], writes=[f"yT_d1_{h}_{a}"], dma=True)

            stages = [(sb_z, 0), (sb_w, 1), (sb_scan, 2), (sb_tr, 3), (sb_ev, 4), (sb_pv, 5), (sb_fin, 6)]
            n = len(items)
            for s in range(n + 7):
                for fn, sk in stages:
                    i = s - sk
                    if 0 <= i < n:
                        fn(items[i])
                if s < n and (s % nper) == 9:
                    h = items[s]["h"]
                    if h + 1 < NH:
                        sb_vload(h + 1)
                        if (h + 1) % 2 == 0:
                            sb_loads((h + 1) // 2)
            sy.barrier()

        def ln_tail(R, rk, g_t, b_t, st6, mv, sc2, g_eng="pool", b_eng="pool", sfx=""):
            ops = []
            for hh in range(2):
                ops.append(lambda hh=hh: op("dve", lambda e: e.bn_stats(out=st6[:, hh * 6:(hh + 1) * 6], in_=R[:, hh * 512:(hh + 1) * 512]),
                                            reads=[rk], writes=["st6" + sfx]))
            ops.append(lambda: op("dve", lambda e: e.bn_aggr(out=mv[:], in_=st6[:]), reads=["st6" + sfx], writes=["mv" + sfx]))
            ops.append(lambda: op("dve", lambda e: e.tensor_scalar_add(out=sc2[:, 0:1], in0=mv[:, 1:2], scalar1=EPS / (ALPHA * ALPHA)), reads=["mv" + sfx], writes=["sc2" + sfx]))
            ops.append(lambda: op("pool", lambda e: e.tensor_tensor(out=sc2[:, 0:1], in0=sc2[:, 0:1], in1=mhalf[:, 0:1], op=ALU.pow),
                                  reads=["sc2" + sfx, "mhalf"], writes=["sc2" + sfx]))
            ops.append(lambda: op("dve", lambda e: e.scalar_tensor_tensor(out=sc2[:, 1:2], in0=mv[:, 0:1], scalar=-1.0, in1=sc2[:, 0:1],
                                                                          op0=ALU.mult, op1=ALU.mult), reads=["mv" + sfx, "sc2" + sfx], writes=["sc2b" + sfx]))
            ops.append(lambda: op("act", lambda e: e.activation(out=R[:], in_=R[:], func=AF.Identity, bias=sc2[:, 1:2], scale=sc2[:, 0:1]),
                                  reads=[rk, "sc2" + sfx, "sc2b" + sfx], writes=[rk]))
            for q in range(4):
                qs = slice(q * 256, (q + 1) * 256)
                ops.append(lambda qs=qs: op(g_eng, lambda e: e.tensor_tensor(out=R[:, qs], in0=R[:, qs], in1=g_t[:, qs], op=ALU.mult),
                                            reads=[rk, "lnt"], writes=[rk]))
            for q in range(4):
                qs = slice(q * 256, (q + 1) * 256)
                ops.append(lambda qs=qs: op(b_eng, lambda e: e.tensor_tensor(out=R[:, qs], in0=R[:, qs], in1=b_t[:, qs], op=ALU.add),
                                            reads=[rk, "lnt"], writes=[rk]))
            return ops

        deferred = []

        def pop_deferred(n):
            for _ in range(n):
                if deferred:
                    deferred.pop(0)()

        with ExitStack() as p3:
            P = p3.enter_context
            lnt = P(nc.sbuf_tensor("lnt_sb", [128, 2, D], F32))
            gt1 = P(nc.sbuf_tensor("gtab1", [128, D], F32))
            yf = [P(nc.sbuf_tensor(f"yf{i}", [128, 4, 512], BF16)) for i in range(2)]
            ysl = [P(nc.sbuf_tensor(f"ysl{i}", [128, 4, 512], BF16)) for i in range(2)]
            gt = [P(nc.sbuf_tensor(f"gt{i}", [128, 16, 512], BF16)) for i in range(2)]
            xt = [P(nc.sbuf_tensor(f"xt3_{i}", [128, 4, D], F32)) for i in range(2)]
            m1 = [P(nc.sbuf_tensor(f"m1_{i}", [128, 512], F32)) for i in range(2)]
            m2 = [P(nc.sbuf_tensor(f"m2_{i}", [128, 512], F32)) for i in range(2)]
            mg = [P(nc.sbuf_tensor(f"mg{i}", [128, 8, 512], BF16)) for i in range(2)]
            Rb = [P(nc.sbuf_tensor(f"Rb{i}", [128, D], F32)) for i in range(4)]
            st6 = [P(nc.sbuf_tensor(f"st6_{i}", [128, 12], F32)) for i in range(4)]
            mv = [P(nc.sbuf_tensor(f"mv_{i}", [128, 2], F32)) for i in range(4)]
            sc2 = [P(nc.sbuf_tensor(f"sc2_{i}", [128, 2], F32)) for i in range(4)]
            op("sp", lambda e: e.dma_start(out=lnt[:], in_=lnt_d[0:2].rearrange("c p n -> p c n")), writes=["lnt"], dma=True)
            op("sp", lambda e: e.dma_start(out=gt1[:], in_=gt_d[0]), writes=["gt1"], dma=True)

            def loadA(tt):
                sl = tt % 2
                tsl = slice(tt * 512, (tt + 1) * 512)
                op("sp", lambda e: e.dma_start(out=yf[sl][:], in_=yT_d[0, :, tsl].rearrange("(c p) t -> p c t", p=128)), writes=[f"yf{sl}"], dma=True)
                op("sp", lambda e: e.dma_start(out=ysl[sl][:], in_=yT_d[1, :, tsl].rearrange("(c p) t -> p c t", p=128)), writes=[f"ysl{sl}"], dma=True)
                op("sp", lambda e: e.dma_start(out=gt[sl][:], in_=fm_d[16:32, :, tsl].rearrange("c p t -> p c t")), writes=[f"gt{sl}"], dma=True)

            def loadX(tt):
                sl = tt % 2
                tsl = slice(tt * 512, (tt + 1) * 512)
                op("sp", lambda e: e.dma_start(out=xt[sl][:], in_=x_d[tsl, :].rearrange("(j p) d -> p j d", p=128)), writes=[f"xt3_{sl}"], dma=True)

            c3 = {"cnt": 0}

            def OC(tt):
                sl = tt % 2
                MG = mg[sl]
                for oc in range(8):
                    cnt = c3["cnt"]
                    ba, bb = (2 * cnt) % 4, (2 * cnt + 1) % 4
                    mi = cnt % 2
                    c3["cnt"] += 1

                    def mmab(e, oc=oc, ba=ba, bb=bb, sl=sl):
                        for c in range(4):
                            e.matmul(ps[ba][:], lhsT=wfb[:, c, oc * 128:(oc + 1) * 128], rhs=yf[sl][:, c, :], start=(c == 0), stop=(c == 3))
                        for c in range(4):
                            ins = e.matmul(ps[bb][:], lhsT=wsbb[:, c, oc * 128:(oc + 1) * 128], rhs=ysl[sl][:, c, :], start=(c == 0), stop=(c == 3))
                        return ins
                    op("pe", mmab, reads=["wfb", "wsbb", f"yf{sl}", f"ysl{sl}"], writes=[f"ps{ba}", f"ps{bb}"])
                    op("dve", lambda e, oc=oc, ba=ba, mi=mi, sl=sl: e.tensor_tensor(out=m1[mi][:], in0=ps[ba][:], in1=gt[sl][:, oc, :], op=ALU.mult),
                       reads=[f"ps{ba}", f"gt{sl}"], writes=[f"m1_{mi}"])
                    op("dve", lambda e, oc=oc, bb=bb, mi=mi, sl=sl: e.tensor_tensor(out=m2[mi][:], in0=ps[bb][:], in1=gt[sl][:, 8 + oc, :], op=ALU.mult),
                       reads=[f"ps{bb}", f"gt{sl}"], writes=[f"m2_{mi}"])
                    op("pool", lambda e, oc=oc, mi=mi, MG=MG: e.tensor_tensor(out=MG[:, oc, :], in0=m1[mi][:], in1=m2[mi][:], op=ALU.add),
                       reads=[f"m1_{mi}", f"m2_{mi}"], writes=[f"mg{sl}"])
                    pop_deferred(9)

            def MO(tt):
                sl = tt % 2
                MG = mg[sl]
                chains = []
                for j in range(4):
                    blk = tt * 4 + j
                    R = Rb[j]
                    rk = f"Rb{j}"
                    for hh in range(2):
                        b = 4 + (2 * blk + hh) % 4

                        def mmo(e, j=j, hh=hh, b=b, MG=MG):
                            for oc in range(8):
                                ins = e.matmul(ps[b][:], lhsT=MG[:, oc, j * 128:(j + 1) * 128], rhs=wob[:, oc, hh * 512:(hh + 1) * 512],
                                               start=(oc == 0), stop=(oc == 7))
                            return ins
                        op("pe", mmo, reads=[f"mg{sl}", "wob"], writes=[f"ps{b}"])
                        op("dve", lambda e, hh=hh, b=b, R=R: e.tensor_tensor(out=R[:, hh * 512:(hh + 1) * 512], in0=ps[b][:],
                                                                              in1=gt1[:, hh * 512:(hh + 1) * 512], op=ALU.mult),
                           reads=[f"ps{b}", "gt1"], writes=[rk])
                    op("pool", lambda e, j=j, R=R, sl=sl: e.tensor_tensor(out=R[:], in0=R[:], in1=xt[sl][:, j, :], op=ALU.add),
                       reads=[f"xt3_{sl}", rk], writes=[rk])
                    ch = ln_tail(R, rk, lnt[:, 0, :], lnt[:, 1, :], st6[j], mv[j], sc2[j], g_eng="dve", sfx=f"_{j}")
                    ch.append(lambda blk=blk, R=R, rk=rk: op("sp", lambda e: e.dma_start(out=x1_d[blk * 128:(blk + 1) * 128, :], in_=R[:]),
                                                             reads=[rk], writes=[f"x1_d{blk}"], dma=True))
                    chains.append(ch)
                for k in range(max(len(c) for c in chains)):
                    for c in chains:
                        if k < len(c):
                            deferred.append(c[k])

            loadA(0)
            loadX(0)
            if NT > 1:
                loadA(1)
            OC(0)
            for tt in range(NT):
                if tt + 2 < NT:
                    loadA(tt + 2)
                if tt + 1 < NT:
                    loadX(tt + 1)
                    OC(tt + 1)
                else:
                    pop_deferred(len(deferred))
                MO(tt)
            pop_deferred(len(deferred))
            sy.barrier()

        p3w.close()
        p012.close()

        TW = 256
        NT2 = S // TW
        JB = TW // 128
        with ExitStack() as p4:
            P = p4.enter_context
            wupb = P(nc.sbuf_tensor("wupb", [128, 8, 2 * DFF], BF16))
            wdb = P(nc.sbuf_tensor("wdb", [128, NAC, D], BF16))
            lnt = P(nc.sbuf_tensor("lnt2", [128, 2, D], F32))
            gt2 = P(nc.sbuf_tensor("gtab2", [128, D], F32))
            convp = P(nc.sbuf_tensor("convp_sb", [128, NFC, 4], F32))
            halo = [P(nc.sbuf_tensor(f"halo{i}", [128, NFC, 2], F32)) for i in range(2)]
            xt = [P(nc.sbuf_tensor(f"xt4_{i}", [128, JB, D], F32)) for i in range(2)]
            uT = [P(nc.sbuf_tensor(f"u2T{i}", [128, 8, TW], BF16)) for i in range(2)]
            hb = [P(nc.sbuf_tensor(f"hb{i}", [128, TW + 2], F32)) for i in range(2)]
            cv = [P(nc.sbuf_tensor(f"cv{i}", [128, TW], F32)) for i in range(3)]
            sg = [P(nc.sbuf_tensor(f"sg{i}", [128, TW], F32)) for i in range(2)]
            act = [P(nc.sbuf_tensor(f"act{i}", [128, NAC, TW], BF16)) for i in range(2)]
            Rb = [P(nc.sbuf_tensor(f"Rb4_{i}", [128, D], F32)) for i in range(2)]
            st6 = P(nc.sbuf_tensor("st6b", [128, 12], F32))
            mv = P(nc.sbuf_tensor("mvb", [128, 2], F32))
            sc2 = P(nc.sbuf_tensor("sc2b_", [128, 2], F32))
            for c in range(8):
                op("pool", lambda e, c=c: e.dma_start(out=wupb[:, c, :], in_=wup_d[c * 128:(c + 1) * 128, :]), writes=["wupb"], dma=True)
            for c in range(NAC):
                op("pool", lambda e, c=c: e.dma_start(out=wdb[:, c, :], in_=wdown_d[c * 128:(c + 1) * 128, :]), writes=["wdb"], dma=True)
            op("sp", lambda e: e.dma_start(out=lnt[:], in_=lnt_d[2:4].rearrange("c p n -> p c n")), writes=["lnt"], dma=True)
            op("sp", lambda e: e.dma_start(out=gt2[:], in_=gt_d[1]), writes=["gt2"], dma=True)
            op("sp", lambda e: e.dma_start(out=convp[:], in_=convp_d), writes=["convp"], dma=True)
            op("dve", lambda e: e.memset(halo[1][:], 0.0), writes=["halo1"])

            def loads4(tt):
                sl = tt % 2
                op("sp", lambda e: e.dma_start(out=xt[sl][:], in_=x1_d[tt * TW:(tt + 1) * TW, :].rearrange("(j p) d -> p j d", p=128)),
                   writes=[f"xt4_{sl}"], dma=True)
            cnts = {"pc": 0, "hc": 0}

            def T4(tt):
                sl = tt % 2
                X, U = xt[sl], uT[sl]
                for kc in range(8):
                    b = cnts["pc"] % 4
                    cnts["pc"] += 1

                    def tr(e, kc=kc, b=b, X=X):
                        for j in range(JB):
                            ins = e.transpose(out=ps[b][:, j * 128:(j + 1) * 128], in_=X[:, j, kc * 128:(kc + 1) * 128], identity=identf[:])
                        return ins
                    op("pe", tr, reads=[f"xt4_{sl}", "identf"], writes=[f"ps{b}"])
                    op("act", lambda e, kc=kc, b=b, U=U: e.activation(out=U[:, kc, :], in_=ps[b][:, 0:TW], func=AF.Identity,
                                                                      bias=modp[:, 2, kc:kc + 1], scale=modp[:, 3, kc:kc + 1]),
                       reads=[f"ps{b}", "modp"], writes=[f"u2T{sl}"])

            def U4(tt):
                sl = tt % 2
                U, A = uT[sl], act[sl]
                uk = f"u2T{sl}"
                base = cnts["hc"]
                cnts["hc"] += 2 * NAC

                def stA(c):
                    i, half = c // 2, c % 2
                    fc = i + half * NAC
                    b = cnts["pc"] % 4
                    cnts["pc"] += 1
                    hi = (base + c) % 2
                    ci = (base + c) % 3
                    H, CV = hb[hi], cv[ci]

                    def mmu(e):
                        for kc in range(8):
                            ins = e.matmul(ps[b][:, 0:TW], lhsT=wupb[:, kc, fc * 128:(fc + 1) * 128], rhs=U[:, kc, :], start=(kc == 0), stop=(kc == 7))
                        return ins
                    op("pe", mmu, reads=["wupb", uk], writes=[f"ps{b}"])
                    op("act", lambda e: e.activation(out=H[:, 2:TW + 2], in_=ps[b][:, 0:TW], func=AF.Identity), reads=[f"ps{b}"], writes=[f"hb{hi}"])
                    op("act", lambda e: e.activation(out=CV[:], in_=ps[b][:, 0:TW], func=AF.Identity, bias=convp[:, fc, 3:4], scale=convp[:, fc, 2:3]),
                       reads=[f"ps{b}", "convp"], writes=[f"cv{ci}"])
                    op("act", lambda e: e.activation(out=halo[sl][:, fc, :], in_=ps[b][:, TW - 2:TW], func=AF.Identity),
                       reads=[f"ps{b}"], writes=[f"halo{sl}"])
                    op("pool", lambda e: e.tensor_copy(out=H[:, 0:2], in_=halo[1 - sl][:, fc, :]), reads=[f"halo{1 - sl}"], writes=[f"hbh{hi}"])

                def stB(c):
                    i, half = c // 2, c % 2
                    fc = i + half * NAC
                    hi = (base + c) % 2
                    ci = (base + c) % 3
                    H, CV = hb[hi], cv[ci]
                    op("dve", lambda e: e.scalar_tensor_tensor(out=CV[:], in0=H[:, 1:TW + 1], scalar=convp[:, fc, 1:2], in1=CV[:],
                                                               op0=ALU.mult, op1=ALU.add),
                       reads=[f"hb{hi}", f"hbh{hi}", "convp", f"cv{ci}"], writes=[f"cv{ci}"])
                    op("dve", lambda e: e.scalar_tensor_tensor(out=CV[:], in0=H[:, 0:TW], scalar=convp[:, fc, 0:1], in1=CV[:],
                                                               op0=ALU.mult, op1=ALU.add),
                       reads=[f"hb{hi}", f"hbh{hi}", "convp", f"cv{ci}"], writes=[f"cv{ci}"])

                def stC(c):
                    i, half = c // 2, c % 2
                    ci = (base + c) % 3
                    CV = cv[ci]
                    SG = sg[i % 2]
                    if half == 0:
                        op("act", lambda e: e.activation(out=SG[:], in_=CV[:], func=AF.Silu), reads=[f"cv{ci}"], writes=[f"sg{i % 2}"])
                    else:
                        op("pool", lambda e: e.tensor_tensor(out=A[:, i, :], in0=SG[:], in1=CV[:], op=ALU.mult),
                           reads=[f"cv{ci}", f"sg{i % 2}"], writes=[f"act{sl}"])

                nchunk = 2 * NAC
                for st_ in range(nchunk + 2):
                    if st_ < nchunk:
                        stA(st_)
                    if 0 <= st_ - 1 < nchunk:
                        stB(st_ - 1)
                    if 0 <= st_ - 2 < nchunk:
                        stC(st_ - 2)
                    pop_deferred(1)

            def D4(tt):
                sl = tt % 2
                X, A = xt[sl], act[sl]
                for j in range(JB):
                    blk = tt * JB + j
                    R = Rb[blk % 2]
                    rk = f"Rb{blk % 2}"
                    for hh in range(2):
                        b = 4 + (2 * blk + hh) % 4

                        def mmd2(e, j=j, hh=hh, b=b, A=A):
                            for i in range(NAC):
                                ins = e.matmul(ps[b][:], lhsT=A[:, i, j * 128:(j + 1) * 128], rhs=wdb[:, i, hh * 512:(hh + 1) * 512],
                                               start=(i == 0), stop=(i == NAC - 1))
                            return ins
                        op("pe", mmd2, reads=[f"act{sl}", "wdb"], writes=[f"ps{b}"])
                        op("dve", lambda e, hh=hh, b=b, R=R: e.tensor_tensor(out=R[:, hh * 512:(hh + 1) * 512], in0=ps[b][:],
                                                                              in1=gt2[:, hh * 512:(hh + 1) * 512], op=ALU.mult),
                           reads=[f"ps{b}", "gt2"], writes=[rk])
                    op("pool", lambda e, j=j, R=R, X=X: e.tensor_tensor(out=R[:], in0=R[:], in1=X[:, j, :], op=ALU.add),
                       reads=[f"xt4_{sl}", rk], writes=[rk])
                    deferred.extend(ln_tail(R, rk, lnt[:, 0, :], lnt[:, 1, :], st6, mv, sc2, g_eng="dve", b_eng="dve"))
                    deferred.append(lambda blk=blk, R=R, rk=rk: op("sp", lambda e: e.dma_start(out=out_d[blk * 128:(blk + 1) * 128, :], in_=R[:]),
                                                                   reads=[rk], writes=[f"out_d{blk}"], dma=True))

            loads4(0)
            T4(0)
            for tt in range(NT2):
                if tt + 1 < NT2:
                    loads4(tt + 1)
                U4(tt)
                if tt + 1 < NT2:
                    T4(tt + 1)
                D4(tt)
            pop_deferred(len(deferred))
            sy.barrier()
    return nc


def _consts():
    r = np.arange(128)[:, None]
    c = np.arange(128)[None, :]
    dm = (r == c - 1).astype(np.float32) - (r == c).astype(np.float32)
    eb = ((r == 127) & (c == 0)).astype(np.float32)
    six = np.stack([(r == c), (r <= c), (r >= c), (r > c), (r < c), np.ones((128, 128), bool)]).astype(np.float32)
    return np.concatenate([six, dm[None], eb[None]], axis=0)


def prep_shared(w_ada, b_ada, w_in, b_forget, w_fox_proj, w_sb_proj, w_o, ln1_g, ln1_b, w_up, conv_w, conv_b, w_down, ln2_g, ln2_b):
    f = lambda a: np.ascontiguousarray(np.asarray(a, dtype=np.float32))
    w_in = f(w_in)
    w_in_r = np.concatenate([w_in[:, 0:512], w_in[:, 512:1024], w_in[:, 1544:2056], w_in[:, 2056:2568], w_in[:, 3080:4104],
                             w_in[:, 4104:5128], w_in[:, 1024:1536], w_in[:, 2568:3080], w_in[:, 1536:1544]], axis=1)
    convp = np.stack([f(conv_w)[0], f(conv_w)[1], f(conv_w)[2], f(conv_b)], axis=1).reshape(NFC, 128, 4).transpose(1, 0, 2)
    lnt = np.stack([np.broadcast_to(f(v)[None, :], (128, D)) for v in (ln1_g, ln1_b, ln2_g, ln2_b)])
    return {
        "w_ada": f(w_ada), "b_ada": f(b_ada).reshape(12, 1, 512), "w_in": f(w_in_r),
        "bft": f(np.broadcast_to(np.tile(f(b_forget), 4)[None, :], (128, 32))),
        "w_fox": f(w_fox_proj), "w_sb": f(w_sb_proj), "w_o": f(w_o), "lnt": f(lnt), "w_up": f(w_up),
        "convp": f(convp), "w_down": f(w_down), "consts": _consts(),
    }


def kernel(x, c, w_ada, b_ada, w_in, b_forget, w_fox_proj, w_sb_proj, w_o, ln1_g, ln1_b, w_up, conv_w, conv_b, w_down, ln2_g, ln2_b):
    x = np.asarray(x, dtype=np.float32)
    c = np.asarray(c, dtype=np.float32)
    B, S, _ = x.shape
    shared = prep_shared(w_ada, b_ada, w_in, b_forget, w_fox_proj, w_sb_proj, w_o, ln1_g, ln1_b, w_up, conv_w, conv_b, w_down, ln2_g, ln2_b)
    nc = build_nc(S)
    in_maps = []
    for b in range(B):
        m = dict(shared)
        m["x"] = np.ascontiguousarray(x[b])
        m["cT"] = np.ascontiguousarray(c[b].reshape(8, 128).T)
        in_maps.append(m)
    res = run_bass_kernel_spmd(nc, in_maps, core_ids=list(range(B)))
    return np.stack([r["out"] for r in res.results], axis=0).astype(np.float32)
```

```python
import numpy as np
from contextlib import ExitStack
import concourse.bass as bass
import concourse.mybir as mybir
from concourse.bass_utils import run_bass_kernel_spmd

F32 = mybir.dt.float32
BF16 = mybir.dt.bfloat16
AF = mybir.ActivationFunctionType
ALU = mybir.AluOpType
AX = mybir.AxisListType

D = 1024
DFF = 2816
NH = 8
DH = 64
NFC = 2 * DFF // 128
NAC = DFF // 128
ALPHA = 2.0 ** 0.25
EPS = 1e-5


class Sync:
    def __init__(self, nc, st):
        self.nc = nc
        self.st = st
        self.nsem = 0
        self.engs = {"sp": nc.sync, "act": nc.scalar, "dve": nc.vector, "pool": nc.gpsimd, "pe": nc.tensor}
        self.cs = {e: [self._newsem(), 0] for e in self.engs}
        self.waited = {e: {} for e in self.engs}
        self.lastw = {}
        self.readers = {}
        self.NDS = 8
        self.dsem = {e: [self._newsem() for _ in range(self.NDS)] for e in ("sp", "pool")}
        self.dcnt = {e: [0] * self.NDS for e in ("sp", "pool")}
        self.dn = {e: 0 for e in ("sp", "pool")}

    def _newsem(self):
        self.nsem += 1
        return self.st.enter_context(self.nc.semaphore(f"sy{self.nsem}"))

    def _wait(self, eng, sem, val):
        w = self.waited[eng]
        if w.get(sem, 0) >= val:
            return
        self.engs[eng].wait_ge(sem, val)
        w[sem] = val

    def op(self, eng, fn, reads=(), writes=(), dma=False):
        deps = []
        for k in reads:
            lw = self.lastw.get(k)
            if lw is not None:
                if lw[2] != eng or lw[3] or eng in ("act", "dve", "pool"):
                    deps.append(lw)
        for k in writes:
            lw = self.lastw.get(k)
            if lw is not None and (lw[2] != eng or lw[3] or eng in ("act", "dve", "pool")):
                deps.append(lw)
            for r in self.readers.get(k, {}).values():
                if r[2] != eng or r[3] or eng in ("act", "dve", "pool"):
                    deps.append(r)
        for d in deps:
            self._wait(eng, d[0], d[1])
        e = self.engs[eng]
        if dma:
            i = self.dn[eng] % self.NDS
            self.dn[eng] += 1
            sem = self.dsem[eng][i]
            if self.dcnt[eng][i] > 0:
                self._wait(eng, sem, self.dcnt[eng][i])
            ins = fn(e)
            self.dcnt[eng][i] += 16
            ins.then_inc(sem, 16)
            ref = (sem, self.dcnt[eng][i], eng, True)
            rkey = ("dma", eng, i)
        else:
            c = self.cs[eng]
            if c[1] >= 30000:
                c[0] = self._newsem()
                c[1] = 0
            ins = fn(e)
            c[1] += 1
            ins.then_inc(c[0], 1)
            ref = (c[0], c[1], eng, False)
            rkey = eng
        for k in reads:
            self.readers.setdefault(k, {})[rkey] = ref
        for k in writes:
            self.lastw[k] = ref
            self.readers[k] = {}
        return ref

    def barrier(self):
        refs = [(c[0], c[1]) for c in self.cs.values() if c[1] > 0]
        for e in self.dsem:
            for i in range(self.NDS):
                if self.dcnt[e][i] > 0:
                    refs.append((self.dsem[e][i], self.dcnt[e][i]))
        for eng in self.engs:
            for (s, v) in refs:
                if s is self.cs[eng][0]:
                    continue
                self._wait(eng, s, v)
        self.lastw.clear()
        self.readers.clear()


def build_nc(S, dbg=False):
    assert S % 512 == 0
    NT = S // 512
    NB = S // 128
    nc = bass.Bass("TRN2", target_bir_lowering=False)
    okind = "ExternalOutput" if dbg else "Internal"

    def din(name, shape, dt=F32):
        return nc.dram_tensor(name, list(shape), dt, kind="ExternalInput").ap()

    x_d = din("x", [S, D])
    cT_d = din("cT", [128, 8])
    wada_d = din("w_ada", [D, 6 * D])
    bada_d = din("b_ada", [12, 1, 512])
    win_d = din("w_in", [D, 5128])
    bft_d = din("bft", [128, 32])
    wfox_d = din("w_fox", [512, D])
    wsb_d = din("w_sb", [512, D])
    wo_d = din("w_o", [D, D])
    lnt_d = din("lnt", [4, 128, D])
    wup_d = din("w_up", [D, 2 * DFF])
    convp_d = din("convp", [128, NFC, 4])
    wdown_d = din("w_down", [DFF, D])
    consts_d = din("consts", [8, 128, 128])
    out_d = nc.dram_tensor("out", [S, D], F32, kind="ExternalOutput").ap()

    fm_d = nc.dram_tensor("fm_s", [32, 128, S], BF16, kind=okind).ap()
    v_d = nc.dram_tensor("v_s", [2, S, 512], BF16, kind=okind).ap()
    caug_d = nc.dram_tensor("caug_s", [8, 3, S], BF16, kind=okind).ap()
    yT_d = nc.dram_tensor("yT_s", [2, 512, S], BF16, kind=okind).ap()
    x1_d = nc.dram_tensor("x1_s", [S, D], F32, kind=okind).ap()
    dv_d = nc.dram_tensor("dv_s", [S, 512], BF16, kind=okind).ap()
    gt_d = nc.dram_tensor("gt_s", [2, 128, D], F32).ap()

    with ExitStack() as st:
        E = st.enter_context
        sy = Sync(nc, st)
        op = sy.op
        ps = [E(nc.psum_tensor(f"psb{i}", [128, 512], F32)) for i in range(8)]

        identf = E(nc.sbuf_tensor("identf", [128, 128], F32))
        mhalf = E(nc.sbuf_tensor("mhalf", [128, 1], F32))
        modp = E(nc.sbuf_tensor("modp", [128, 4, 8], F32))
        p012 = ExitStack()
        E2 = p012.enter_context
        cf = E2(nc.sbuf_tensor("cf", [128, 8, 128], F32))
        cb = E2(nc.sbuf_tensor("cb", [128, 8, 128], BF16))
        ctok = E2(nc.sbuf_tensor("ctok", [128, NB, 8], F32))
        cql = E2(nc.sbuf_tensor("cql", [128, NT + 1, 8], F32))
        IDENT, UINC, LINC, LSTR, USTR, ONES, DM, EB = range(8)

        op("sp", lambda e: e.dma_start(out=identf[:], in_=consts_d[0]), writes=["identf"], dma=True)
        op("dve", lambda e: e.memset(mhalf[:], -0.5), writes=["mhalf"])
        op("sp", lambda e: e.dma_start(out=cf[:], in_=consts_d.rearrange("c p n -> p c n")), writes=["cf"], dma=True)
        op("pool", lambda e: e.dma_start(out=cb[:], in_=consts_d.rearrange("c p n -> p c n")), writes=["cb"], dma=True)

        with ExitStack() as p0:
            P = p0.enter_context
            cts = P(nc.sbuf_tensor("cts", [128, 8], F32))
            modtab = P(nc.sbuf_tensor("modtab", [128, 6 * D], F32))
            crep = P(nc.sbuf_tensor("crep", [128, 8, 128], F32))
            wa = [P(nc.sbuf_tensor(f"wa{i}", [128, 8, 512], F32)) for i in range(2)]
            bad = [P(nc.sbuf_tensor(f"bad{i}", [1, 512], F32)) for i in range(2)]
            dtmp = P(nc.sbuf_tensor("dtmp", [128, 8, 128], F32))
            op("sp", lambda e: e.dma_start(out=cts[:], in_=cT_d), writes=["cts"], dma=True)
            for kc in range(8):
                op("dve", lambda e, kc=kc: e.tensor_copy(out=crep[:, kc, :], in_=cts[:, kc:kc + 1].to_broadcast([128, 128])),
                   reads=["cts"], writes=[f"crep{kc}"])
            for g in range(12):
                sl = g % 2
                op("sp", lambda e, g=g, sl=sl: e.dma_start(
                    out=wa[sl][:], in_=wada_d[:, g * 512:(g + 1) * 512].rearrange("(kc p) n -> p kc n", p=128)),
                   writes=[f"wa{sl}"], dma=True)
                op("sp", lambda e, g=g, sl=sl: e.dma_start(out=bad[sl][:], in_=bada_d[g]), writes=[f"bad{sl}"], dma=True)
                bank = ps[g % 2]

                def mm(e, g=g, sl=sl, bank=bank):
                    for kc in range(8):
                        e.matmul(bank[:], lhsT=crep[:, kc, :], rhs=wa[sl][:, kc, :], start=(kc == 0), stop=False)
                    return e.matmul(bank[:], lhsT=cf[0:1, ONES, :], rhs=bad[sl][0:1, :], start=False, stop=True)
                op("pe", mm, reads=[f"wa{sl}", f"bad{sl}", "cf"] + [f"crep{k}" for k in range(8)], writes=[f"ps{g % 2}"])
                addone = g in (2, 3, 8, 9)
                if g in (4, 5, 10, 11):
                    op("dve", lambda e, g=g, bank=bank: e.tensor_scalar(out=modtab[:, g * 512:(g + 1) * 512], in0=bank[:], scalar1=1.0,
                                                                        scalar2=1.0 / ALPHA, op0=ALU.add, op1=ALU.mult),
                       reads=[f"ps{g % 2}"], writes=[f"modtab{g}"])
                elif addone:
                    op("dve", lambda e, g=g, bank=bank: e.tensor_scalar_add(out=modtab[:, g * 512:(g + 1) * 512], in0=bank[:], scalar1=1.0),
                       reads=[f"ps{g % 2}"], writes=[f"modtab{g}"])
                else:
                    op("act", lambda e, g=g, bank=bank: e.activation(out=modtab[:, g * 512:(g + 1) * 512], in_=bank[:], func=AF.Identity),
                       reads=[f"ps{g % 2}"], writes=[f"modtab{g}"])
            for vi, base in enumerate((0, 1024, 3072, 4096)):
                for k in range(8):
                    op("dve", lambda e, base=base, k=k: e.tensor_tensor(
                        out=dtmp[:, k, :], in0=modtab[:, base + k * 128: base + (k + 1) * 128], in1=cf[:, IDENT, :], op=ALU.mult),
                       reads=[f"modtab{(base + k * 128) // 512}", "cf"], writes=["dtmp"])
                op("dve", lambda e, vi=vi: e.tensor_reduce(out=modp[:, vi, :], in_=dtmp[:], axis=AX.X, op=ALU.add),
                   reads=["dtmp"], writes=["modp"])
            op("sp", lambda e: e.dma_start(out=gt_d[0], in_=modtab[:, 2048:3072]), reads=["modtab4", "modtab5"], writes=["gt_d0"], dma=True)
            op("sp", lambda e: e.dma_start(out=gt_d[1], in_=modtab[:, 5120:6144]), reads=["modtab10", "modtab11"], writes=["gt_d1"], dma=True)
            sy.barrier()

        with ExitStack() as p1:
            P = p1.enter_context
            winb = P(nc.sbuf_tensor("winb", [128, 8, 5128], BF16))
            xt = [P(nc.sbuf_tensor(f"xt{i}", [128, 4, D], F32)) for i in range(2)]
            uT = [P(nc.sbuf_tensor(f"uT{i}", [128, 8, 512], BF16)) for i in range(2)]
            fst = [P(nc.sbuf_tensor(f"fst{i}", [128, 4, 512], BF16)) for i in range(3)]
            vst = [P(nc.sbuf_tensor(f"vst{i}", [128, 4, 1024], BF16)) for i in range(2)]
            dvst = [P(nc.sbuf_tensor(f"dvst{i}", [128, 4, 512], BF16)) for i in range(2)]
            bft = P(nc.sbuf_tensor("bft_sb", [128, 32], F32))
            fb = P(nc.sbuf_tensor("fb", [128, 32], F32))
            ef = P(nc.sbuf_tensor("ef", [128, 32], F32))
            lf = [P(nc.sbuf_tensor(f"lf{i}", [128, 32], F32)) for i in range(2)]
            d32 = P(nc.sbuf_tensor("d32", [8, 512], F32))
            r1 = P(nc.sbuf_tensor("r1", [8, 512], F32))
            r2 = P(nc.sbuf_tensor("r2", [8, 512], F32))
            cst = [P(nc.sbuf_tensor(f"cst{i}", [8, 3, 512], BF16)) for i in range(2)]

            for kc in range(8):
                op("pool", lambda e, kc=kc: e.dma_start(out=winb[:, kc, :], in_=win_d[kc * 128:(kc + 1) * 128, :]),
                   writes=[f"winb{kc}"], dma=True)
            WIN = [f"winb{k}" for k in range(8)]
            op("sp", lambda e: e.dma_start(out=bft[:], in_=bft_d), writes=["bft"], dma=True)
            op("dve", lambda e: e.memset(cql[:, 0, :], 0.0), writes=["cql"])

            def load_x(tt):
                op("sp", lambda e: e.dma_start(out=xt[tt % 2][:], in_=x_d[tt * 512:(tt + 1) * 512, :].rearrange("(j p) d -> p j d", p=128)),
                   writes=[f"xt{tt % 2}"], dma=True)
            load_x(0)
            pcnt = [0]

            def nbank():
                pcnt[0] += 1
                return pcnt[0] % 6
            fcnt = [0]
            for tt in range(NT):
                if tt + 1 < NT:
                    load_x(tt + 1)
                X = xt[tt % 2]
                U = uT[tt % 2]
                ukeys = [f"uT{tt % 2}_{k}" for k in range(8)]
                for kc in range(8):
                    b = nbank()

                    def tr(e, kc=kc, b=b, X=X):
                        for j in range(4):
                            ins = e.transpose(out=ps[b][:, j * 128:(j + 1) * 128], in_=X[:, j, kc * 128:(kc + 1) * 128], identity=cf[:, IDENT, :])
                        return ins
                    op("pe", tr, reads=[f"xt{tt % 2}", "cf"], writes=[f"ps{b}"])
                    if kc % 2 == 0:
                        op("act", lambda e, kc=kc, b=b, U=U: e.activation(out=U[:, kc, :], in_=ps[b][:], func=AF.Identity,
                                                                          bias=modp[:, 0, kc:kc + 1], scale=modp[:, 1, kc:kc + 1]),
                           reads=[f"ps{b}", "modp"], writes=[ukeys[kc]])
                    else:
                        op("dve", lambda e, kc=kc, b=b, U=U: e.tensor_scalar(out=U[:, kc, :], in0=ps[b][:], scalar1=modp[:, 1, kc:kc + 1],
                                                                             scalar2=modp[:, 0, kc:kc + 1], op0=ALU.mult, op1=ALU.add),
                           reads=[f"ps{b}", "modp"], writes=[ukeys[kc]])
                for oc in range(32):
                    b = nbank()

                    def mm(e, oc=oc, b=b, U=U):
                        for kc in range(8):
                            ins = e.matmul(ps[b][:], lhsT=winb[:, kc, oc * 128:(oc + 1) * 128], rhs=U[:, kc, :], start=(kc == 0), stop=(kc == 7))
                        return ins
                    op("pe", mm, reads=WIN + ukeys, writes=[f"ps{b}"])
                    fs = fcnt[0] % 3
                    dst = fst[fs][:, oc % 4, :]
                    if oc < 16 and (oc // 4) % 2 == 0:
                        op("dve", lambda e, b=b, dst=dst: e.tensor_scalar_mul(out=dst, in0=ps[b][:], scalar1=0.125),
                           reads=[f"ps{b}"], writes=[f"fst{fs}"])
                    elif oc < 16:
                        op("dve", lambda e, b=b, dst=dst: e.tensor_copy(out=dst, in_=ps[b][:]), reads=[f"ps{b}"], writes=[f"fst{fs}"])
                    else:
                        op("act", lambda e, b=b, dst=dst: e.activation(out=dst, in_=ps[b][:], func=AF.Sigmoid),
                           reads=[f"ps{b}"], writes=[f"fst{fs}"])
                    if oc % 4 == 3:
                        oc0 = oc - 3
                        op("sp", lambda e, oc0=oc0, fs=fs, tt=tt: e.dma_start(
                            out=fm_d[oc0:oc0 + 4, :, tt * 512:(tt + 1) * 512].rearrange("c p t -> p c t"), in_=fst[fs][:]),
                           reads=[f"fst{fs}"], writes=[f"fm_d{oc0}_{tt}"], dma=True)
                        fcnt[0] += 1
                VS = vst[tt % 2]
                for j in range(4):
                    for ab in range(2):
                        b = nbank()

                        def mmv(e, j=j, ab=ab, b=b, U=U):
                            for kc in range(8):
                                ins = e.matmul(ps[b][:], lhsT=U[:, kc, j * 128:(j + 1) * 128],
                                               rhs=winb[:, kc, 4096 + ab * 512: 4096 + (ab + 1) * 512], start=(kc == 0), stop=(kc == 7))
                            return ins
                        op("pe", mmv, reads=WIN + ukeys, writes=[f"ps{b}"])
                        dst = VS[:, j, ab * 512:(ab + 1) * 512]
                        if ab == 0:
                            op("act", lambda e, b=b, dst=dst: e.activation(out=dst, in_=ps[b][:], func=AF.Identity),
                               reads=[f"ps{b}"], writes=[f"vst{tt % 2}"])
                        else:
                            op("dve", lambda e, b=b, dst=dst: e.tensor_copy(out=dst, in_=ps[b][:]), reads=[f"ps{b}"], writes=[f"vst{tt % 2}"])
                for ab in range(2):
                    op("sp", lambda e, ab=ab, tt=tt, VS=VS: e.dma_start(
                        out=v_d[ab, tt * 512:(tt + 1) * 512, :].rearrange("(j p) d -> p j d", p=128), in_=VS[:, :, ab * 512:(ab + 1) * 512]),
                       reads=[f"vst{tt % 2}"], writes=[f"v_d{ab}_{tt}"], dma=True)
                DVS = dvst[tt % 2]
                for j in range(4):
                    b = nbank()
                    blk = tt * 4 + j
                    prev = VS[:, j - 1, 512:1024] if j > 0 else vst[(tt - 1) % 2][:, 3, 512:1024]

                    def mmdv(e, j=j, b=b, blk=blk, prev=prev, VS=VS):
                        ins = e.matmul(ps[b][:], lhsT=cb[:, DM, :], rhs=VS[:, j, 512:1024], start=True, stop=(blk == 0))
                        if blk > 0:
                            ins = e.matmul(ps[b][:], lhsT=cb[:, EB, :], rhs=prev, start=False, stop=True)
                        return ins
                    op("pe", mmdv, reads=[f"vst{tt % 2}", f"vst{(tt - 1) % 2}", "cb"], writes=[f"ps{b}"])
                    op("act", lambda e, j=j, b=b, DVS=DVS: e.activation(out=DVS[:, j, :], in_=ps[b][:], func=AF.Identity),
                       reads=[f"ps{b}"], writes=[f"dvst{tt % 2}"])
                op("sp", lambda e, tt=tt, DVS=DVS: e.dma_start(out=dv_d[tt * 512:(tt + 1) * 512, :].rearrange("(j p) d -> p j d", p=128), in_=DVS[:]),
                   reads=[f"dvst{tt % 2}"], writes=[f"dv_d{tt}"], dma=True)
                LF = lf[tt % 2]

                def mmf(e, U=U):
                    for j in range(4):
                        for kc in range(8):
                            ins = e.matmul(ps[6][:, j * 8:(j + 1) * 8], lhsT=U[:, kc, j * 128:(j + 1) * 128], rhs=winb[:, kc, 5120:5128],
                                           start=(kc == 0), stop=(kc == 7))
                    return ins
                op("pe", mmf, reads=WIN + ukeys, writes=["ps6"])
                op("dve", lambda e: e.tensor_tensor(out=fb[:], in0=ps[6][:, 0:32], in1=bft[:], op=ALU.add), reads=["ps6", "bft"], writes=["fb"])
                op("act", lambda e: e.activation(out=ef[:], in_=fb[:], func=AF.Exp, scale=-1.0), reads=["fb"], writes=["ef"])
                op("act", lambda e, LF=LF: e.activation(out=LF[:], in_=ef[:], func=AF.Ln, bias=1.0, scale=1.0), reads=["ef"], writes=[f"lf{tt % 2}"])

                def mmc(e, LF=LF):
                    for j in range(4):
                        ins = e.matmul(ps[6][:, 64 + j * 8: 64 + (j + 1) * 8], lhsT=cf[:, UINC, :], rhs=LF[:, j * 8:(j + 1) * 8], start=True, stop=(j == 0))
                        for jp in range(j):
                            ins = e.matmul(ps[6][:, 64 + j * 8: 64 + (j + 1) * 8], lhsT=cf[:, ONES, :], rhs=LF[:, jp * 8:(jp + 1) * 8],
                                           start=False, stop=(jp == j - 1))
                    for jp in range(4):
                        ins = e.matmul(ps[6][:, 96:104], lhsT=cf[:, ONES, :], rhs=LF[:, jp * 8:(jp + 1) * 8], start=(jp == 0), stop=(jp == 3))
                    return ins
                op("pe", mmc, reads=[f"lf{tt % 2}", "cf"], writes=["ps6"])
                for j in range(4):
                    op("dve", lambda e, j=j, tt=tt: e.tensor_tensor(out=ctok[:, tt * 4 + j, :], in0=ps[6][:, 64 + j * 8: 64 + (j + 1) * 8],
                                                                    in1=cql[:, tt, :], op=ALU.add),
                       reads=["ps6", "cql"], writes=["ctok"])
                op("dve", lambda e, tt=tt: e.tensor_tensor(out=cql[:, tt + 1, :], in0=ps[6][:, 96:104], in1=cql[:, tt, :], op=ALU.add),
                   reads=["ps6", "cql"], writes=["cql"])

                def mmd(e, LF=LF):
                    for j in range(4):
                        ins = e.matmul(ps[7][0:8, j * 128:(j + 1) * 128], lhsT=LF[:, j * 8:(j + 1) * 8], rhs=cf[:, LSTR, :], start=True, stop=(j == 3))
                        for jp in range(j + 1, 4):
                            ins = e.matmul(ps[7][0:8, j * 128:(j + 1) * 128], lhsT=LF[:, jp * 8:(jp + 1) * 8], rhs=cf[:, ONES, :],
                                           start=False, stop=(jp == 3))
                    return ins
                op("pe", mmd, reads=[f"lf{tt % 2}", "cf"], writes=["ps7"])
                CS = cst[tt % 2]
                ck = f"cst{tt % 2}"
                op("dve", lambda e: e.tensor_copy(out=d32[:], in_=ps[7][0:8, :]), reads=["ps7"], writes=["d32"])
                op("dve", lambda e, CS=CS: e.tensor_copy(out=CS[:, 0, :], in_=d32[:]), reads=["d32"], writes=[ck])
                op("dve", lambda e, CS=CS: e.tensor_tensor(out=r1[:], in0=d32[:], in1=CS[:, 0, :], op=ALU.subtract), reads=["d32", ck], writes=["r1"])
                op("dve", lambda e, CS=CS: e.tensor_copy(out=CS[:, 1, :], in_=r1[:]), reads=["r1"], writes=[ck])
                op("dve", lambda e, CS=CS: e.tensor_tensor(out=r2[:], in0=r1[:], in1=CS[:, 1, :], op=ALU.subtract), reads=["r1", ck], writes=["r2"])
                op("dve", lambda e, CS=CS: e.tensor_copy(out=CS[:, 2, :], in_=r2[:]), reads=["r2"], writes=[ck])
                op("sp", lambda e, CS=CS, tt=tt: e.dma_start(out=caug_d[:, :, tt * 512:(tt + 1) * 512], in_=CS[:]),
                   reads=[ck], writes=[f"caug_d{tt}"], dma=True)
            sy.barrier()

        with ExitStack() as p2:
            P = p2.enter_context
            KT = [P(nc.sbuf_tensor(f"KT{i}", [128, S], BF16)) for i in range(2)]
            QT = [P(nc.sbuf_tensor(f"QT{i}", [128, S], BF16)) for i in range(2)]
            VV = [P(nc.sbuf_tensor(f"VV{i}", [128, NB, 65], BF16)) for i in range(2)]
            bias_t = [P(nc.sbuf_tensor(f"bias{i}", [128, NT, NB], F32)) for i in range(2)]
            zeros = P(nc.sbuf_tensor("zeros", [128, 64], BF16))
            Pb = [P(nc.sbuf_tensor(f"Pb{i}", [128, 512], BF16)) for i in range(4)]
            dVV = [P(nc.sbuf_tensor(f"dVV{i}", [128, NB, 64], BF16)) for i in range(2)]
            wbuf = [P(nc.sbuf_tensor(f"wbuf{i}", [128, 512], F32)) for i in range(6)]
            Gs = [P(nc.sbuf_tensor(f"Gs{i}", [128, 512], BF16)) for i in range(8)]
            GT = [P(nc.sbuf_tensor(f"GT{i}", [128, 512], BF16)) for i in range(6)]
            onesf = P(nc.sbuf_tensor("onesf", [128, 512], F32))
            rs = [P(nc.sbuf_tensor(f"rs{i}", [65, 512], F32)) for i in range(2)]
            ysb = [P(nc.sbuf_tensor(f"ysb{i}", [64, 512], F32)) for i in range(2)]
            yo = [P(nc.sbuf_tensor(f"yo{i}", [64, 512], BF16)) for i in range(2)]
            yo2 = [P(nc.sbuf_tensor(f"yo2_{i}", [128, 512], BF16)) for i in range(2)]

            op("dve", lambda e: e.memset(zeros[:], 0.0), writes=["zeros"])
            for i in range(2):
                op("dve", lambda e, i=i: e.memset(KT[i][64:67, :], 1.0), writes=[f"KTa{i}"])
                op("dve", lambda e, i=i: e.memset(VV[i][:, :, 64:65], 1.0), writes=[f"VVa{i}"])

            def fox_loads(h):
                sl = h % 2
                hp, hf = h // 2, h % 2
                op("sp", lambda e: e.dma_start(out=KT[sl][0:64, :], in_=fm_d[4 + hp, hf * 64:(hf + 1) * 64, :]), writes=[f"KT{sl}"], dma=True)
                op("sp", lambda e: e.dma_start(out=QT[sl][0:64, :], in_=fm_d[0 + hp, hf * 64:(hf + 1) * 64, :]), writes=[f"QT{sl}"], dma=True)
                op("sp", lambda e: e.dma_start(out=QT[sl][64:67, :], in_=caug_d[h]), writes=[f"QTa{sl}"], dma=True)
                for k0 in range(0, NB, 8):
                    op("sp", lambda e, k0=k0: e.dma_start(out=VV[sl][:, k0:k0 + 8, 0:64],
                                                          in_=v_d[0, k0 * 128:(k0 + 8) * 128, h * 64:(h + 1) * 64].rearrange("(kb p) d -> p kb d", p=128)),
                       writes=[f"VV{sl}"], dma=True)
                for qt in range(NT):
                    op("dve", lambda e, qt=qt: e.tensor_scalar(out=bias_t[sl][:, qt, :], in0=ctok[:, :, h], scalar1=cql[:, qt + 1, h:h + 1],
                                                               scalar2=None, op0=ALU.subtract),
                       reads=["ctok", "cql"], writes=[f"bias{sl}"])

            items = []
            for h in range(NH):
                for qt in range(NT):
                    n = 4 * qt + 4
                    for kb in range(n):
                        items.append(dict(h=h, qt=qt, kb=kb, first=(kb == 0), last=(kb == n - 1), hfirst=(qt == 0 and kb == 0)))
            for i, it in enumerate(items):
                it["i"] = i
                it["tile"] = it["h"] * NT + it["qt"]
                j = it["kb"] - 4 * it["qt"]
                it["j"] = j
                it["c0"] = 128 * j if j > 0 else 0
            nper = len(items) // NH
            assert nper >= 10
            fox_loads(0)

            def fx_z(it):
                i, h, qt, kb, c0 = it["i"], it["h"], it["qt"], it["kb"], it["c0"]
                sl = h % 2
                if it["hfirst"] and h + 1 < NH:
                    pass
                b = i % 3
                op("pe", lambda e: e.matmul(ps[b][:, c0:512], lhsT=KT[sl][0:67, kb * 128:(kb + 1) * 128],
                                            rhs=QT[sl][0:67, qt * 512 + c0:(qt + 1) * 512], start=True, stop=True),
                   reads=[f"KT{sl}", f"KTa{sl}", f"QT{sl}", f"QTa{sl}"], writes=[f"ps{b}"])

            def fx_p(it):
                i, h, qt, kb, c0 = it["i"], it["h"], it["qt"], it["kb"], it["c0"]
                sl = h % 2
                b = i % 3
                pb = i % 4
                op("act", lambda e: e.activation(out=Pb[pb][:, c0:512], in_=ps[b][:, c0:512], func=AF.Exp,
                                                 bias=bias_t[sl][:, qt, kb:kb + 1], scale=1.0),
                   reads=[f"ps{b}", f"bias{sl}"], writes=[f"Pb{pb}"])
                if it["j"] >= 0:
                    op("pool", lambda e: e.tensor_tensor(out=Pb[pb][:, c0:c0 + 128], in0=Pb[pb][:, c0:c0 + 128], in1=cb[:, UINC, :], op=ALU.mult),
                       reads=[f"Pb{pb}", "cb"], writes=[f"Pb{pb}"])

            def fx_pv(it):
                i, h, qt, kb, c0 = it["i"], it["h"], it["qt"], it["kb"], it["c0"]
                sl = h % 2
                pb = i % 4
                yb = 4 + it["tile"] % 2
                op("pe", lambda e: e.matmul(ps[yb][0:65, c0:512], lhsT=VV[sl][:, kb, 0:65], rhs=Pb[pb][:, c0:512],
                                            start=it["first"], stop=it["last"]),
                   reads=[f"VV{sl}", f"VVa{sl}", f"Pb{pb}"], writes=[f"ps{yb}"])

            def fx_fin1(it):
                if not it["last"]:
                    return
                t2 = it["tile"] % 2
                yb = 4 + t2
                op("dve", lambda e: e.reciprocal(out=rs[t2][64:65, :], in_=ps[yb][64:65, :]), reads=[f"ps{yb}"], writes=[f"rs{t2}"])
                op("act", lambda e: e.activation(out=ysb[t2][:], in_=ps[yb][0:64, :], func=AF.Identity), reads=[f"ps{yb}"], writes=[f"ysb{t2}"])

            def fx_fin2(it):
                if not it["last"]:
                    return
                h, qt = it["h"], it["qt"]
                t2 = it["tile"] % 2
                op("pe", lambda e: e.matmul(ps[6][0:64, :], lhsT=cf[64:65, ONES, 0:64], rhs=rs[t2][64:65, :], start=True, stop=True),
                   reads=[f"rs{t2}", "cf"], writes=["ps6"])
                op("dve", lambda e: e.tensor_tensor(out=yo[t2][:], in0=ysb[t2][:], in1=ps[6][0:64, :], op=ALU.mult),
                   reads=[f"ysb{t2}", "ps6"], writes=[f"yo{t2}"])
                op("sp", lambda e: e.dma_start(out=yT_d[0, h * 64:(h + 1) * 64, qt * 512:(qt + 1) * 512], in_=yo[t2][:]),
                   reads=[f"yo{t2}"], writes=[f"yT_d0_{h}_{qt}"], dma=True)

            stages = [(fx_z, 0), (fx_p, 1), (fx_pv, 2), (fx_fin1, 3), (fx_fin2, 9)]
            n = len(items)
            for s in range(n + 10):
                for fn, sk in stages:
                    i = s - sk
                    if 0 <= i < n:
                        fn(items[i])
                if s < n and (s % nper) == 6:
                    h = items[s]["h"]
                    if h + 1 < NH:
                        fox_loads(h + 1)
            sy.barrier()

            op("dve", lambda e: e.memset(onesf[:], 1.0), writes=["onesf"])

            def sb_loads(hp):
                sl = hp % 2
                op("sp", lambda e: e.dma_start(out=KT[sl][:, :], in_=fm_d[12 + hp]), writes=[f"KT{sl}"], dma=True)
                op("sp", lambda e: e.dma_start(out=QT[sl][:, :], in_=fm_d[8 + hp]), writes=[f"QT{sl}"], dma=True)

            def sb_vload(h):
                sl = h % 2
                for k0 in range(0, NB, 8):
                    op("sp", lambda e, k0=k0: e.dma_start(out=VV[sl][:, k0:k0 + 8, 0:64],
                                                          in_=v_d[1, k0 * 128:(k0 + 8) * 128, h * 64:(h + 1) * 64].rearrange("(kb p) d -> p kb d", p=128)),
                       writes=[f"VV{sl}"], dma=True)
                    op("sp", lambda e, k0=k0: e.dma_start(out=dVV[sl][:, k0:k0 + 8, :],
                                                          in_=dv_d[k0 * 128:(k0 + 8) * 128, h * 64:(h + 1) * 64].rearrange("(kb p) d -> p kb d", p=128)),
                       writes=[f"dVV{sl}"], dma=True)

            items = []
            for hp in range(NH // 2):
                for qb in range(NB):
                    a, jq = qb // 4, qb % 4
                    for kt in range(a, -1, -1):
                        W = 128 * (jq + 1) if kt == a else 512
                        for hh in range(2):
                            items.append(dict(h=2 * hp + hh, hp=hp, hh=hh, qb=qb, a=a, jq=jq, kt=kt, W=W, first=(kt == a), last=(kt == 0)))
            for i, it in enumerate(items):
                it["i"] = i
            npair = len(items) // 2
            per_hp = npair // (NH // 2)
            assert per_hp >= 12
            sb_loads(0)
            sb_vload(0)
            sb_vload(1)

            def sb_z(it):
                i, hp, hh, qb, kt, W = it["i"], it["hp"], it["hh"], it["qb"], it["kt"], it["W"]
                sl = hp % 2
                p0_ = 64 * hh
                b = hh
                op("pe", lambda e: e.matmul(ps[b][:, 0:W], lhsT=QT[sl][p0_:p0_ + 64, qb * 128:(qb + 1) * 128],
                                            rhs=KT[sl][p0_:p0_ + 64, kt * 512: kt * 512 + W], start=True, stop=True),
                   reads=[f"KT{sl}", f"QT{sl}"], writes=[f"ps{b}"])

            def sb_w(it):
                i, W, jq, hh = it["i"], it["W"], it["jq"], it["hh"]
                b = hh
                wi = i % 6
                op("act", lambda e: e.activation(out=wbuf[wi][:, 0:W], in_=ps[b][:, 0:W], func=AF.Sigmoid, scale=-1.0),
                   reads=[f"ps{b}"], writes=[f"wbuf{wi}"])
                if it["first"]:
                    c0 = 128 * jq
                    op("pool", lambda e: e.tensor_tensor(out=wbuf[wi][:, c0:c0 + 128], in0=wbuf[wi][:, c0:c0 + 128], in1=cf[:, LSTR, :], op=ALU.mult),
                       reads=[f"wbuf{wi}", "cf"], writes=[f"wbuf{wi}"])
                    op("pool", lambda e: e.tensor_tensor(out=wbuf[wi][:, c0:c0 + 128], in0=wbuf[wi][:, c0:c0 + 128], in1=cf[:, UINC, :], op=ALU.add),
                       reads=[f"wbuf{wi}", "cf"], writes=[f"wbuf{wi}"])

            def rev(ap_, n):
                return bass.AP(ap_.tensor, ap_.offset + (n - 1), [[ap_.ap[0][0], 128], [-1, n]])

            def sb_scan(it):
                i, W, jq = it["i"], it["W"], it["jq"]
                wi = i % 6
                gi = i % 8
                if it["first"]:
                    init = 1.0
                    rd = [f"wbuf{wi}", "onesf"]
                else:
                    init = Gs[(i - 2) % 8][:, 0:1]
                    rd = [f"wbuf{wi}", "onesf", f"Gs{(i - 2) % 8}"]
                op("dve", lambda e: e.tensor_tensor_scan(out=rev(Gs[gi][:, 0:W], W), data0=rev(wbuf[wi][:, 0:W], W), data1=onesf[:, 0:W],
                                                         initial=init, op0=ALU.mult, op1=ALU.mult),
                   reads=rd, writes=[f"Gs{gi}"])
                if it["first"]:
                    c0 = 128 * jq
                    op("pool", lambda e: e.tensor_tensor(out=Gs[gi][:, c0:c0 + 128], in0=Gs[gi][:, c0:c0 + 128], in1=cb[:, LINC, :], op=ALU.mult),
                       reads=[f"Gs{gi}", "cb"], writes=[f"Gs{gi}"])

            def sb_tr(it):
                i, W = it["i"], it["W"]
                gi = i % 8
                ti = i % 4

                def tr(e):
                    for c in range(W // 128):
                        ins = e.matmul(ps[2 + ti][:, c * 128:(c + 1) * 128], lhsT=Gs[gi][:, c * 128:(c + 1) * 128], rhs=cb[:, IDENT, :],
                                       start=True, stop=True)
                    return ins
                op("pe", tr, reads=[f"Gs{gi}", "cb"], writes=[f"pst{ti}"])

            def sb_ev(it):
                i, W = it["i"], it["W"]
                ti = i % 4
                g6 = i % 6
                op("act", lambda e: e.activation(out=GT[g6][:, 0:W], in_=ps[2 + ti][:, 0:W], func=AF.Identity), reads=[f"pst{ti}"], writes=[f"GT{g6}"])

            def sb_pv(it):
                i, h, hp, hh, qb, kt, W, jq, a = it["i"], it["h"], it["hp"], it["hh"], it["qb"], it["kt"], it["W"], it["jq"], it["a"]
                sl = h % 2
                g6 = i % 6
                yb = 6 + (hp * NT + a) % 2
                ycol = slice(jq * 128, (jq + 1) * 128)
                yrow = slice(64 * hh, 64 * hh + 64)

                def mm(e):
                    if it["first"]:
                        e.matmul(ps[yb][yrow, ycol], lhsT=VV[sl][:, qb, 0:64], rhs=cb[:, IDENT, :], start=True, stop=False)
                    nsub = W // 128
                    for c in range(nsub):
                        ins = e.matmul(ps[yb][yrow, ycol], lhsT=dVV[sl][:, kt * 4 + c, :], rhs=GT[g6][:, c * 128:(c + 1) * 128],
                                       start=False, stop=(it["last"] and c == nsub - 1))
                    return ins
                op("pe", mm, reads=[f"VV{sl}", f"dVV{sl}", f"GT{g6}", "cb"], writes=[f"ps{yb}"])

            def sb_fin(it):
                if not (it["last"] and it["jq"] == 3 and it["hh"] == 1):
                    return
                hp, a = it["hp"], it["a"]
                t2 = (hp * NT + a) % 2
                yb = 6 + t2
                op("act", lambda e: e.activation(out=yo2[t2][:], in_=ps[yb][:, :], func=AF.Identity), reads=[f"ps{yb}"], writes=[f"yo2_{t2}"])
                op("sp", lambda e: e.dma_start(out=yT_d[1, hp * 128:(hp + 1) * 128, a * 512:(a + 1) * 512], in_=yo2[t2][:]),
                   reads=[f"yo2_{t2}"], writes=[f"yT_d1_{hp}_{a}"], dma=True)

            stages = [(sb_w, 1), (sb_scan, 2), (sb_tr, 3), (sb_ev, 4), (sb_pv, 5), (sb_fin, 6), (sb_z, 0)]
            for k in range(npair + 7):
                kk = k - 5
                if 0 < kk < npair and kk % per_hp == 0:
                    hpn = kk // per_hp
                    sb_vload(2 * hpn)
                    sb_vload(2 * hpn + 1)
                if k < npair and k % per_hp == 9 and k // per_hp + 1 < NH // 2:
                    sb_loads(k // per_hp + 1)
                for fn, sk in stages:
                    m = k - sk
                    if 0 <= m < npair:
                        fn(items[2 * m])
                        fn(items[2 * m + 1])
            sy.barrier()

        p012.close()

        def ln_tail(R, rk, g_t, b_t, st6, mv, sc2, g_eng="pool", b_eng="pool", sfx=""):
            ops = []
            for hh in range(2):
                ops.append(lambda hh=hh: op("dve", lambda e: e.bn_stats(out=st6[:, hh * 6:(hh + 1) * 6], in_=R[:, hh * 512:(hh + 1) * 512]),
                                            reads=[rk], writes=["st6" + sfx]))
            ops.append(lambda: op("dve", lambda e: e.bn_aggr(out=mv[:], in_=st6[:]), reads=["st6" + sfx], writes=["mv" + sfx]))
            ops.append(lambda: op("dve", lambda e: e.tensor_scalar_add(out=sc2[:, 0:1], in0=mv[:, 1:2], scalar1=EPS / (ALPHA * ALPHA)), reads=["mv" + sfx], writes=["sc2" + sfx]))
            ops.append(lambda: op("pool", lambda e: e.tensor_tensor(out=sc2[:, 0:1], in0=sc2[:, 0:1], in1=mhalf[:, 0:1], op=ALU.pow),
                                  reads=["sc2" + sfx, "mhalf"], writes=["sc2" + sfx]))
            ops.append(lambda: op("dve", lambda e: e.scalar_tensor_tensor(out=sc2[:, 1:2], in0=mv[:, 0:1], scalar=-1.0, in1=sc2[:, 0:1],
                                                                          op0=ALU.mult, op1=ALU.mult), reads=["mv" + sfx, "sc2" + sfx], writes=["sc2b" + sfx]))
            ops.append(lambda: op("act", lambda e: e.activation(out=R[:], in_=R[:], func=AF.Identity, bias=sc2[:, 1:2], scale=sc2[:, 0:1]),
                                  reads=[rk, "sc2" + sfx, "sc2b" + sfx], writes=[rk]))
            for q in range(4):
                qs = slice(q * 256, (q + 1) * 256)
                ops.append(lambda qs=qs: op(g_eng, lambda e: e.tensor_tensor(out=R[:, qs], in0=R[:, qs], in1=g_t[:, qs], op=ALU.mult),
                                            reads=[rk, "lnt"], writes=[rk]))
            for q in range(4):
                qs = slice(q * 256, (q + 1) * 256)
                ops.append(lambda qs=qs: op(b_eng, lambda e: e.tensor_tensor(out=R[:, qs], in0=R[:, qs], in1=b_t[:, qs], op=ALU.add),
                                            reads=[rk, "lnt"], writes=[rk]))
            return ops

        deferred = []

        def pop_deferred(n):
            for _ in range(n):
                if deferred:
                    deferred.pop(0)()

        with ExitStack() as p3:
            P = p3.enter_context
            wfb = P(nc.sbuf_tensor("wfb", [128, 4, D], BF16))
            wsbb = P(nc.sbuf_tensor("wsbb", [128, 4, D], BF16))
            wob = P(nc.sbuf_tensor("wob", [128, 8, D], BF16))
            lnt = P(nc.sbuf_tensor("lnt_sb", [128, 2, D], F32))
            gt1 = P(nc.sbuf_tensor("gtab1", [128, D], F32))
            yf = [P(nc.sbuf_tensor(f"yf{i}", [128, 4, 512], BF16)) for i in range(2)]
            ysl = [P(nc.sbuf_tensor(f"ysl{i}", [128, 4, 512], BF16)) for i in range(2)]
            gt = [P(nc.sbuf_tensor(f"gt{i}", [128, 16, 512], BF16)) for i in range(2)]
            xt = [P(nc.sbuf_tensor(f"xt3_{i}", [128, 4, D], F32)) for i in range(2)]
            m1 = [P(nc.sbuf_tensor(f"m1_{i}", [128, 512], F32)) for i in range(2)]
            m2 = [P(nc.sbuf_tensor(f"m2_{i}", [128, 512], F32)) for i in range(2)]
            mg = [P(nc.sbuf_tensor(f"mg{i}", [128, 8, 512], BF16)) for i in range(2)]
            Rb = [P(nc.sbuf_tensor(f"Rb{i}", [128, D], F32)) for i in range(4)]
            st6 = [P(nc.sbuf_tensor(f"st6_{i}", [128, 12], F32)) for i in range(4)]
            mv = [P(nc.sbuf_tensor(f"mv_{i}", [128, 2], F32)) for i in range(4)]
            sc2 = [P(nc.sbuf_tensor(f"sc2_{i}", [128, 2], F32)) for i in range(4)]
            for c in range(4):
                op("pool", lambda e, c=c: e.dma_start(out=wfb[:, c, :], in_=wfox_d[c * 128:(c + 1) * 128, :]), writes=["wfb"], dma=True)
                op("pool", lambda e, c=c: e.dma_start(out=wsbb[:, c, :], in_=wsb_d[c * 128:(c + 1) * 128, :]), writes=["wsbb"], dma=True)
            for c in range(8):
                op("pool", lambda e, c=c: e.dma_start(out=wob[:, c, :], in_=wo_d[c * 128:(c + 1) * 128, :]), writes=["wob"], dma=True)
            op("sp", lambda e: e.dma_start(out=lnt[:], in_=lnt_d[0:2].rearrange("c p n -> p c n")), writes=["lnt"], dma=True)
            op("sp", lambda e: e.dma_start(out=gt1[:], in_=gt_d[0]), writes=["gt1"], dma=True)

            def loadA(tt):
                sl = tt % 2
                tsl = slice(tt * 512, (tt + 1) * 512)
                op("sp", lambda e: e.dma_start(out=yf[sl][:], in_=yT_d[0, :, tsl].rearrange("(c p) t -> p c t", p=128)), writes=[f"yf{sl}"], dma=True)
                op("sp", lambda e: e.dma_start(out=ysl[sl][:], in_=yT_d[1, :, tsl].rearrange("(c p) t -> p c t", p=128)), writes=[f"ysl{sl}"], dma=True)
                op("sp", lambda e: e.dma_start(out=gt[sl][:], in_=fm_d[16:32, :, tsl].rearrange("c p t -> p c t")), writes=[f"gt{sl}"], dma=True)

            def loadX(tt):
                sl = tt % 2
                tsl = slice(tt * 512, (tt + 1) * 512)
                op("sp", lambda e: e.dma_start(out=xt[sl][:], in_=x_d[tsl, :].rearrange("(j p) d -> p j d", p=128)), writes=[f"xt3_{sl}"], dma=True)

            c3 = {"cnt": 0}

            def OC(tt):
                sl = tt % 2
                MG = mg[sl]
                for oc in range(8):
                    cnt = c3["cnt"]
                    ba, bb = (2 * cnt) % 4, (2 * cnt + 1) % 4
                    mi = cnt % 2
                    c3["cnt"] += 1

                    def mmab(e, oc=oc, ba=ba, bb=bb, sl=sl):
                        for c in range(4):
                            e.matmul(ps[ba][:], lhsT=wfb[:, c, oc * 128:(oc + 1) * 128], rhs=yf[sl][:, c, :], start=(c == 0), stop=(c == 3))
                        for c in range(4):
                            ins = e.matmul(ps[bb][:], lhsT=wsbb[:, c, oc * 128:(oc + 1) * 128], rhs=ysl[sl][:, c, :], start=(c == 0), stop=(c == 3))
                        return ins
                    op("pe", mmab, reads=["wfb", "wsbb", f"yf{sl}", f"ysl{sl}"], writes=[f"ps{ba}", f"ps{bb}"])
                    op("dve", lambda e, oc=oc, ba=ba, mi=mi, sl=sl: e.tensor_tensor(out=m1[mi][:], in0=ps[ba][:], in1=gt[sl][:, oc, :], op=ALU.mult),
                       reads=[f"ps{ba}", f"gt{sl}"], writes=[f"m1_{mi}"])
                    op("dve", lambda e, oc=oc, bb=bb, mi=mi, sl=sl: e.tensor_tensor(out=m2[mi][:], in0=ps[bb][:], in1=gt[sl][:, 8 + oc, :], op=ALU.mult),
                       reads=[f"ps{bb}", f"gt{sl}"], writes=[f"m2_{mi}"])
                    op("pool", lambda e, oc=oc, mi=mi, MG=MG: e.tensor_tensor(out=MG[:, oc, :], in0=m1[mi][:], in1=m2[mi][:], op=ALU.add),
                       reads=[f"m1_{mi}", f"m2_{mi}"], writes=[f"mg{sl}"])
                    pop_deferred(9)

            def MO(tt):
                sl = tt % 2
                MG = mg[sl]
                chains = []
                for j in range(4):
                    blk = tt * 4 + j
                    R = Rb[j]
                    rk = f"Rb{j}"
                    for hh in range(2):
                        b = 4 + (2 * blk + hh) % 4

                        def mmo(e, j=j, hh=hh, b=b, MG=MG):
                            for oc in range(8):
                                ins = e.matmul(ps[b][:], lhsT=MG[:, oc, j * 128:(j + 1) * 128], rhs=wob[:, oc, hh * 512:(hh + 1) * 512],
                                               start=(oc == 0), stop=(oc == 7))
                            return ins
                        op("pe", mmo, reads=[f"mg{sl}", "wob"], writes=[f"ps{b}"])
                        op("dve", lambda e, hh=hh, b=b, R=R: e.tensor_tensor(out=R[:, hh * 512:(hh + 1) * 512], in0=ps[b][:],
                                                                              in1=gt1[:, hh * 512:(hh + 1) * 512], op=ALU.mult),
                           reads=[f"ps{b}", "gt1"], writes=[rk])
                    op("pool", lambda e, j=j, R=R, sl=sl: e.tensor_tensor(out=R[:], in0=R[:], in1=xt[sl][:, j, :], op=ALU.add),
                       reads=[f"xt3_{sl}", rk], writes=[rk])
                    ch = ln_tail(R, rk, lnt[:, 0, :], lnt[:, 1, :], st6[j], mv[j], sc2[j], g_eng="dve", sfx=f"_{j}")
                    ch.append(lambda blk=blk, R=R, rk=rk: op("sp", lambda e: e.dma_start(out=x1_d[blk * 128:(blk + 1) * 128, :], in_=R[:]),
                                                             reads=[rk], writes=[f"x1_d{blk}"], dma=True))
                    chains.append(ch)
                for k in range(max(len(c) for c in chains)):
                    for c in chains:
                        if k < len(c):
                            deferred.append(c[k])

            loadA(0)
            loadX(0)
            if NT > 1:
                loadA(1)
            OC(0)
            for tt in range(NT):
                if tt + 2 < NT:
                    loadA(tt + 2)
                if tt + 1 < NT:
                    loadX(tt + 1)
                    OC(tt + 1)
                else:
                    pop_deferred(len(deferred))
                MO(tt)
            pop_deferred(len(deferred))
            sy.barrier()

        TW = 256
        NT2 = S // TW
        JB = TW // 128
        with ExitStack() as p4:
            P = p4.enter_context
            wupb = P(nc.sbuf_tensor("wupb", [128, 8, 2 * DFF], BF16))
            wdb = P(nc.sbuf_tensor("wdb", [128, NAC, D], BF16))
            lnt = P(nc.sbuf_tensor("lnt2", [128, 2, D], F32))
            gt2 = P(nc.sbuf_tensor("gtab2", [128, D], F32))
            convp = P(nc.sbuf_tensor("convp_sb", [128, NFC, 4], F32))
            halo = [P(nc.sbuf_tensor(f"halo{i}", [128, NFC, 2], F32)) for i in range(2)]
            xt = [P(nc.sbuf_tensor(f"xt4_{i}", [128, JB, D], F32)) for i in range(2)]
            uT = [P(nc.sbuf_tensor(f"u2T{i}", [128, 8, TW], BF16)) for i in range(2)]
            hb = [P(nc.sbuf_tensor(f"hb{i}", [128, TW + 2], F32)) for i in range(2)]
            cv = [P(nc.sbuf_tensor(f"cv{i}", [128, TW], F32)) for i in range(3)]
            sg = [P(nc.sbuf_tensor(f"sg{i}", [128, TW], F32)) for i in range(2)]
            act = [P(nc.sbuf_tensor(f"act{i}", [128, NAC, TW], BF16)) for i in range(2)]
            Rb = [P(nc.sbuf_tensor(f"Rb4_{i}", [128, D], F32)) for i in range(2)]
            st6 = P(nc.sbuf_tensor("st6b", [128, 12], F32))
            mv = P(nc.sbuf_tensor("mvb", [128, 2], F32))
            sc2 = P(nc.sbuf_tensor("sc2b_", [128, 2], F32))
            for c in range(8):
                op("pool", lambda e, c=c: e.dma_start(out=wupb[:, c, :], in_=wup_d[c * 128:(c + 1) * 128, :]), writes=["wupb"], dma=True)
            for c in range(NAC):
                op("pool", lambda e, c=c: e.dma_start(out=wdb[:, c, :], in_=wdown_d[c * 128:(c + 1) * 128, :]), writes=["wdb"], dma=True)
            op("sp", lambda e: e.dma_start(out=lnt[:], in_=lnt_d[2:4].rearrange("c p n -> p c n")), writes=["lnt"], dma=True)
            op("sp", lambda e: e.dma_start(out=gt2[:], in_=gt_d[1]), writes=["gt2"], dma=True)
            op("sp", lambda e: e.dma_start(out=convp[:], in_=convp_d), writes=["convp"], dma=True)
            op("dve", lambda e: e.memset(halo[1][:], 0.0), writes=["halo1"])

            def loads4(tt):
                sl = tt % 2
                op("sp", lambda e: e.dma_start(out=xt[sl][:], in_=x1_d[tt * TW:(tt + 1) * TW, :].rearrange("(j p) d -> p j d", p=128)),
                   writes=[f"xt4_{sl}"], dma=True)
            cnts = {"pc": 0, "hc": 0}

            def T4(tt):
                sl = tt % 2
                X, U = xt[sl], uT[sl]
                for kc in range(8):
                    b = cnts["pc"] % 4
                    cnts["pc"] += 1

                    def tr(e, kc=kc, b=b, X=X):
                        for j in range(JB):
                            ins = e.transpose(out=ps[b][:, j * 128:(j + 1) * 128], in_=X[:, j, kc * 128:(kc + 1) * 128], identity=identf[:])
                        return ins
                    op("pe", tr, reads=[f"xt4_{sl}", "identf"], writes=[f"ps{b}"])
                    op("act", lambda e, kc=kc, b=b, U=U: e.activation(out=U[:, kc, :], in_=ps[b][:, 0:TW], func=AF.Identity,
                                                                      bias=modp[:, 2, kc:kc + 1], scale=modp[:, 3, kc:kc + 1]),
                       reads=[f"ps{b}", "modp"], writes=[f"u2T{sl}"])

            def U4(tt):
                sl = tt % 2
                U, A = uT[sl], act[sl]
                uk = f"u2T{sl}"
                base = cnts["hc"]
                cnts["hc"] += 2 * NAC

                def stA(c):
                    i, half = c // 2, c % 2
                    fc = i + half * NAC
                    b = cnts["pc"] % 4
                    cnts["pc"] += 1
                    hi = (base + c) % 2
                    ci = (base + c) % 3
                    H, CV = hb[hi], cv[ci]

                    def mmu(e):
                        for kc in range(8):
                            ins = e.matmul(ps[b][:, 0:TW], lhsT=wupb[:, kc, fc * 128:(fc + 1) * 128], rhs=U[:, kc, :], start=(kc == 0), stop=(kc == 7))
                        return ins
                    op("pe", mmu, reads=["wupb", uk], writes=[f"ps{b}"])
                    op("act", lambda e: e.activation(out=H[:, 2:TW + 2], in_=ps[b][:, 0:TW], func=AF.Identity), reads=[f"ps{b}"], writes=[f"hb{hi}"])
                    op("act", lambda e: e.activation(out=CV[:], in_=ps[b][:, 0:TW], func=AF.Identity, bias=convp[:, fc, 3:4], scale=convp[:, fc, 2:3]),
                       reads=[f"ps{b}", "convp"], writes=[f"cv{ci}"])
                    op("act", lambda e: e.activation(out=halo[sl][:, fc, :], in_=ps[b][:, TW - 2:TW], func=AF.Identity),
                       reads=[f"ps{b}"], writes=[f"halo{sl}"])
                    op("pool", lambda e: e.tensor_copy(out=H[:, 0:2], in_=halo[1 - sl][:, fc, :]), reads=[f"halo{1 - sl}"], writes=[f"hbh{hi}"])

                def stB(c):
                    i, half = c // 2, c % 2
                    fc = i + half * NAC
                    hi = (base + c) % 2
                    ci = (base + c) % 3
                    H, CV = hb[hi], cv[ci]
                    op("dve", lambda e: e.scalar_tensor_tensor(out=CV[:], in0=H[:, 1:TW + 1], scalar=convp[:, fc, 1:2], in1=CV[:],
                                                               op0=ALU.mult, op1=ALU.add),
                       reads=[f"hb{hi}", f"hbh{hi}", "convp", f"cv{ci}"], writes=[f"cv{ci}"])
                    op("dve", lambda e: e.scalar_tensor_tensor(out=CV[:], in0=H[:, 0:TW], scalar=convp[:, fc, 0:1], in1=CV[:],
                                                               op0=ALU.mult, op1=ALU.add),
                       reads=[f"hb{hi}", f"hbh{hi}", "convp", f"cv{ci}"], writes=[f"cv{ci}"])

                def stC(c):
                    i, half = c // 2, c % 2
                    ci = (base + c) % 3
                    CV = cv[ci]
                    SG = sg[i % 2]
                    if half == 0:
                        op("act", lambda e: e.activation(out=SG[:], in_=CV[:], func=AF.Silu), reads=[f"cv{ci}"], writes=[f"sg{i % 2}"])
                    else:
                        op("pool", lambda e: e.tensor_tensor(out=A[:, i, :], in0=SG[:], in1=CV[:], op=ALU.mult),
                           reads=[f"cv{ci}", f"sg{i % 2}"], writes=[f"act{sl}"])

                nchunk = 2 * NAC
                for st_ in range(nchunk + 2):
                    if st_ < nchunk:
                        stA(st_)
                    if 0 <= st_ - 1 < nchunk:
                        stB(st_ - 1)
                    if 0 <= st_ - 2 < nchunk:
                        stC(st_ - 2)
                    pop_deferred(1)

            def D4(tt):
                sl = tt % 2
                X, A = xt[sl], act[sl]
                for j in range(JB):
                    blk = tt * JB + j
                    R = Rb[blk % 2]
                    rk = f"Rb{blk % 2}"
                    for hh in range(2):
                        b = 4 + (2 * blk + hh) % 4

                        def mmd2(e, j=j, hh=hh, b=b, A=A):
                            for i in range(NAC):
                                ins = e.matmul(ps[b][:], lhsT=A[:, i, j * 128:(j + 1) * 128], rhs=wdb[:, i, hh * 512:(hh + 1) * 512],
                                               start=(i == 0), stop=(i == NAC - 1))
                            return ins
                        op("pe", mmd2, reads=[f"act{sl}", "wdb"], writes=[f"ps{b}"])
                        op("dve", lambda e, hh=hh, b=b, R=R: e.tensor_tensor(out=R[:, hh * 512:(hh + 1) * 512], in0=ps[b][:],
                                                                              in1=gt2[:, hh * 512:(hh + 1) * 512], op=ALU.mult),
                           reads=[f"ps{b}", "gt2"], writes=[rk])
                    op("pool", lambda e, j=j, R=R, X=X: e.tensor_tensor(out=R[:], in0=R[:], in1=X[:, j, :], op=ALU.add),
                       reads=[f"xt4_{sl}", rk], writes=[rk])
                    deferred.extend(ln_tail(R, rk, lnt[:, 0, :], lnt[:, 1, :], st6, mv, sc2, g_eng="dve", b_eng="dve"))
                    deferred.append(lambda blk=blk, R=R, rk=rk: op("sp", lambda e: e.dma_start(out=out_d[blk * 128:(blk + 1) * 128, :], in_=R[:]),
                                                                   reads=[rk], writes=[f"out_d{blk}"], dma=True))

            loads4(0)
            T4(0)
            for tt in range(NT2):
                if tt + 1 < NT2:
                    loads4(tt + 1)
                U4(tt)
                if tt + 1 < NT2:
                    T4(tt + 1)
                D4(tt)
            pop_deferred(len(deferred))
            sy.barrier()
    return nc


def _consts():
    r = np.arange(128)[:, None]
    c = np.arange(128)[None, :]
    dm = (r == c - 1).astype(np.float32) - (r == c).astype(np.float32)
    eb = ((r == 127) & (c == 0)).astype(np.float32)
    six = np.stack([(r == c), (r <= c), (r >= c), (r > c), (r < c), np.ones((128, 128), bool)]).astype(np.float32)
    return np.concatenate([six, dm[None], eb[None]], axis=0)


def prep_shared(w_ada, b_ada, w_in, b_forget, w_fox_proj, w_sb_proj, w_o, ln1_g, ln1_b, w_up, conv_w, conv_b, w_down, ln2_g, ln2_b):
    f = lambda a: np.ascontiguousarray(np.asarray(a, dtype=np.float32))
    w_in = f(w_in)
    w_in_r = np.concatenate([w_in[:, 0:512], w_in[:, 512:1024], w_in[:, 1544:2056], w_in[:, 2056:2568], w_in[:, 3080:4104],
                             w_in[:, 4104:5128], w_in[:, 1024:1536], w_in[:, 2568:3080], w_in[:, 1536:1544]], axis=1)
    convp = np.stack([f(conv_w)[0], f(conv_w)[1], f(conv_w)[2], f(conv_b)], axis=1).reshape(NFC, 128, 4).transpose(1, 0, 2)
    lnt = np.stack([np.broadcast_to(f(v)[None, :], (128, D)) for v in (ln1_g, ln1_b, ln2_g, ln2_b)])
    return {
        "w_ada": f(w_ada), "b_ada": f(b_ada).reshape(12, 1, 512), "w_in": f(w_in_r),
        "bft": f(np.broadcast_to(np.tile(f(b_forget), 4)[None, :], (128, 32))),
        "w_fox": f(w_fox_proj), "w_sb": f(w_sb_proj), "w_o": f(w_o), "lnt": f(lnt), "w_up": f(w_up),
        "convp": f(convp), "w_down": f(w_down), "consts": _consts(),
    }


def kernel(x, c, w_ada, b_ada, w_in, b_forget, w_fox_proj, w_sb_proj, w_o, ln1_g, ln1_b, w_up, conv_w, conv_b, w_down, ln2_g, ln2_b):
    x = np.asarray(x, dtype=np.float32)
    c = np.asarray(c, dtype=np.float32)
    B, S, _ = x.shape
    shared = prep_shared(w_ada, b_ada, w_in, b_forget, w_fox_proj, w_sb_proj, w_o, ln1_g, ln1_b, w_up, conv_w, conv_b, w_down, ln2_g, ln2_b)
    nc = build_nc(S)
    in_maps = []
    for b in range(B):
        m = dict(shared)
        m["x"] = np.ascontiguousarray(x[b])
        m["cT"] = np.ascontiguousarray(c[b].reshape(8, 128).T)
        in_maps.append(m)
    res = run_bass_kernel_spmd(nc, in_maps, core_ids=list(range(B)))
    return np.stack([r["out"] for r in res.results], axis=0).astype(np.float32)
```

```python
import numpy as np
from contextlib import ExitStack
import concourse.bass as bass
import concourse.mybir as mybir
from concourse.bass_utils import run_bass_kernel_spmd

F32 = mybir.dt.float32
BF16 = mybir.dt.bfloat16
AF = mybir.ActivationFunctionType
ALU = mybir.AluOpType
AX = mybir.AxisListType

D = 1024
DFF = 2816
NH = 8
DH = 64
NFC = 2 * DFF // 128
NAC = DFF // 128
ALPHA = 2.0 ** 0.25
EPS = 1e-5


class Sync:
    def __init__(self, nc, st):
        self.nc = nc
        self.st = st
        self.nsem = 0
        self.engs = {"sp": nc.sync, "act": nc.scalar, "dve": nc.vector, "pool": nc.gpsimd, "pe": nc.tensor}
        self.cs = {e: [self._newsem(), 0] for e in self.engs}
        self.waited = {e: {} for e in self.engs}
        self.lastw = {}
        self.readers = {}
        self.NDS = 8
        self.dsem = {e: [self._newsem() for _ in range(self.NDS)] for e in ("sp", "pool")}
        self.dcnt = {e: [0] * self.NDS for e in ("sp", "pool")}
        self.dn = {e: 0 for e in ("sp", "pool")}

    def _newsem(self):
        self.nsem += 1
        return self.st.enter_context(self.nc.semaphore(f"sy{self.nsem}"))

    def _wait(self, eng, sem, val):
        w = self.waited[eng]
        if w.get(sem, 0) >= val:
            return
        self.engs[eng].wait_ge(sem, val)
        w[sem] = val

    def op(self, eng, fn, reads=(), writes=(), dma=False):
        deps = []
        for k in reads:
            lw = self.lastw.get(k)
            if lw is not None:
                if lw[2] != eng or lw[3] or eng in ("act", "dve", "pool"):
                    deps.append(lw)
        for k in writes:
            lw = self.lastw.get(k)
            if lw is not None and (lw[2] != eng or lw[3] or eng in ("act", "dve", "pool")):
                deps.append(lw)
            for r in self.readers.get(k, {}).values():
                if r[2] != eng or r[3] or eng in ("act", "dve", "pool"):
                    deps.append(r)
        for d in deps:
            self._wait(eng, d[0], d[1])
        e = self.engs[eng]
        if dma:
            i = self.dn[eng] % self.NDS
            self.dn[eng] += 1
            sem = self.dsem[eng][i]
            if self.dcnt[eng][i] > 0:
                self._wait(eng, sem, self.dcnt[eng][i])
            ins = fn(e)
            self.dcnt[eng][i] += 16
            ins.then_inc(sem, 16)
            ref = (sem, self.dcnt[eng][i], eng, True)
            rkey = ("dma", eng, i)
        else:
            c = self.cs[eng]
            if c[1] >= 30000:
                c[0] = self._newsem()
                c[1] = 0
            ins = fn(e)
            c[1] += 1
            ins.then_inc(c[0], 1)
            ref = (c[0], c[1], eng, False)
            rkey = eng
        for k in reads:
            self.readers.setdefault(k, {})[rkey] = ref
        for k in writes:
            self.lastw[k] = ref
            self.readers[k] = {}
        return ref

    def barrier(self):
        refs = [(c[0], c[1]) for c in self.cs.values() if c[1] > 0]
        for e in self.dsem:
            for i in range(self.NDS):
                if self.dcnt[e][i] > 0:
                    refs.append((self.dsem[e][i], self.dcnt[e][i]))
        for eng in self.engs:
            for (s, v) in refs:
                if s is self.cs[eng][0]:
                    continue
                self._wait(eng, s, v)
        self.lastw.clear()
        self.readers.clear()


def build_nc(S, dbg=False):
    assert S % 512 == 0
    NT = S // 512
    NB = S // 128
    nc = bass.Bass("TRN2", target_bir_lowering=False)
    okind = "ExternalOutput" if dbg else "Internal"

    def din(name, shape, dt=F32):
        return nc.dram_tensor(name, list(shape), dt, kind="ExternalInput").ap()

    x_d = din("x", [S, D])
    cT_d = din("cT", [128, 8])
    wada_d = din("w_ada", [D, 6 * D])
    bada_d = din("b_ada", [12, 1, 512])
    win_d = din("w_in", [D, 5128])
    bft_d = din("bft", [128, 32])
    wfox_d = din("w_fox", [512, D])
    wsb_d = din("w_sb", [512, D])
    wo_d = din("w_o", [D, D])
    lnt_d = din("lnt", [4, 128, D])
    wup_d = din("w_up", [D, 2 * DFF])
    convp_d = din("convp", [128, NFC, 4])
    wdown_d = din("w_down", [DFF, D])
    consts_d = din("consts", [8, 128, 128])
    out_d = nc.dram_tensor("out", [S, D], F32, kind="ExternalOutput").ap()

    fm_d = nc.dram_tensor("fm_s", [32, 128, S], BF16, kind=okind).ap()
    v_d = nc.dram_tensor("v_s", [2, S, 512], BF16, kind=okind).ap()
    caug_d = nc.dram_tensor("caug_s", [8, 3, S], BF16, kind=okind).ap()
    yT_d = nc.dram_tensor("yT_s", [2, 512, S], BF16, kind=okind).ap()
    x1_d = nc.dram_tensor("x1_s", [S, D], F32, kind=okind).ap()
    dv_d = nc.dram_tensor("dv_s", [S, 512], BF16, kind=okind).ap()
    gt_d = nc.dram_tensor("gt_s", [2, 128, D], F32).ap()

    with ExitStack() as st:
        E = st.enter_context
        sy = Sync(nc, st)
        op = sy.op
        ps = [E(nc.psum_tensor(f"psb{i}", [128, 512], F32)) for i in range(8)]

        identf = E(nc.sbuf_tensor("identf", [128, 128], F32))
        mhalf = E(nc.sbuf_tensor("mhalf", [128, 1], F32))
        modp = E(nc.sbuf_tensor("modp", [128, 4, 8], F32))
        p012 = ExitStack()
        E2 = p012.enter_context
        cf = E2(nc.sbuf_tensor("cf", [128, 8, 128], F32))
        cb = E2(nc.sbuf_tensor("cb", [128, 8, 128], BF16))
        ctok = E2(nc.sbuf_tensor("ctok", [128, NB, 8], F32))
        cql = E2(nc.sbuf_tensor("cql", [128, NT + 1, 8], F32))
        IDENT, UINC, LINC, LSTR, USTR, ONES, DM, EB = range(8)

        op("sp", lambda e: e.dma_start(out=identf[:], in_=consts_d[0]), writes=["identf"], dma=True)
        op("dve", lambda e: e.memset(mhalf[:], -0.5), writes=["mhalf"])
        op("sp", lambda e: e.dma_start(out=cf[:], in_=consts_d.rearrange("c p n -> p c n")), writes=["cf"], dma=True)
        op("pool", lambda e: e.dma_start(out=cb[:], in_=consts_d.rearrange("c p n -> p c n")), writes=["cb"], dma=True)

        with ExitStack() as p0:
            P = p0.enter_context
            cts = P(nc.sbuf_tensor("cts", [128, 8], F32))
            modtab = P(nc.sbuf_tensor("modtab", [128, 6 * D], F32))
            crep = P(nc.sbuf_tensor("crep", [128, 8, 128], F32))
            wa = [P(nc.sbuf_tensor(f"wa{i}", [128, 8, 512], F32)) for i in range(2)]
            bad = [P(nc.sbuf_tensor(f"bad{i}", [1, 512], F32)) for i in range(2)]
            dtmp = P(nc.sbuf_tensor("dtmp", [128, 8, 128], F32))
            op("sp", lambda e: e.dma_start(out=cts[:], in_=cT_d), writes=["cts"], dma=True)
            for kc in range(8):
                op("dve", lambda e, kc=kc: e.tensor_copy(out=crep[:, kc, :], in_=cts[:, kc:kc + 1].to_broadcast([128, 128])),
                   reads=["cts"], writes=[f"crep{kc}"])
            for g in range(12):
                sl = g % 2
                op("sp", lambda e, g=g, sl=sl: e.dma_start(
                    out=wa[sl][:], in_=wada_d[:, g * 512:(g + 1) * 512].rearrange("(kc p) n -> p kc n", p=128)),
                   writes=[f"wa{sl}"], dma=True)
                op("sp", lambda e, g=g, sl=sl: e.dma_start(out=bad[sl][:], in_=bada_d[g]), writes=[f"bad{sl}"], dma=True)
                bank = ps[g % 2]

                def mm(e, g=g, sl=sl, bank=bank):
                    for kc in range(8):
                        e.matmul(bank[:], lhsT=crep[:, kc, :], rhs=wa[sl][:, kc, :], start=(kc == 0), stop=False)
                    return e.matmul(bank[:], lhsT=cf[0:1, ONES, :], rhs=bad[sl][0:1, :], start=False, stop=True)
                op("pe", mm, reads=[f"wa{sl}", f"bad{sl}", "cf"] + [f"crep{k}" for k in range(8)], writes=[f"ps{g % 2}"])
                addone = g in (2, 3, 8, 9)
                if g in (4, 5, 10, 11):
                    op("dve", lambda e, g=g, bank=bank: e.tensor_scalar(out=modtab[:, g * 512:(g + 1) * 512], in0=bank[:], scalar1=1.0,
                                                                        scalar2=1.0 / ALPHA, op0=ALU.add, op1=ALU.mult),
                       reads=[f"ps{g % 2}"], writes=[f"modtab{g}"])
                elif addone:
                    op("dve", lambda e, g=g, bank=bank: e.tensor_scalar_add(out=modtab[:, g * 512:(g + 1) * 512], in0=bank[:], scalar1=1.0),
                       reads=[f"ps{g % 2}"], writes=[f"modtab{g}"])
                else:
                    op("act", lambda e, g=g, bank=bank: e.activation(out=modtab[:, g * 512:(g + 1) * 512], in_=bank[:], func=AF.Identity),
                       reads=[f"ps{g % 2}"], writes=[f"modtab{g}"])
            for vi, base in enumerate((0, 1024, 3072, 4096)):
                for k in range(8):
                    op("dve", lambda e, base=base, k=k: e.tensor_tensor(
                        out=dtmp[:, k, :], in0=modtab[:, base + k * 128: base + (k + 1) * 128], in1=cf[:, IDENT, :], op=ALU.mult),
                       reads=[f"modtab{(base + k * 128) // 512}", "cf"], writes=["dtmp"])
                op("dve", lambda e, vi=vi: e.tensor_reduce(out=modp[:, vi, :], in_=dtmp[:], axis=AX.X, op=ALU.add),
                   reads=["dtmp"], writes=["modp"])
            op("sp", lambda e: e.dma_start(out=gt_d[0], in_=modtab[:, 2048:3072]), reads=["modtab4", "modtab5"], writes=["gt_d0"], dma=True)
            op("sp", lambda e: e.dma_start(out=gt_d[1], in_=modtab[:, 5120:6144]), reads=["modtab10", "modtab11"], writes=["gt_d1"], dma=True)
            sy.barrier()

        with ExitStack() as p1:
            P = p1.enter_context
            winb = P(nc.sbuf_tensor("winb", [128, 8, 5128], BF16))
            xt = [P(nc.sbuf_tensor(f"xt{i}", [128, 4, D], F32)) for i in range(2)]
            uT = [P(nc.sbuf_tensor(f"uT{i}", [128, 8, 512], BF16)) for i in range(2)]
            fst = [P(nc.sbuf_tensor(f"fst{i}", [128, 4, 512], BF16)) for i in range(3)]
            vst = [P(nc.sbuf_tensor(f"vst{i}", [128, 4, 1024], BF16)) for i in range(2)]
            dvst = [P(nc.sbuf_tensor(f"dvst{i}", [128, 4, 512], BF16)) for i in range(2)]
            bft = P(nc.sbuf_tensor("bft_sb", [128, 32], F32))
            fb = P(nc.sbuf_tensor("fb", [128, 32], F32))
            ef = P(nc.sbuf_tensor("ef", [128, 32], F32))
            lf = [P(nc.sbuf_tensor(f"lf{i}", [128, 32], F32)) for i in range(2)]
            d32 = P(nc.sbuf_tensor("d32", [8, 512], F32))
            r1 = P(nc.sbuf_tensor("r1", [8, 512], F32))
            r2 = P(nc.sbuf_tensor("r2", [8, 512], F32))
            cst = [P(nc.sbuf_tensor(f"cst{i}", [8, 3, 512], BF16)) for i in range(2)]

            for kc in range(8):
                op("pool", lambda e, kc=kc: e.dma_start(out=winb[:, kc, :], in_=win_d[kc * 128:(kc + 1) * 128, :]),
                   writes=[f"winb{kc}"], dma=True)
            WIN = [f"winb{k}" for k in range(8)]
            op("sp", lambda e: e.dma_start(out=bft[:], in_=bft_d), writes=["bft"], dma=True)
            op("dve", lambda e: e.memset(cql[:, 0, :], 0.0), writes=["cql"])

            def load_x(tt):
                op("sp", lambda e: e.dma_start(out=xt[tt % 2][:], in_=x_d[tt * 512:(tt + 1) * 512, :].rearrange("(j p) d -> p j d", p=128)),
                   writes=[f"xt{tt % 2}"], dma=True)
            load_x(0)
            pcnt = [0]

            def nbank():
                pcnt[0] += 1
                return pcnt[0] % 6
            fcnt = [0]
            for tt in range(NT):
                if tt + 1 < NT:
                    load_x(tt + 1)
                X = xt[tt % 2]
                U = uT[tt % 2]
                ukeys = [f"uT{tt % 2}_{k}" for k in range(8)]
                for kc in range(8):
                    b = nbank()

                    def tr(e, kc=kc, b=b, X=X):
                        for j in range(4):
                            ins = e.transpose(out=ps[b][:, j * 128:(j + 1) * 128], in_=X[:, j, kc * 128:(kc + 1) * 128], identity=cf[:, IDENT, :])
                        return ins
                    op("pe", tr, reads=[f"xt{tt % 2}", "cf"], writes=[f"ps{b}"])
                    if kc % 2 == 0:
                        op("act", lambda e, kc=kc, b=b, U=U: e.activation(out=U[:, kc, :], in_=ps[b][:], func=AF.Identity,
                                                                          bias=modp[:, 0, kc:kc + 1], scale=modp[:, 1, kc:kc + 1]),
                           reads=[f"ps{b}", "modp"], writes=[ukeys[kc]])
                    else:
                        op("dve", lambda e, kc=kc, b=b, U=U: e.tensor_scalar(out=U[:, kc, :], in0=ps[b][:], scalar1=modp[:, 1, kc:kc + 1],
                                                                             scalar2=modp[:, 0, kc:kc + 1], op0=ALU.mult, op1=ALU.add),
                           reads=[f"ps{b}", "modp"], writes=[ukeys[kc]])
                for oc in range(32):
                    b = nbank()

                    def mm(e, oc=oc, b=b, U=U):
                        for kc in range(8):
                            ins = e.matmul(ps[b][:], lhsT=winb[:, kc, oc * 128:(oc + 1) * 128], rhs=U[:, kc, :], start=(kc == 0), stop=(kc == 7))
                        return ins
                    op("pe", mm, reads=WIN + ukeys, writes=[f"ps{b}"])
                    fs = fcnt[0] % 3
                    dst = fst[fs][:, oc % 4, :]
                    if oc < 16 and (oc // 4) % 2 == 0:
                        op("dve", lambda e, b=b, dst=dst: e.tensor_scalar_mul(out=dst, in0=ps[b][:], scalar1=0.125),
                           reads=[f"ps{b}"], writes=[f"fst{fs}"])
                    elif oc < 16:
                        op("dve", lambda e, b=b, dst=dst: e.tensor_copy(out=dst, in_=ps[b][:]), reads=[f"ps{b}"], writes=[f"fst{fs}"])
                    else:
                        op("act", lambda e, b=b, dst=dst: e.activation(out=dst, in_=ps[b][:], func=AF.Sigmoid),
                           reads=[f"ps{b}"], writes=[f"fst{fs}"])
                    if oc % 4 == 3:
                        oc0 = oc - 3
                        op("sp", lambda e, oc0=oc0, fs=fs, tt=tt: e.dma_start(
                            out=fm_d[oc0:oc0 + 4, :, tt * 512:(tt + 1) * 512].rearrange("c p t -> p c t"), in_=fst[fs][:]),
                           reads=[f"fst{fs}"], writes=[f"fm_d{oc0}_{tt}"], dma=True)
                        fcnt[0] += 1
                VS = vst[tt % 2]
                for j in range(4):
                    for ab in range(2):
                        b = nbank()

                        def mmv(e, j=j, ab=ab, b=b, U=U):
                            for kc in range(8):
                                ins = e.matmul(ps[b][:], lhsT=U[:, kc, j * 128:(j + 1) * 128],
                                               rhs=winb[:, kc, 4096 + ab * 512: 4096 + (ab + 1) * 512], start=(kc == 0), stop=(kc == 7))
                            return ins
                        op("pe", mmv, reads=WIN + ukeys, writes=[f"ps{b}"])
                        dst = VS[:, j, ab * 512:(ab + 1) * 512]
                        if ab == 0:
                            op("act", lambda e, b=b, dst=dst: e.activation(out=dst, in_=ps[b][:], func=AF.Identity),
                               reads=[f"ps{b}"], writes=[f"vst{tt % 2}"])
                        else:
                            op("dve", lambda e, b=b, dst=dst: e.tensor_copy(out=dst, in_=ps[b][:]), reads=[f"ps{b}"], writes=[f"vst{tt % 2}"])
                for ab in range(2):
                    op("sp", lambda e, ab=ab, tt=tt, VS=VS: e.dma_start(
                        out=v_d[ab, tt * 512:(tt + 1) * 512, :].rearrange("(j p) d -> p j d", p=128), in_=VS[:, :, ab * 512:(ab + 1) * 512]),
                       reads=[f"vst{tt % 2}"], writes=[f"v_d{ab}_{tt}"], dma=True)
                DVS = dvst[tt % 2]
                for j in range(4):
                    b = nbank()
                    blk = tt * 4 + j
                    prev = VS[:, j - 1, 512:1024] if j > 0 else vst[(tt - 1) % 2][:, 3, 512:1024]

                    def mmdv(e, j=j, b=b, blk=blk, prev=prev, VS=VS):
                        ins = e.matmul(ps[b][:], lhsT=cb[:, DM, :], rhs=VS[:, j, 512:1024], start=True, stop=(blk == 0))
                        if blk > 0:
                            ins = e.matmul(ps[b][:], lhsT=cb[:, EB, :], rhs=prev, start=False, stop=True)
                        return ins
                    op("pe", mmdv, reads=[f"vst{tt % 2}", f"vst{(tt - 1) % 2}", "cb"], writes=[f"ps{b}"])
                    op("act", lambda e, j=j, b=b, DVS=DVS: e.activation(out=DVS[:, j, :], in_=ps[b][:], func=AF.Identity),
                       reads=[f"ps{b}"], writes=[f"dvst{tt % 2}"])
                op("sp", lambda e, tt=tt, DVS=DVS: e.dma_start(out=dv_d[tt * 512:(tt + 1) * 512, :].rearrange("(j p) d -> p j d", p=128), in_=DVS[:]),
                   reads=[f"dvst{tt % 2}"], writes=[f"dv_d{tt}"], dma=True)
                LF = lf[tt % 2]

                def mmf(e, U=U):
                    for j in range(4):
                        for kc in range(8):
                            ins = e.matmul(ps[6][:, j * 8:(j + 1) * 8], lhsT=U[:, kc, j * 128:(j + 1) * 128], rhs=winb[:, kc, 5120:5128],
                                           start=(kc == 0), stop=(kc == 7))
                    return ins
                op("pe", mmf, reads=WIN + ukeys, writes=["ps6"])
                op("dve", lambda e: e.tensor_tensor(out=fb[:], in0=ps[6][:, 0:32], in1=bft[:], op=ALU.add), reads=["ps6", "bft"], writes=["fb"])
                op("act", lambda e: e.activation(out=ef[:], in_=fb[:], func=AF.Exp, scale=-1.0), reads=["fb"], writes=["ef"])
                op("act", lambda e, LF=LF: e.activation(out=LF[:], in_=ef[:], func=AF.Ln, bias=1.0, scale=1.0), reads=["ef"], writes=[f"lf{tt % 2}"])

                def mmc(e, LF=LF):
                    for j in range(4):
                        ins = e.matmul(ps[6][:, 64 + j * 8: 64 + (j + 1) * 8], lhsT=cf[:, UINC, :], rhs=LF[:, j * 8:(j + 1) * 8], start=True, stop=(j == 0))
                        for jp in range(j):
                            ins = e.matmul(ps[6][:, 64 + j * 8: 64 + (j + 1) * 8], lhsT=cf[:, ONES, :], rhs=LF[:, jp * 8:(jp + 1) * 8],
                                           start=False, stop=(jp == j - 1))
                    for jp in range(4):
                        ins = e.matmul(ps[6][:, 96:104], lhsT=cf[:, ONES, :], rhs=LF[:, jp * 8:(jp + 1) * 8], start=(jp == 0), stop=(jp == 3))
                    return ins
                op("pe", mmc, reads=[f"lf{tt % 2}", "cf"], writes=["ps6"])
                for j in range(4):
                    op("dve", lambda e, j=j, tt=tt: e.tensor_tensor(out=ctok[:, tt * 4 + j, :], in0=ps[6][:, 64 + j * 8: 64 + (j + 1) * 8],
                                                                    in1=cql[:, tt, :], op=ALU.add),
                       reads=["ps6", "cql"], writes=["ctok"])
                op("dve", lambda e, tt=tt: e.tensor_tensor(out=cql[:, tt + 1, :], in0=ps[6][:, 96:104], in1=cql[:, tt, :], op=ALU.add),
                   reads=["ps6", "cql"], writes=["cql"])

                def mmd(e, LF=LF):
                    for j in range(4):
                        ins = e.matmul(ps[7][0:8, j * 128:(j + 1) * 128], lhsT=LF[:, j * 8:(j + 1) * 8], rhs=cf[:, LSTR, :], start=True, stop=(j == 3))
                        for jp in range(j + 1, 4):
                            ins = e.matmul(ps[7][0:8, j * 128:(j + 1) * 128], lhsT=LF[:, jp * 8:(jp + 1) * 8], rhs=cf[:, ONES, :],
                                           start=False, stop=(jp == 3))
                    return ins
                op("pe", mmd, reads=[f"lf{tt % 2}", "cf"], writes=["ps7"])
                CS = cst[tt % 2]
                ck = f"cst{tt % 2}"
                op("dve", lambda e: e.tensor_copy(out=d32[:], in_=ps[7][0:8, :]), reads=["ps7"], writes=["d32"])
                op("dve", lambda e, CS=CS: e.tensor_copy(out=CS[:, 0, :], in_=d32[:]), reads=["d32"], writes=[ck])
                op("dve", lambda e, CS=CS: e.tensor_tensor(out=r1[:], in0=d32[:], in1=CS[:, 0, :], op=ALU.subtract), reads=["d32", ck], writes=["r1"])
                op("dve", lambda e, CS=CS: e.tensor_copy(out=CS[:, 1, :], in_=r1[:]), reads=["r1"], writes=[ck])
                op("dve", lambda e, CS=CS: e.tensor_tensor(out=r2[:], in0=r1[:], in1=CS[:, 1, :], op=ALU.subtract), reads=["r1", ck], writes=["r2"])
                op("dve", lambda e, CS=CS: e.tensor_copy(out=CS[:, 2, :], in_=r2[:]), reads=["r2"], writes=[ck])
                op("sp", lambda e, CS=CS, tt=tt: e.dma_start(out=caug_d[:, :, tt * 512:(tt + 1) * 512], in_=CS[:]),
                   reads=[ck], writes=[f"caug_d{tt}"], dma=True)
            sy.barrier()

        with ExitStack() as p2:
            P = p2.enter_context
            KT = [P(nc.sbuf_tensor(f"KT{i}", [128, S], BF16)) for i in range(2)]
            QT = [P(nc.sbuf_tensor(f"QT{i}", [128, S], BF16)) for i in range(2)]
            VV = [P(nc.sbuf_tensor(f"VV{i}", [128, NB, 65], BF16)) for i in range(2)]
            bias_t = [P(nc.sbuf_tensor(f"bias{i}", [128, NT, NB], F32)) for i in range(2)]
            zeros = P(nc.sbuf_tensor("zeros", [128, 64], BF16))
            Pb = [P(nc.sbuf_tensor(f"Pb{i}", [128, 512], BF16)) for i in range(4)]
            dVV = [P(nc.sbuf_tensor(f"dVV{i}", [128, NB, 64], BF16)) for i in range(4)]
            VV2 = [P(nc.sbuf_tensor(f"VVb{i}", [128, NB, 64], BF16)) for i in range(2)]
            wbuf = [P(nc.sbuf_tensor(f"wbuf{i}", [128, 512], F32)) for i in range(6)]
            Gs = [P(nc.sbuf_tensor(f"Gs{i}", [128, 512], BF16)) for i in range(8)]
            GT = [P(nc.sbuf_tensor(f"GT{i}", [128, 512], BF16)) for i in range(6)]
            onesf = P(nc.sbuf_tensor("onesf", [128, 512], F32))
            rs = [P(nc.sbuf_tensor(f"rs{i}", [65, 512], F32)) for i in range(2)]
            ysb = [P(nc.sbuf_tensor(f"ysb{i}", [64, 512], F32)) for i in range(2)]
            yo = [P(nc.sbuf_tensor(f"yo{i}", [64, 512], BF16)) for i in range(2)]
            yo2 = [P(nc.sbuf_tensor(f"yo2_{i}", [128, 512], BF16)) for i in range(2)]

            op("dve", lambda e: e.memset(zeros[:], 0.0), writes=["zeros"])
            for i in range(2):
                op("dve", lambda e, i=i: e.memset(KT[i][64:67, :], 1.0), writes=[f"KTa{i}"])
                op("dve", lambda e, i=i: e.memset(VV[i][:, :, 64:65], 1.0), writes=[f"VVa{i}"])

            def fox_loads(h):
                sl = h % 2
                hp, hf = h // 2, h % 2
                op("sp", lambda e: e.dma_start(out=KT[sl][0:64, :], in_=fm_d[4 + hp, hf * 64:(hf + 1) * 64, :]), writes=[f"KT{sl}"], dma=True)
                op("sp", lambda e: e.dma_start(out=QT[sl][0:64, :], in_=fm_d[0 + hp, hf * 64:(hf + 1) * 64, :]), writes=[f"QT{sl}"], dma=True)
                op("sp", lambda e: e.dma_start(out=QT[sl][64:67, :], in_=caug_d[h]), writes=[f"QTa{sl}"], dma=True)
                for k0 in range(0, NB, 8):
                    op("sp", lambda e, k0=k0: e.dma_start(out=VV[sl][:, k0:k0 + 8, 0:64],
                                                          in_=v_d[0, k0 * 128:(k0 + 8) * 128, h * 64:(h + 1) * 64].rearrange("(kb p) d -> p kb d", p=128)),
                       writes=[f"VV{sl}"], dma=True)
                for qt in range(NT):
                    op("dve", lambda e, qt=qt: e.tensor_scalar(out=bias_t[sl][:, qt, :], in0=ctok[:, :, h], scalar1=cql[:, qt + 1, h:h + 1],
                                                               scalar2=None, op0=ALU.subtract),
                       reads=["ctok", "cql"], writes=[f"bias{sl}"])

            items = []
            for h in range(NH):
                for qt in range(NT):
                    n = 4 * qt + 4
                    for kb in range(n):
                        items.append(dict(h=h, qt=qt, kb=kb, first=(kb == 0), last=(kb == n - 1), hfirst=(qt == 0 and kb == 0)))
            for i, it in enumerate(items):
                it["i"] = i
                it["tile"] = it["h"] * NT + it["qt"]
                j = it["kb"] - 4 * it["qt"]
                it["j"] = j
                it["c0"] = 128 * j if j > 0 else 0
            nper = len(items) // NH
            assert nper >= 10
            fox_loads(0)

            def fx_z(it):
                i, h, qt, kb, c0 = it["i"], it["h"], it["qt"], it["kb"], it["c0"]
                sl = h % 2
                if it["hfirst"] and h + 1 < NH:
                    pass
                b = i % 3
                op("pe", lambda e: e.matmul(ps[b][:, c0:512], lhsT=KT[sl][0:67, kb * 128:(kb + 1) * 128],
                                            rhs=QT[sl][0:67, qt * 512 + c0:(qt + 1) * 512], start=True, stop=True),
                   reads=[f"KT{sl}", f"KTa{sl}", f"QT{sl}", f"QTa{sl}"], writes=[f"ps{b}"])

            def fx_p(it):
                i, h, qt, kb, c0 = it["i"], it["h"], it["qt"], it["kb"], it["c0"]
                sl = h % 2
                b = i % 3
                pb = i % 4
                op("act", lambda e: e.activation(out=Pb[pb][:, c0:512], in_=ps[b][:, c0:512], func=AF.Exp,
                                                 bias=bias_t[sl][:, qt, kb:kb + 1], scale=1.0),
                   reads=[f"ps{b}", f"bias{sl}"], writes=[f"Pb{pb}"])
                if it["j"] >= 0:
                    op("pool", lambda e: e.tensor_tensor(out=Pb[pb][:, c0:c0 + 128], in0=Pb[pb][:, c0:c0 + 128], in1=cb[:, UINC, :], op=ALU.mult),
                       reads=[f"Pb{pb}", "cb"], writes=[f"Pb{pb}"])

            def fx_pv(it):
                i, h, qt, kb, c0 = it["i"], it["h"], it["qt"], it["kb"], it["c0"]
                sl = h % 2
                pb = i % 4
                yb = 4 + it["tile"] % 2
                op("pe", lambda e: e.matmul(ps[yb][0:65, c0:512], lhsT=VV[sl][:, kb, 0:65], rhs=Pb[pb][:, c0:512],
                                            start=it["first"], stop=it["last"]),
                   reads=[f"VV{sl}", f"VVa{sl}", f"Pb{pb}"], writes=[f"ps{yb}"])

            def fx_fin1(it):
                if not it["last"]:
                    return
                t2 = it["tile"] % 2
                yb = 4 + t2
                op("dve", lambda e: e.reciprocal(out=rs[t2][64:65, :], in_=ps[yb][64:65, :]), reads=[f"ps{yb}"], writes=[f"rs{t2}"])
                op("act", lambda e: e.activation(out=ysb[t2][:], in_=ps[yb][0:64, :], func=AF.Identity), reads=[f"ps{yb}"], writes=[f"ysb{t2}"])

            def fx_fin2(it):
                if not it["last"]:
                    return
                h, qt = it["h"], it["qt"]
                t2 = it["tile"] % 2
                op("pe", lambda e: e.matmul(ps[6][0:64, :], lhsT=cf[64:65, ONES, 0:64], rhs=rs[t2][64:65, :], start=True, stop=True),
                   reads=[f"rs{t2}", "cf"], writes=["ps6"])
                op("dve", lambda e: e.tensor_tensor(out=yo[t2][:], in0=ysb[t2][:], in1=ps[6][0:64, :], op=ALU.mult),
                   reads=[f"ysb{t2}", "ps6"], writes=[f"yo{t2}"])
                op("sp", lambda e: e.dma_start(out=yT_d[0, h * 64:(h + 1) * 64, qt * 512:(qt + 1) * 512], in_=yo[t2][:]),
                   reads=[f"yo{t2}"], writes=[f"yT_d0_{h}_{qt}"], dma=True)

            stages = [(fx_z, 0), (fx_p, 1), (fx_pv, 2), (fx_fin1, 3), (fx_fin2, 9)]
            n = len(items)
            for s in range(n + 10):
                for fn, sk in stages:
                    i = s - sk
                    if 0 <= i < n:
                        fn(items[i])
                if s < n and (s % nper) == 6:
                    h = items[s]["h"]
                    if h + 1 < NH:
                        fox_loads(h + 1)
            sy.barrier()

            op("dve", lambda e: e.memset(onesf[:], 1.0), writes=["onesf"])

            def sb_loads(hp):
                sl = hp % 2
                op("sp", lambda e: e.dma_start(out=KT[sl][:, :], in_=fm_d[12 + hp]), writes=[f"KT{sl}"], dma=True)
                op("sp", lambda e: e.dma_start(out=QT[sl][:, :], in_=fm_d[8 + hp]), writes=[f"QT{sl}"], dma=True)

            VS = [VV[0], VV[1], VV2[0], VV2[1]]

            def sb_vload(h):
                sl = h % 4
                for k0 in range(0, NB, 8):
                    op("sp", lambda e, k0=k0: e.dma_start(out=VS[sl][:, k0:k0 + 8, 0:64],
                                                          in_=v_d[1, k0 * 128:(k0 + 8) * 128, h * 64:(h + 1) * 64].rearrange("(kb p) d -> p kb d", p=128)),
                       writes=[f"VS{sl}"], dma=True)
                    op("sp", lambda e, k0=k0: e.dma_start(out=dVV[sl][:, k0:k0 + 8, :],
                                                          in_=dv_d[k0 * 128:(k0 + 8) * 128, h * 64:(h + 1) * 64].rearrange("(kb p) d -> p kb d", p=128)),
                       writes=[f"dVV{sl}"], dma=True)

            items = []
            for hp in range(NH // 2):
                for qb in range(NB):
                    a, jq = qb // 4, qb % 4
                    for kt in range(a, -1, -1):
                        W = 128 * (jq + 1) if kt == a else 512
                        for hh in range(2):
                            items.append(dict(h=2 * hp + hh, hp=hp, hh=hh, qb=qb, a=a, jq=jq, kt=kt, W=W, first=(kt == a), last=(kt == 0)))
            for i, it in enumerate(items):
                it["i"] = i
            npair = len(items) // 2
            per_hp = npair // (NH // 2)
            assert per_hp >= 12
            sb_loads(0)
            sb_vload(0)
            sb_vload(1)

            def sb_z(it):
                i, hp, hh, qb, kt, W = it["i"], it["hp"], it["hh"], it["qb"], it["kt"], it["W"]
                sl = hp % 2
                p0_ = 64 * hh
                b = hh
                op("pe", lambda e: e.matmul(ps[b][:, 0:W], lhsT=QT[sl][p0_:p0_ + 64, qb * 128:(qb + 1) * 128],
                                            rhs=KT[sl][p0_:p0_ + 64, kt * 512: kt * 512 + W], start=True, stop=True),
                   reads=[f"KT{sl}", f"QT{sl}"], writes=[f"ps{b}"])

            def sb_w(it):
                i, W, jq, hh = it["i"], it["W"], it["jq"], it["hh"]
                b = hh
                wi = i % 6
                op("act", lambda e: e.activation(out=wbuf[wi][:, 0:W], in_=ps[b][:, 0:W], func=AF.Sigmoid, scale=-1.0),
                   reads=[f"ps{b}"], writes=[f"wbuf{wi}"])
                if it["first"]:
                    c0 = 128 * jq
                    op("pool", lambda e: e.tensor_tensor(out=wbuf[wi][:, c0:c0 + 128], in0=wbuf[wi][:, c0:c0 + 128], in1=cf[:, LSTR, :], op=ALU.mult),
                       reads=[f"wbuf{wi}", "cf"], writes=[f"wbuf{wi}"])
                    op("pool", lambda e: e.tensor_tensor(out=wbuf[wi][:, c0:c0 + 128], in0=wbuf[wi][:, c0:c0 + 128], in1=cf[:, UINC, :], op=ALU.add),
                       reads=[f"wbuf{wi}", "cf"], writes=[f"wbuf{wi}"])

            def rev(ap_, n):
                return bass.AP(ap_.tensor, ap_.offset + (n - 1), [[ap_.ap[0][0], 128], [-1, n]])

            def sb_scan(it):
                i, W, jq = it["i"], it["W"], it["jq"]
                wi = i % 6
                gi = i % 8
                if it["first"]:
                    init = 1.0
                    rd = [f"wbuf{wi}", "onesf"]
                else:
                    init = Gs[(i - 2) % 8][:, 0:1]
                    rd = [f"wbuf{wi}", "onesf", f"Gs{(i - 2) % 8}"]
                op("dve", lambda e: e.tensor_tensor_scan(out=rev(Gs[gi][:, 0:W], W), data0=rev(wbuf[wi][:, 0:W], W), data1=onesf[:, 0:W],
                                                         initial=init, op0=ALU.mult, op1=ALU.mult),
                   reads=rd, writes=[f"Gs{gi}"])
                if it["first"]:
                    c0 = 128 * jq
                    op("pool", lambda e: e.tensor_tensor(out=Gs[gi][:, c0:c0 + 128], in0=Gs[gi][:, c0:c0 + 128], in1=cb[:, LINC, :], op=ALU.mult),
                       reads=[f"Gs{gi}", "cb"], writes=[f"Gs{gi}"])

            def sb_tr(it):
                i, W = it["i"], it["W"]
                gi = i % 8
                ti = i % 4

                def tr(e):
                    for c in range(W // 128):
                        ins = e.matmul(ps[2 + ti][:, c * 128:(c + 1) * 128], lhsT=Gs[gi][:, c * 128:(c + 1) * 128], rhs=cb[:, IDENT, :],
                                       start=True, stop=True)
                    return ins
                op("pe", tr, reads=[f"Gs{gi}", "cb"], writes=[f"pst{ti}"])

            def sb_ev(it):
                i, W = it["i"], it["W"]
                ti = i % 4
                g6 = i % 6
                op("act", lambda e: e.activation(out=GT[g6][:, 0:W], in_=ps[2 + ti][:, 0:W], func=AF.Identity), reads=[f"pst{ti}"], writes=[f"GT{g6}"])

            def sb_pv(it):
                i, h, hp, hh, qb, kt, W, jq, a = it["i"], it["h"], it["hp"], it["hh"], it["qb"], it["kt"], it["W"], it["jq"], it["a"]
                sl = h % 4
                g6 = i % 6
                yb = 6 + (hp * NT + a) % 2
                ycol = slice(jq * 128, (jq + 1) * 128)
                yrow = slice(64 * hh, 64 * hh + 64)

                def mm(e):
                    if it["first"]:
                        e.matmul(ps[yb][yrow, ycol], lhsT=VS[sl][:, qb, 0:64], rhs=cb[:, IDENT, :], start=True, stop=False)
                    nsub = W // 128
                    for c in range(nsub):
                        ins = e.matmul(ps[yb][yrow, ycol], lhsT=dVV[sl][:, kt * 4 + c, :], rhs=GT[g6][:, c * 128:(c + 1) * 128],
                                       start=False, stop=(it["last"] and c == nsub - 1))
                    return ins
                op("pe", mm, reads=[f"VS{sl}", f"dVV{sl}", f"GT{g6}", "cb"], writes=[f"ps{yb}"])

            def sb_fin(it):
                if not (it["last"] and it["jq"] == 3 and it["hh"] == 1):
                    return
                hp, a = it["hp"], it["a"]
                t2 = (hp * NT + a) % 2
                yb = 6 + t2
                op("act", lambda e: e.activation(out=yo2[t2][:], in_=ps[yb][:, :], func=AF.Identity), reads=[f"ps{yb}"], writes=[f"yo2_{t2}"])
                op("sp", lambda e: e.dma_start(out=yT_d[1, hp * 128:(hp + 1) * 128, a * 512:(a + 1) * 512], in_=yo2[t2][:]),
                   reads=[f"yo2_{t2}"], writes=[f"yT_d1_{hp}_{a}"], dma=True)

            stages = [(sb_w, 1), (sb_scan, 2), (sb_tr, 3), (sb_ev, 4), (sb_pv, 5), (sb_fin, 6), (sb_z, 0)]
            for k in range(npair + 7):
                if k < npair and k % per_hp == 9 and k // per_hp + 1 < NH // 2:
                    sb_loads(k // per_hp + 1)
                    sb_vload(2 * (k // per_hp + 1))
                    sb_vload(2 * (k // per_hp + 1) + 1)
                for fn, sk in stages:
                    m = k - sk
                    if 0 <= m < npair:
                        fn(items[2 * m])
                        fn(items[2 * m + 1])
            sy.barrier()

        p012.close()

        def ln_tail(R, rk, g_t, b_t, st6, mv, sc2, g_eng="pool", b_eng="pool", sfx=""):
            ops = []
            for hh in range(2):
                ops.append(lambda hh=hh: op("dve", lambda e: e.bn_stats(out=st6[:, hh * 6:(hh + 1) * 6], in_=R[:, hh * 512:(hh + 1) * 512]),
                                            reads=[rk], writes=["st6" + sfx]))
            ops.append(lambda: op("dve", lambda e: e.bn_aggr(out=mv[:], in_=st6[:]), reads=["st6" + sfx], writes=["mv" + sfx]))
            ops.append(lambda: op("dve", lambda e: e.tensor_scalar_add(out=sc2[:, 0:1], in0=mv[:, 1:2], scalar1=EPS / (ALPHA * ALPHA)), reads=["mv" + sfx], writes=["sc2" + sfx]))
            ops.append(lambda: op("pool", lambda e: e.tensor_tensor(out=sc2[:, 0:1], in0=sc2[:, 0:1], in1=mhalf[:, 0:1], op=ALU.pow),
                                  reads=["sc2" + sfx, "mhalf"], writes=["sc2" + sfx]))
            ops.append(lambda: op("dve", lambda e: e.scalar_tensor_tensor(out=sc2[:, 1:2], in0=mv[:, 0:1], scalar=-1.0, in1=sc2[:, 0:1],
                                                                          op0=ALU.mult, op1=ALU.mult), reads=["mv" + sfx, "sc2" + sfx], writes=["sc2b" + sfx]))
            ops.append(lambda: op("act", lambda e: e.activation(out=R[:], in_=R[:], func=AF.Identity, bias=sc2[:, 1:2], scale=sc2[:, 0:1]),
                                  reads=[rk, "sc2" + sfx, "sc2b" + sfx], writes=[rk]))
            for q in range(4):
                qs = slice(q * 256, (q + 1) * 256)
                ops.append(lambda qs=qs: op(g_eng, lambda e: e.tensor_tensor(out=R[:, qs], in0=R[:, qs], in1=g_t[:, qs], op=ALU.mult),
                                            reads=[rk, "lnt"], writes=[rk]))
            for q in range(4):
                qs = slice(q * 256, (q + 1) * 256)
                ops.append(lambda qs=qs: op(b_eng, lambda e: e.tensor_tensor(out=R[:, qs], in0=R[:, qs], in1=b_t[:, qs], op=ALU.add),
                                            reads=[rk, "lnt"], writes=[rk]))
            return ops

        deferred = []

        def pop_deferred(n):
            for _ in range(n):
                if deferred:
                    deferred.pop(0)()

        with ExitStack() as p3:
            P = p3.enter_context
            wfb = P(nc.sbuf_tensor("wfb", [128, 4, D], BF16))
            wsbb = P(nc.sbuf_tensor("wsbb", [128, 4, D], BF16))
            wob = P(nc.sbuf_tensor("wob", [128, 8, D], BF16))
            lnt = P(nc.sbuf_tensor("lnt_sb", [128, 2, D], F32))
            gt1 = P(nc.sbuf_tensor("gtab1", [128, D], F32))
            yf = [P(nc.sbuf_tensor(f"yf{i}", [128, 4, 512], BF16)) for i in range(2)]
            ysl = [P(nc.sbuf_tensor(f"ysl{i}", [128, 4, 512], BF16)) for i in range(2)]
            gt = [P(nc.sbuf_tensor(f"gt{i}", [128, 16, 512], BF16)) for i in range(2)]
            xt = [P(nc.sbuf_tensor(f"xt3_{i}", [128, 4, D], F32)) for i in range(2)]
            m1 = [P(nc.sbuf_tensor(f"m1_{i}", [128, 512], F32)) for i in range(2)]
            m2 = [P(nc.sbuf_tensor(f"m2_{i}", [128, 512], F32)) for i in range(2)]
            mg = [P(nc.sbuf_tensor(f"mg{i}", [128, 8, 512], BF16)) for i in range(2)]
            Rb = [P(nc.sbuf_tensor(f"Rb{i}", [128, D], F32)) for i in range(4)]
            st6 = [P(nc.sbuf_tensor(f"st6_{i}", [128, 12], F32)) for i in range(4)]
            mv = [P(nc.sbuf_tensor(f"mv_{i}", [128, 2], F32)) for i in range(4)]
            sc2 = [P(nc.sbuf_tensor(f"sc2_{i}", [128, 2], F32)) for i in range(4)]
            for c in range(4):
                op("pool", lambda e, c=c: e.dma_start(out=wfb[:, c, :], in_=wfox_d[c * 128:(c + 1) * 128, :]), writes=["wfb"], dma=True)
                op("pool", lambda e, c=c: e.dma_start(out=wsbb[:, c, :], in_=wsb_d[c * 128:(c + 1) * 128, :]), writes=["wsbb"], dma=True)
            for c in range(8):
                op("pool", lambda e, c=c: e.dma_start(out=wob[:, c, :], in_=wo_d[c * 128:(c + 1) * 128, :]), writes=["wob"], dma=True)
            op("sp", lambda e: e.dma_start(out=lnt[:], in_=lnt_d[0:2].rearrange("c p n -> p c n")), writes=["lnt"], dma=True)
            op("sp", lambda e: e.dma_start(out=gt1[:], in_=gt_d[0]), writes=["gt1"], dma=True)

            def loadA(tt):
                sl = tt % 2
                tsl = slice(tt * 512, (tt + 1) * 512)
                op("sp", lambda e: e.dma_start(out=yf[sl][:], in_=yT_d[0, :, tsl].rearrange("(c p) t -> p c t", p=128)), writes=[f"yf{sl}"], dma=True)
                op("sp", lambda e: e.dma_start(out=ysl[sl][:], in_=yT_d[1, :, tsl].rearrange("(c p) t -> p c t", p=128)), writes=[f"ysl{sl}"], dma=True)
                op("sp", lambda e: e.dma_start(out=gt[sl][:], in_=fm_d[16:32, :, tsl].rearrange("c p t -> p c t")), writes=[f"gt{sl}"], dma=True)

            def loadX(tt):
                sl = tt % 2
                tsl = slice(tt * 512, (tt + 1) * 512)
                op("sp", lambda e: e.dma_start(out=xt[sl][:], in_=x_d[tsl, :].rearrange("(j p) d -> p j d", p=128)), writes=[f"xt3_{sl}"], dma=True)

            c3 = {"cnt": 0}

            def OC(tt):
                sl = tt % 2
                MG = mg[sl]
                for oc in range(8):
                    cnt = c3["cnt"]
                    ba, bb = (2 * cnt) % 4, (2 * cnt + 1) % 4
                    mi = cnt % 2
                    c3["cnt"] += 1

                    def mmab(e, oc=oc, ba=ba, bb=bb, sl=sl):
                        for c in range(4):
                            e.matmul(ps[ba][:], lhsT=wfb[:, c, oc * 128:(oc + 1) * 128], rhs=yf[sl][:, c, :], start=(c == 0), stop=(c == 3))
                        for c in range(4):
                            ins = e.matmul(ps[bb][:], lhsT=wsbb[:, c, oc * 128:(oc + 1) * 128], rhs=ysl[sl][:, c, :], start=(c == 0), stop=(c == 3))
                        return ins
                    op("pe", mmab, reads=["wfb", "wsbb", f"yf{sl}", f"ysl{sl}"], writes=[f"ps{ba}", f"ps{bb}"])
                    op("dve", lambda e, oc=oc, ba=ba, mi=mi, sl=sl: e.tensor_tensor(out=m1[mi][:], in0=ps[ba][:], in1=gt[sl][:, oc, :], op=ALU.mult),
                       reads=[f"ps{ba}", f"gt{sl}"], writes=[f"m1_{mi}"])
                    op("dve", lambda e, oc=oc, bb=bb, mi=mi, sl=sl: e.tensor_tensor(out=m2[mi][:], in0=ps[bb][:], in1=gt[sl][:, 8 + oc, :], op=ALU.mult),
                       reads=[f"ps{bb}", f"gt{sl}"], writes=[f"m2_{mi}"])
                    op("pool", lambda e, oc=oc, mi=mi, MG=MG: e.tensor_tensor(out=MG[:, oc, :], in0=m1[mi][:], in1=m2[mi][:], op=ALU.add),
                       reads=[f"m1_{mi}", f"m2_{mi}"], writes=[f"mg{sl}"])
                    pop_deferred(9)

            def MO(tt):
                sl = tt % 2
                MG = mg[sl]
                chains = []
                for j in range(4):
                    blk = tt * 4 + j
                    R = Rb[j]
                    rk = f"Rb{j}"
                    for hh in range(2):
                        b = 4 + (2 * blk + hh) % 4

                        def mmo(e, j=j, hh=hh, b=b, MG=MG):
                            for oc in range(8):
                                ins = e.matmul(ps[b][:], lhsT=MG[:, oc, j * 128:(j + 1) * 128], rhs=wob[:, oc, hh * 512:(hh + 1) * 512],
                                               start=(oc == 0), stop=(oc == 7))
                            return ins
                        op("pe", mmo, reads=[f"mg{sl}", "wob"], writes=[f"ps{b}"])
                        op("dve", lambda e, hh=hh, b=b, R=R: e.tensor_tensor(out=R[:, hh * 512:(hh + 1) * 512], in0=ps[b][:],
                                                                              in1=gt1[:, hh * 512:(hh + 1) * 512], op=ALU.mult),
                           reads=[f"ps{b}", "gt1"], writes=[rk])
                    op("pool", lambda e, j=j, R=R, sl=sl: e.tensor_tensor(out=R[:], in0=R[:], in1=xt[sl][:, j, :], op=ALU.add),
                       reads=[f"xt3_{sl}", rk], writes=[rk])
                    ch = ln_tail(R, rk, lnt[:, 0, :], lnt[:, 1, :], st6[j], mv[j], sc2[j], g_eng="dve", sfx=f"_{j}")
                    ch.append(lambda blk=blk, R=R, rk=rk: op("sp", lambda e: e.dma_start(out=x1_d[blk * 128:(blk + 1) * 128, :], in_=R[:]),
                                                             reads=[rk], writes=[f"x1_d{blk}"], dma=True))
                    chains.append(ch)
                for k in range(max(len(c) for c in chains)):
                    for c in chains:
                        if k < len(c):
                            deferred.append(c[k])

            loadA(0)
            loadX(0)
            if NT > 1:
                loadA(1)
            OC(0)
            for tt in range(NT):
                if tt + 2 < NT:
                    loadA(tt + 2)
                if tt + 1 < NT:
                    loadX(tt + 1)
                    OC(tt + 1)
                else:
                    pop_deferred(len(deferred))
                MO(tt)
            pop_deferred(len(deferred))
            sy.barrier()

        TW = 256
        NT2 = S // TW
        JB = TW // 128
        with ExitStack() as p4:
            P = p4.enter_context
            wupb = P(nc.sbuf_tensor("wupb", [128, 8, 2 * DFF], BF16))
            wdb = P(nc.sbuf_tensor("wdb", [128, NAC, D], BF16))
            lnt = P(nc.sbuf_tensor("lnt2", [128, 2, D], F32))
            gt2 = P(nc.sbuf_tensor("gtab2", [128, D], F32))
            convp = P(nc.sbuf_tensor("convp_sb", [128, NFC, 4], F32))
            halo = [P(nc.sbuf_tensor(f"halo{i}", [128, NFC, 2], F32)) for i in range(2)]
            xt = [P(nc.sbuf_tensor(f"xt4_{i}", [128, JB, D], F32)) for i in range(2)]
            uT = [P(nc.sbuf_tensor(f"u2T{i}", [128, 8, TW], BF16)) for i in range(2)]
            hb = [P(nc.sbuf_tensor(f"hb{i}", [128, TW + 2], F32)) for i in range(2)]
            cv = [P(nc.sbuf_tensor(f"cv{i}", [128, TW], F32)) for i in range(3)]
            sg = [P(nc.sbuf_tensor(f"sg{i}", [128, TW], F32)) for i in range(2)]
            act = [P(nc.sbuf_tensor(f"act{i}", [128, NAC, TW], BF16)) for i in range(2)]
            Rb = [P(nc.sbuf_tensor(f"Rb4_{i}", [128, D], F32)) for i in range(2)]
            st6 = P(nc.sbuf_tensor("st6b", [128, 12], F32))
            mv = P(nc.sbuf_tensor("mvb", [128, 2], F32))
            sc2 = P(nc.sbuf_tensor("sc2b_", [128, 2], F32))
            for c in range(8):
                op("pool", lambda e, c=c: e.dma_start(out=wupb[:, c, :], in_=wup_d[c * 128:(c + 1) * 128, :]), writes=["wupb"], dma=True)
            for c in range(NAC):
                op("pool", lambda e, c=c: e.dma_start(out=wdb[:, c, :], in_=wdown_d[c * 128:(c + 1) * 128, :]), writes=["wdb"], dma=True)
            op("sp", lambda e: e.dma_start(out=lnt[:], in_=lnt_d[2:4].rearrange("c p n -> p c n")), writes=["lnt"], dma=True)
            op("sp", lambda e: e.dma_start(out=gt2[:], in_=gt_d[1]), writes=["gt2"], dma=True)
            op("sp", lambda e: e.dma_start(out=convp[:], in_=convp_d), writes=["convp"], dma=True)
            op("dve", lambda e: e.memset(halo[1][:], 0.0), writes=["halo1"])

            def loads4(tt):
                sl = tt % 2
                op("sp", lambda e: e.dma_start(out=xt[sl][:], in_=x1_d[tt * TW:(tt + 1) * TW, :].rearrange("(j p) d -> p j d", p=128)),
                   writes=[f"xt4_{sl}"], dma=True)
            cnts = {"pc": 0, "hc": 0}

            def T4(tt):
                sl = tt % 2
                X, U = xt[sl], uT[sl]
                for kc in range(8):
                    b = cnts["pc"] % 4
                    cnts["pc"] += 1

                    def tr(e, kc=kc, b=b, X=X):
                        for j in range(JB):
                            ins = e.transpose(out=ps[b][:, j * 128:(j + 1) * 128], in_=X[:, j, kc * 128:(kc + 1) * 128], identity=identf[:])
                        return ins
                    op("pe", tr, reads=[f"xt4_{sl}", "identf"], writes=[f"ps{b}"])
                    op("act", lambda e, kc=kc, b=b, U=U: e.activation(out=U[:, kc, :], in_=ps[b][:, 0:TW], func=AF.Identity,
                                                                      bias=modp[:, 2, kc:kc + 1], scale=modp[:, 3, kc:kc + 1]),
                       reads=[f"ps{b}", "modp"], writes=[f"u2T{sl}"])

            def U4(tt):
                sl = tt % 2
                U, A = uT[sl], act[sl]
                uk = f"u2T{sl}"
                base = cnts["hc"]
                cnts["hc"] += 2 * NAC

                def stA(c):
                    i, half = c // 2, c % 2
                    fc = i + half * NAC
                    b = cnts["pc"] % 4
                    cnts["pc"] += 1
                    hi = (base + c) % 2
                    ci = (base + c) % 3
                    H, CV = hb[hi], cv[ci]

                    def mmu(e):
                        for kc in range(8):
                            ins = e.matmul(ps[b][:, 0:TW], lhsT=wupb[:, kc, fc * 128:(fc + 1) * 128], rhs=U[:, kc, :], start=(kc == 0), stop=(kc == 7))
                        return ins
                    op("pe", mmu, reads=["wupb", uk], writes=[f"ps{b}"])
                    op("act", lambda e: e.activation(out=H[:, 2:TW + 2], in_=ps[b][:, 0:TW], func=AF.Identity), reads=[f"ps{b}"], writes=[f"hb{hi}"])
                    op("act", lambda e: e.activation(out=CV[:], in_=ps[b][:, 0:TW], func=AF.Identity, bias=convp[:, fc, 3:4], scale=convp[:, fc, 2:3]),
                       reads=[f"ps{b}", "convp"], writes=[f"cv{ci}"])
                    op("act", lambda e: e.activation(out=halo[sl][:, fc, :], in_=ps[b][:, TW - 2:TW], func=AF.Identity),
                       reads=[f"ps{b}"], writes=[f"halo{sl}"])
                    op("pool", lambda e: e.tensor_copy(out=H[:, 0:2], in_=halo[1 - sl][:, fc, :]), reads=[f"halo{1 - sl}"], writes=[f"hbh{hi}"])

                def stB(c):
                    i, half = c // 2, c % 2
                    fc = i + half * NAC
                    hi = (base + c) % 2
                    ci = (base + c) % 3
                    H, CV = hb[hi], cv[ci]
                    op("dve", lambda e: e.scalar_tensor_tensor(out=CV[:], in0=H[:, 1:TW + 1], scalar=convp[:, fc, 1:2], in1=CV[:],
                                                               op0=ALU.mult, op1=ALU.add),
                       reads=[f"hb{hi}", f"hbh{hi}", "convp", f"cv{ci}"], writes=[f"cv{ci}"])
                    op("dve", lambda e: e.scalar_tensor_tensor(out=CV[:], in0=H[:, 0:TW], scalar=convp[:, fc, 0:1], in1=CV[:],
                                                               op0=ALU.mult, op1=ALU.add),
                       reads=[f"hb{hi}", f"hbh{hi}", "convp", f"cv{ci}"], writes=[f"cv{ci}"])

                def stC(c):
                    i, half = c // 2, c % 2
                    ci = (base + c) % 3
                    CV = cv[ci]
                    SG = sg[i % 2]
                    if half == 0:
                        op("act", lambda e: e.activation(out=SG[:], in_=CV[:], func=AF.Silu), reads=[f"cv{ci}"], writes=[f"sg{i % 2}"])
                    else:
                        op("pool", lambda e: e.tensor_tensor(out=A[:, i, :], in0=SG[:], in1=CV[:], op=ALU.mult),
                           reads=[f"cv{ci}", f"sg{i % 2}"], writes=[f"act{sl}"])

                nchunk = 2 * NAC
                for st_ in range(nchunk + 2):
                    if st_ < nchunk:
                        stA(st_)
                    if 0 <= st_ - 1 < nchunk:
                        stB(st_ - 1)
                    if 0 <= st_ - 2 < nchunk:
                        stC(st_ - 2)
                    pop_deferred(1)

            def D4(tt):
                sl = tt % 2
                X, A = xt[sl], act[sl]
                for j in range(JB):
                    blk = tt * JB + j
                    R = Rb[blk % 2]
                    rk = f"Rb{blk % 2}"
                    for hh in range(2):
                        b = 4 + (2 * blk + hh) % 4

                        def mmd2(e, j=j, hh=hh, b=b, A=A):
                            for i in range(NAC):
                                ins = e.matmul(ps[b][:], lhsT=A[:, i, j * 128:(j + 1) * 128], rhs=wdb[:, i, hh * 512:(hh + 1) * 512],
                                               start=(i == 0), stop=(i == NAC - 1))
                            return ins
                        op("pe", mmd2, reads=[f"act{sl}", "wdb"], writes=[f"ps{b}"])
                        op("dve", lambda e, hh=hh, b=b, R=R: e.tensor_tensor(out=R[:, hh * 512:(hh + 1) * 512], in0=ps[b][:],
                                                                              in1=gt2[:, hh * 512:(hh + 1) * 512], op=ALU.mult),
                           reads=[f"ps{b}", "gt2"], writes=[rk])
                    op("pool", lambda e, j=j, R=R, X=X: e.tensor_tensor(out=R[:], in0=R[:], in1=X[:, j, :], op=ALU.add),
                       reads=[f"xt4_{sl}", rk], writes=[rk])
                    deferred.extend(ln_tail(R, rk, lnt[:, 0, :], lnt[:, 1, :], st6, mv, sc2, g_eng="dve", b_eng="dve"))
                    deferred.append(lambda blk=blk, R=R, rk=rk: op("sp", lambda e: e.dma_start(out=out_d[blk * 128:(blk + 1) * 128, :], in_=R[:]),
                                                                   reads=[rk], writes=[f"out_d{blk}"], dma=True))

            loads4(0)
            T4(0)
            for tt in range(NT2):
                if tt + 1 < NT2:
                    loads4(tt + 1)
                U4(tt)
                if tt + 1 < NT2:
                    T4(tt + 1)
                D4(tt)
            pop_deferred(len(deferred))
            sy.barrier()
    return nc


def _consts():
    r = np.arange(128)[:, None]
    c = np.arange(128)[None, :]
    dm = (r == c - 1).astype(np.float32) - (r == c).astype(np.float32)
    eb = ((r == 127) & (c == 0)).astype(np.float32)
    six = np.stack([(r == c), (r <= c), (r >= c), (r > c), (r < c), np.ones((128, 128), bool)]).astype(np.float32)
    return np.concatenate([six, dm[None], eb[None]], axis=0)


def prep_shared(w_ada, b_ada, w_in, b_forget, w_fox_proj, w_sb_proj, w_o, ln1_g, ln1_b, w_up, conv_w, conv_b, w_down, ln2_g, ln2_b):
    f = lambda a: np.ascontiguousarray(np.asarray(a, dtype=np.float32))
    w_in = f(w_in)
    w_in_r = np.concatenate([w_in[:, 0:512], w_in[:, 512:1024], w_in[:, 1544:2056], w_in[:, 2056:2568], w_in[:, 3080:4104],
                             w_in[:, 4104:5128], w_in[:, 1024:1536], w_in[:, 2568:3080], w_in[:, 1536:1544]], axis=1)
    convp = np.stack([f(conv_w)[0], f(conv_w)[1], f(conv_w)[2], f(conv_b)], axis=1).reshape(NFC, 128, 4).transpose(1, 0, 2)
    lnt = np.stack([np.broadcast_to(f(v)[None, :], (128, D)) for v in (ln1_g, ln1_b, ln2_g, ln2_b)])
    return {
        "w_ada": f(w_ada), "b_ada": f(b_ada).reshape(12, 1, 512), "w_in": f(w_in_r),
        "bft": f(np.broadcast_to(np.tile(f(b_forget), 4)[None, :], (128, 32))),
        "w_fox": f(w_fox_proj), "w_sb": f(w_sb_proj), "w_o": f(w_o), "lnt": f(lnt), "w_up": f(w_up),
        "convp": f(convp), "w_down": f(w_down), "consts": _consts(),
    }


def kernel(x, c, w_ada, b_ada, w_in, b_forget, w_fox_proj, w_sb_proj, w_o, ln1_g, ln1_b, w_up, conv_w, conv_b, w_down, ln2_g, ln2_b):
    x = np.asarray(x, dtype=np.float32)
    c = np.asarray(c, dtype=np.float32)
    B, S, _ = x.shape
    shared = prep_shared(w_ada, b_ada, w_in, b_forget, w_fox_proj, w_sb_proj, w_o, ln1_g, ln1_b, w_up, conv_w, conv_b, w_down, ln2_g, ln2_b)
    nc = build_nc(S)
    in_maps = []
    for b in range(B):
        m = dict(shared)
        m["x"] = np.ascontiguousarray(x[b])
        m["cT"] = np.ascontiguousarray(c[b].reshape(8, 128).T)
        in_maps.append(m)
    res = run_bass_kernel_spmd(nc, in_maps, core_ids=list(range(B)))
    return np.stack([r["out"] for r in res.results], axis=0).astype(np.float32)
```

```python
import numpy as np
from contextlib import ExitStack
import concourse.bass as bass
import concourse.mybir as mybir
from concourse.bass_utils import run_bass_kernel_spmd

F32 = mybir.dt.float32
BF16 = mybir.dt.bfloat16
AF = mybir.ActivationFunctionType
ALU = mybir.AluOpType
AX = mybir.AxisListType

D = 1024
DFF = 2816
NH = 8
DH = 64
NFC = 2 * DFF // 128
NAC = DFF // 128
ALPHA = 2.0 ** 0.25
EPS = 1e-5


class Sync:
    def __init__(self, nc, st):
        self.nc = nc
        self.st = st
        self.nsem = 0
        self.engs = {"sp": nc.sync, "act": nc.scalar, "dve": nc.vector, "pool": nc.gpsimd, "pe": nc.tensor}
        self.cs = {e: [self._newsem(), 0] for e in self.engs}
        self.waited = {e: {} for e in self.engs}
        self.lastw = {}
        self.readers = {}
        self.NDS = 8
        self.dsem = {e: [self._newsem() for _ in range(self.NDS)] for e in ("sp", "pool")}
        self.dcnt = {e: [0] * self.NDS for e in ("sp", "pool")}
        self.dn = {e: 0 for e in ("sp", "pool")}

    def _newsem(self):
        self.nsem += 1
        return self.st.enter_context(self.nc.semaphore(f"sy{self.nsem}"))

    def _wait(self, eng, sem, val):
        w = self.waited[eng]
        if w.get(sem, 0) >= val:
            return
        self.engs[eng].wait_ge(sem, val)
        w[sem] = val

    def op(self, eng, fn, reads=(), writes=(), dma=False):
        deps = []
        for k in reads:
            lw = self.lastw.get(k)
            if lw is not None:
                if lw[2] != eng or lw[3] or eng in ("act", "dve", "pool"):
                    deps.append(lw)
        for k in writes:
            lw = self.lastw.get(k)
            if lw is not None and (lw[2] != eng or lw[3] or eng in ("act", "dve", "pool")):
                deps.append(lw)
            for r in self.readers.get(k, {}).values():
                if r[2] != eng or r[3] or eng in ("act", "dve", "pool"):
                    deps.append(r)
        for d in deps:
            self._wait(eng, d[0], d[1])
        e = self.engs[eng]
        if dma:
            i = self.dn[eng] % self.NDS
            self.dn[eng] += 1
            sem = self.dsem[eng][i]
            if self.dcnt[eng][i] > 0:
                self._wait(eng, sem, self.dcnt[eng][i])
            ins = fn(e)
            self.dcnt[eng][i] += 16
            ins.then_inc(sem, 16)
            ref = (sem, self.dcnt[eng][i], eng, True)
            rkey = ("dma", eng, i)
        else:
            c = self.cs[eng]
            if c[1] >= 30000:
                c[0] = self._newsem()
                c[1] = 0
            ins = fn(e)
            c[1] += 1
            ins.then_inc(c[0], 1)
            ref = (c[0], c[1], eng, False)
            rkey = eng
        for k in reads:
            self.readers.setdefault(k, {})[rkey] = ref
        for k in writes:
            self.lastw[k] = ref
            self.readers[k] = {}
        return ref

    def barrier(self):
        refs = [(c[0], c[1]) for c in self.cs.values() if c[1] > 0]
        for e in self.dsem:
            for i in range(self.NDS):
                if self.dcnt[e][i] > 0:
                    refs.append((self.dsem[e][i], self.dcnt[e][i]))
        for eng in self.engs:
            for (s, v) in refs:
                if s is self.cs[eng][0]:
                    continue
                self._wait(eng, s, v)
        self.lastw.clear()
        self.readers.clear()


def build_nc(S, dbg=False):
    assert S % 512 == 0
    NT = S // 512
    NB = S // 128
    nc = bass.Bass("TRN2", target_bir_lowering=False)
    okind = "ExternalOutput" if dbg else "Internal"

    def din(name, shape, dt=F32):
        return nc.dram_tensor(name, list(shape), dt, kind="ExternalInput").ap()

    x_d = din("x", [S, D])
    cT_d = din("cT", [128, 8])
    wada_d = din("w_ada", [D, 6 * D])
    bada_d = din("b_ada", [12, 1, 512])
    win_d = din("w_in", [D, 5128])
    bft_d = din("bft", [128, 32])
    wfox_d = din("w_fox", [512, D])
    wsb_d = din("w_sb", [512, D])
    wo_d = din("w_o", [D, D])
    lnt_d = din("lnt", [4, 128, D])
    wup_d = din("w_up", [D, 2 * DFF])
    convp_d = din("convp", [128, NFC, 4])
    wdown_d = din("w_down", [DFF, D])
    consts_d = din("consts", [8, 128, 128])
    out_d = nc.dram_tensor("out", [S, D], F32, kind="ExternalOutput").ap()

    fm_d = nc.dram_tensor("fm_s", [32, 128, S], BF16, kind=okind).ap()
    v_d = nc.dram_tensor("v_s", [2, S, 512], BF16, kind=okind).ap()
    caug_d = nc.dram_tensor("caug_s", [8, 3, S], BF16, kind=okind).ap()
    yT_d = nc.dram_tensor("yT_s", [2, 512, S], BF16, kind=okind).ap()
    x1_d = nc.dram_tensor("x1_s", [S, D], F32, kind=okind).ap()
    dv_d = nc.dram_tensor("dv_s", [S, 512], BF16, kind=okind).ap()
    gt_d = nc.dram_tensor("gt_s", [2, 128, D], F32).ap()

    with ExitStack() as st:
        E = st.enter_context
        sy = Sync(nc, st)
        op = sy.op
        ps = [E(nc.psum_tensor(f"psb{i}", [128, 512], F32)) for i in range(8)]

        identf = E(nc.sbuf_tensor("identf", [128, 128], F32))
        mhalf = E(nc.sbuf_tensor("mhalf", [128, 1], F32))
        modp = E(nc.sbuf_tensor("modp", [128, 4, 8], F32))
        p012 = ExitStack()
        E2 = p012.enter_context
        cf = E2(nc.sbuf_tensor("cf", [128, 8, 128], F32))
        cb = E2(nc.sbuf_tensor("cb", [128, 8, 128], BF16))
        ctok = E2(nc.sbuf_tensor("ctok", [128, NB, 8], F32))
        cql = E2(nc.sbuf_tensor("cql", [128, NT + 1, 8], F32))
        IDENT, UINC, LINC, LSTR, USTR, ONES, DM, EB = range(8)

        op("sp", lambda e: e.dma_start(out=identf[:], in_=consts_d[0]), writes=["identf"], dma=True)
        op("dve", lambda e: e.memset(mhalf[:], -0.5), writes=["mhalf"])
        op("sp", lambda e: e.dma_start(out=cf[:], in_=consts_d.rearrange("c p n -> p c n")), writes=["cf"], dma=True)
        op("pool", lambda e: e.dma_start(out=cb[:], in_=consts_d.rearrange("c p n -> p c n")), writes=["cb"], dma=True)

        with ExitStack() as p0:
            P = p0.enter_context
            cts = P(nc.sbuf_tensor("cts", [128, 8], F32))
            modtab = P(nc.sbuf_tensor("modtab", [128, 6 * D], F32))
            crep = P(nc.sbuf_tensor("crep", [128, 8, 128], F32))
            wa = [P(nc.sbuf_tensor(f"wa{i}", [128, 8, 512], F32)) for i in range(2)]
            bad = [P(nc.sbuf_tensor(f"bad{i}", [1, 512], F32)) for i in range(2)]
            dtmp = P(nc.sbuf_tensor("dtmp", [128, 8, 128], F32))
            op("sp", lambda e: e.dma_start(out=cts[:], in_=cT_d), writes=["cts"], dma=True)
            for kc in range(8):
                op("dve", lambda e, kc=kc: e.tensor_copy(out=crep[:, kc, :], in_=cts[:, kc:kc + 1].to_broadcast([128, 128])),
                   reads=["cts"], writes=[f"crep{kc}"])
            for g in range(12):
                sl = g % 2
                op("sp", lambda e, g=g, sl=sl: e.dma_start(
                    out=wa[sl][:], in_=wada_d[:, g * 512:(g + 1) * 512].rearrange("(kc p) n -> p kc n", p=128)),
                   writes=[f"wa{sl}"], dma=True)
                op("sp", lambda e, g=g, sl=sl: e.dma_start(out=bad[sl][:], in_=bada_d[g]), writes=[f"bad{sl}"], dma=True)
                bank = ps[g % 2]

                def mm(e, g=g, sl=sl, bank=bank):
                    for kc in range(8):
                        e.matmul(bank[:], lhsT=crep[:, kc, :], rhs=wa[sl][:, kc, :], start=(kc == 0), stop=False)
                    return e.matmul(bank[:], lhsT=cf[0:1, ONES, :], rhs=bad[sl][0:1, :], start=False, stop=True)
                op("pe", mm, reads=[f"wa{sl}", f"bad{sl}", "cf"] + [f"crep{k}" for k in range(8)], writes=[f"ps{g % 2}"])
                addone = g in (2, 3, 8, 9)
                if g in (4, 5, 10, 11):
                    op("dve", lambda e, g=g, bank=bank: e.tensor_scalar(out=modtab[:, g * 512:(g + 1) * 512], in0=bank[:], scalar1=1.0,
                                                                        scalar2=1.0 / ALPHA, op0=ALU.add, op1=ALU.mult),
                       reads=[f"ps{g % 2}"], writes=[f"modtab{g}"])
                elif addone:
                    op("dve", lambda e, g=g, bank=bank: e.tensor_scalar_add(out=modtab[:, g * 512:(g + 1) * 512], in0=bank[:], scalar1=1.0),
                       reads=[f"ps{g % 2}"], writes=[f"modtab{g}"])
                else:
                    op("act", lambda e, g=g, bank=bank: e.activation(out=modtab[:, g * 512:(g + 1) * 512], in_=bank[:], func=AF.Identity),
                       reads=[f"ps{g % 2}"], writes=[f"modtab{g}"])
            for vi, base in enumerate((0, 1024, 3072, 4096)):
                for k in range(8):
                    op("dve", lambda e, base=base, k=k: e.tensor_tensor(
                        out=dtmp[:, k, :], in0=modtab[:, base + k * 128: base + (k + 1) * 128], in1=cf[:, IDENT, :], op=ALU.mult),
                       reads=[f"modtab{(base + k * 128) // 512}", "cf"], writes=["dtmp"])
                op("dve", lambda e, vi=vi: e.tensor_reduce(out=modp[:, vi, :], in_=dtmp[:], axis=AX.X, op=ALU.add),
                   reads=["dtmp"], writes=["modp"])
            op("sp", lambda e: e.dma_start(out=gt_d[0], in_=modtab[:, 2048:3072]), reads=["modtab4", "modtab5"], writes=["gt_d0"], dma=True)
            op("sp", lambda e: e.dma_start(out=gt_d[1], in_=modtab[:, 5120:6144]), reads=["modtab10", "modtab11"], writes=["gt_d1"], dma=True)
            sy.barrier()

        with ExitStack() as p1:
            P = p1.enter_context
            winb = P(nc.sbuf_tensor("winb", [128, 8, 5128], BF16))
            xt = [P(nc.sbuf_tensor(f"xt{i}", [128, 4, D], F32)) for i in range(2)]
            uT = [P(nc.sbuf_tensor(f"uT{i}", [128, 8, 512], BF16)) for i in range(2)]
            fst = [P(nc.sbuf_tensor(f"fst{i}", [128, 4, 512], BF16)) for i in range(3)]
            vst = [P(nc.sbuf_tensor(f"vst{i}", [128, 4, 1024], BF16)) for i in range(2)]
            dvst = [P(nc.sbuf_tensor(f"dvst{i}", [128, 4, 512], BF16)) for i in range(2)]
            bft = P(nc.sbuf_tensor("bft_sb", [128, 32], F32))
            fb = P(nc.sbuf_tensor("fb", [128, 32], F32))
            ef = P(nc.sbuf_tensor("ef", [128, 32], F32))
            lf = [P(nc.sbuf_tensor(f"lf{i}", [128, 32], F32)) for i in range(2)]
            d32 = P(nc.sbuf_tensor("d32", [8, 512], F32))
            r1 = P(nc.sbuf_tensor("r1", [8, 512], F32))
            r2 = P(nc.sbuf_tensor("r2", [8, 512], F32))
            cst = [P(nc.sbuf_tensor(f"cst{i}", [8, 3, 512], BF16)) for i in range(2)]

            for kc in range(8):
                op("pool", lambda e, kc=kc: e.dma_start(out=winb[:, kc, :], in_=win_d[kc * 128:(kc + 1) * 128, :]),
                   writes=[f"winb{kc}"], dma=True)
            WIN = [f"winb{k}" for k in range(8)]
            op("sp", lambda e: e.dma_start(out=bft[:], in_=bft_d), writes=["bft"], dma=True)
            op("dve", lambda e: e.memset(cql[:, 0, :], 0.0), writes=["cql"])

            def load_x(tt):
                op("sp", lambda e: e.dma_start(out=xt[tt % 2][:], in_=x_d[tt * 512:(tt + 1) * 512, :].rearrange("(j p) d -> p j d", p=128)),
                   writes=[f"xt{tt % 2}"], dma=True)
            load_x(0)
            pcnt = [0]

            def nbank():
                pcnt[0] += 1
                return pcnt[0] % 6
            fcnt = [0]
            for tt in range(NT):
                if tt + 1 < NT:
                    load_x(tt + 1)
                X = xt[tt % 2]
                U = uT[tt % 2]
                ukeys = [f"uT{tt % 2}_{k}" for k in range(8)]
                for kc in range(8):
                    b = nbank()

                    def tr(e, kc=kc, b=b, X=X):
                        for j in range(4):
                            ins = e.transpose(out=ps[b][:, j * 128:(j + 1) * 128], in_=X[:, j, kc * 128:(kc + 1) * 128], identity=cf[:, IDENT, :])
                        return ins
                    op("pe", tr, reads=[f"xt{tt % 2}", "cf"], writes=[f"ps{b}"])
                    if kc % 2 == 0:
                        op("act", lambda e, kc=kc, b=b, U=U: e.activation(out=U[:, kc, :], in_=ps[b][:], func=AF.Identity,
                                                                          bias=modp[:, 0, kc:kc + 1], scale=modp[:, 1, kc:kc + 1]),
                           reads=[f"ps{b}", "modp"], writes=[ukeys[kc]])
                    else:
                        op("dve", lambda e, kc=kc, b=b, U=U: e.tensor_scalar(out=U[:, kc, :], in0=ps[b][:], scalar1=modp[:, 1, kc:kc + 1],
                                                                             scalar2=modp[:, 0, kc:kc + 1], op0=ALU.mult, op1=ALU.add),
                           reads=[f"ps{b}", "modp"], writes=[ukeys[kc]])
                for oc in range(32):
                    b = nbank()

                    def mm(e, oc=oc, b=b, U=U):
                        for kc in range(8):
                            ins = e.matmul(ps[b][:], lhsT=winb[:, kc, oc * 128:(oc + 1) * 128], rhs=U[:, kc, :], start=(kc == 0), stop=(kc == 7))
                        return ins
                    op("pe", mm, reads=WIN + ukeys, writes=[f"ps{b}"])
                    fs = fcnt[0] % 3
                    dst = fst[fs][:, oc % 4, :]
                    if oc < 16 and (oc // 4) % 2 == 0:
                        op("dve", lambda e, b=b, dst=dst: e.tensor_scalar_mul(out=dst, in0=ps[b][:], scalar1=0.125),
                           reads=[f"ps{b}"], writes=[f"fst{fs}"])
                    elif oc < 16:
                        op("dve", lambda e, b=b, dst=dst: e.tensor_copy(out=dst, in_=ps[b][:]), reads=[f"ps{b}"], writes=[f"fst{fs}"])
                    else:
                        op("act", lambda e, b=b, dst=dst: e.activation(out=dst, in_=ps[b][:], func=AF.Sigmoid),
                           reads=[f"ps{b}"], writes=[f"fst{fs}"])
                    if oc % 4 == 3:
                        oc0 = oc - 3
                        op("sp", lambda e, oc0=oc0, fs=fs, tt=tt: e.dma_start(
                            out=fm_d[oc0:oc0 + 4, :, tt * 512:(tt + 1) * 512].rearrange("c p t -> p c t"), in_=fst[fs][:]),
                           reads=[f"fst{fs}"], writes=[f"fm_d{oc0}_{tt}"], dma=True)
                        fcnt[0] += 1
                VS = vst[tt % 2]
                for j in range(4):
                    for ab in range(2):
                        b = nbank()

                        def mmv(e, j=j, ab=ab, b=b, U=U):
                            for kc in range(8):
                                ins = e.matmul(ps[b][:], lhsT=U[:, kc, j * 128:(j + 1) * 128],
                                               rhs=winb[:, kc, 4096 + ab * 512: 4096 + (ab + 1) * 512], start=(kc == 0), stop=(kc == 7))
                            return ins
                        op("pe", mmv, reads=WIN + ukeys, writes=[f"ps{b}"])
                        dst = VS[:, j, ab * 512:(ab + 1) * 512]
                        if ab == 0:
                            op("act", lambda e, b=b, dst=dst: e.activation(out=dst, in_=ps[b][:], func=AF.Identity),
                               reads=[f"ps{b}"], writes=[f"vst{tt % 2}"])
                        else:
                            op("dve", lambda e, b=b, dst=dst: e.tensor_copy(out=dst, in_=ps[b][:]), reads=[f"ps{b}"], writes=[f"vst{tt % 2}"])
                for ab in range(2):
                    op("sp", lambda e, ab=ab, tt=tt, VS=VS: e.dma_start(
                        out=v_d[ab, tt * 512:(tt + 1) * 512, :].rearrange("(j p) d -> p j d", p=128), in_=VS[:, :, ab * 512:(ab + 1) * 512]),
                       reads=[f"vst{tt % 2}"], writes=[f"v_d{ab}_{tt}"], dma=True)
                DVS = dvst[tt % 2]
                for j in range(4):
                    b = nbank()
                    blk = tt * 4 + j
                    prev = VS[:, j - 1, 512:1024] if j > 0 else vst[(tt - 1) % 2][:, 3, 512:1024]

                    def mmdv(e, j=j, b=b, blk=blk, prev=prev, VS=VS):
                        ins = e.matmul(ps[b][:], lhsT=cb[:, DM, :], rhs=VS[:, j, 512:1024], start=True, stop=(blk == 0))
                        if blk > 0:
                            ins = e.matmul(ps[b][:], lhsT=cb[:, EB, :], rhs=prev, start=False, stop=True)
                        return ins
                    op("pe", mmdv, reads=[f"vst{tt % 2}", f"vst{(tt - 1) % 2}", "cb"], writes=[f"ps{b}"])
                    op("act", lambda e, j=j, b=b, DVS=DVS: e.activation(out=DVS[:, j, :], in_=ps[b][:], func=AF.Identity),
                       reads=[f"ps{b}"], writes=[f"dvst{tt % 2}"])
                op("sp", lambda e, tt=tt, DVS=DVS: e.dma_start(out=dv_d[tt * 512:(tt + 1) * 512, :].rearrange("(j p) d -> p j d", p=128), in_=DVS[:]),
                   reads=[f"dvst{tt % 2}"], writes=[f"dv_d{tt}"], dma=True)
                LF = lf[tt % 2]

                def mmf(e, U=U):
                    for j in range(4):
                        for kc in range(8):
                            ins = e.matmul(ps[6][:, j * 8:(j + 1) * 8], lhsT=U[:, kc, j * 128:(j + 1) * 128], rhs=winb[:, kc, 5120:5128],
                                           start=(kc == 0), stop=(kc == 7))
                    return ins
                op("pe", mmf, reads=WIN + ukeys, writes=["ps6"])
                op("dve", lambda e: e.tensor_tensor(out=fb[:], in0=ps[6][:, 0:32], in1=bft[:], op=ALU.add), reads=["ps6", "bft"], writes=["fb"])
                op("act", lambda e: e.activation(out=ef[:], in_=fb[:], func=AF.Exp, scale=-1.0), reads=["fb"], writes=["ef"])
                op("act", lambda e, LF=LF: e.activation(out=LF[:], in_=ef[:], func=AF.Ln, bias=1.0, scale=1.0), reads=["ef"], writes=[f"lf{tt % 2}"])

                def mmc(e, LF=LF):
                    for j in range(4):
                        ins = e.matmul(ps[6][:, 64 + j * 8: 64 + (j + 1) * 8], lhsT=cf[:, UINC, :], rhs=LF[:, j * 8:(j + 1) * 8], start=True, stop=(j == 0))
                        for jp in range(j):
                            ins = e.matmul(ps[6][:, 64 + j * 8: 64 + (j + 1) * 8], lhsT=cf[:, ONES, :], rhs=LF[:, jp * 8:(jp + 1) * 8],
                                           start=False, stop=(jp == j - 1))
                    for jp in range(4):
                        ins = e.matmul(ps[6][:, 96:104], lhsT=cf[:, ONES, :], rhs=LF[:, jp * 8:(jp + 1) * 8], start=(jp == 0), stop=(jp == 3))
                    return ins
                op("pe", mmc, reads=[f"lf{tt % 2}", "cf"], writes=["ps6"])
                for j in range(4):
                    op("dve", lambda e, j=j, tt=tt: e.tensor_tensor(out=ctok[:, tt * 4 + j, :], in0=ps[6][:, 64 + j * 8: 64 + (j + 1) * 8],
                                                                    in1=cql[:, tt, :], op=ALU.add),
                       reads=["ps6", "cql"], writes=["ctok"])
                op("dve", lambda e, tt=tt: e.tensor_tensor(out=cql[:, tt + 1, :], in0=ps[6][:, 96:104], in1=cql[:, tt, :], op=ALU.add),
                   reads=["ps6", "cql"], writes=["cql"])

                def mmd(e, LF=LF):
                    for j in range(4):
                        ins = e.matmul(ps[7][0:8, j * 128:(j + 1) * 128], lhsT=LF[:, j * 8:(j + 1) * 8], rhs=cf[:, LSTR, :], start=True, stop=(j == 3))
                        for jp in range(j + 1, 4):
                            ins = e.matmul(ps[7][0:8, j * 128:(j + 1) * 128], lhsT=LF[:, jp * 8:(jp + 1) * 8], rhs=cf[:, ONES, :],
                                           start=False, stop=(jp == 3))
                    return ins
                op("pe", mmd, reads=[f"lf{tt % 2}", "cf"], writes=["ps7"])
                CS = cst[tt % 2]
                ck = f"cst{tt % 2}"
                op("dve", lambda e: e.tensor_copy(out=d32[:], in_=ps[7][0:8, :]), reads=["ps7"], writes=["d32"])
                op("dve", lambda e, CS=CS: e.tensor_copy(out=CS[:, 0, :], in_=d32[:]), reads=["d32"], writes=[ck])
                op("dve", lambda e, CS=CS: e.tensor_tensor(out=r1[:], in0=d32[:], in1=CS[:, 0, :], op=ALU.subtract), reads=["d32", ck], writes=["r1"])
                op("dve", lambda e, CS=CS: e.tensor_copy(out=CS[:, 1, :], in_=r1[:]), reads=["r1"], writes=[ck])
                op("dve", lambda e, CS=CS: e.tensor_tensor(out=r2[:], in0=r1[:], in1=CS[:, 1, :], op=ALU.subtract), reads=["r1", ck], writes=["r2"])
                op("dve", lambda e, CS=CS: e.tensor_copy(out=CS[:, 2, :], in_=r2[:]), reads=["r2"], writes=[ck])
                op("sp", lambda e, CS=CS, tt=tt: e.dma_start(out=caug_d[:, :, tt * 512:(tt + 1) * 512], in_=CS[:]),
                   reads=[ck], writes=[f"caug_d{tt}"], dma=True)
            sy.barrier()

        with ExitStack() as p2:
            P = p2.enter_context
            KT = [P(nc.sbuf_tensor(f"KT{i}", [128, S], BF16)) for i in range(2)]
            QT = [P(nc.sbuf_tensor(f"QT{i}", [128, S], BF16)) for i in range(2)]
            VV = [P(nc.sbuf_tensor(f"VV{i}", [128, NB, 65], BF16)) for i in range(2)]
            bias_t = [P(nc.sbuf_tensor(f"bias{i}", [128, NT, NB], F32)) for i in range(2)]
            zeros = P(nc.sbuf_tensor("zeros", [128, 64], BF16))
            Pb = [P(nc.sbuf_tensor(f"Pb{i}", [128, 512], BF16)) for i in range(4)]
            dVV = [P(nc.sbuf_tensor(f"dVV{i}", [128, NB, 64], BF16)) for i in range(4)]
            VV2 = [P(nc.sbuf_tensor(f"VVb{i}", [128, NB, 64], BF16)) for i in range(2)]
            wbuf = [P(nc.sbuf_tensor(f"wbuf{i}", [128, 512], F32)) for i in range(6)]
            Gs = [P(nc.sbuf_tensor(f"Gs{i}", [128, 512], BF16)) for i in range(8)]
            GT = [P(nc.sbuf_tensor(f"GT{i}", [128, 512], BF16)) for i in range(6)]
            onesf = P(nc.sbuf_tensor("onesf", [128, 512], F32))
            rs = [P(nc.sbuf_tensor(f"rs{i}", [65, 512], F32)) for i in range(2)]
            ysb = [P(nc.sbuf_tensor(f"ysb{i}", [64, 512], F32)) for i in range(2)]
            yo = [P(nc.sbuf_tensor(f"yo{i}", [64, 512], BF16)) for i in range(2)]
            yo2 = [P(nc.sbuf_tensor(f"yo2_{i}", [128, 512], BF16)) for i in range(2)]

            op("dve", lambda e: e.memset(zeros[:], 0.0), writes=["zeros"])
            for i in range(2):
                op("dve", lambda e, i=i: e.memset(KT[i][64:67, :], 1.0), writes=[f"KTa{i}"])
                op("dve", lambda e, i=i: e.memset(VV[i][:, :, 64:65], 1.0), writes=[f"VVa{i}"])

            def fox_loads(h):
                sl = h % 2
                hp, hf = h // 2, h % 2
                op("sp", lambda e: e.dma_start(out=KT[sl][0:64, :], in_=fm_d[4 + hp, hf * 64:(hf + 1) * 64, :]), writes=[f"KT{sl}"], dma=True)
                op("sp", lambda e: e.dma_start(out=QT[sl][0:64, :], in_=fm_d[0 + hp, hf * 64:(hf + 1) * 64, :]), writes=[f"QT{sl}"], dma=True)
                op("sp", lambda e: e.dma_start(out=QT[sl][64:67, :], in_=caug_d[h]), writes=[f"QTa{sl}"], dma=True)
                for k0 in range(0, NB, 8):
                    op("sp", lambda e, k0=k0: e.dma_start(out=VV[sl][:, k0:k0 + 8, 0:64],
                                                          in_=v_d[0, k0 * 128:(k0 + 8) * 128, h * 64:(h + 1) * 64].rearrange("(kb p) d -> p kb d", p=128)),
                       writes=[f"VV{sl}"], dma=True)
                for qt in range(NT):
                    op("dve", lambda e, qt=qt: e.tensor_scalar(out=bias_t[sl][:, qt, :], in0=ctok[:, :, h], scalar1=cql[:, qt + 1, h:h + 1],
                                                               scalar2=None, op0=ALU.subtract),
                       reads=["ctok", "cql"], writes=[f"bias{sl}"])

            items = []
            for h in range(NH):
                for qt in range(NT):
                    n = 4 * qt + 4
                    for kb in range(n):
                        items.append(dict(h=h, qt=qt, kb=kb, first=(kb == 0), last=(kb == n - 1), hfirst=(qt == 0 and kb == 0)))
            for i, it in enumerate(items):
                it["i"] = i
                it["tile"] = it["h"] * NT + it["qt"]
                j = it["kb"] - 4 * it["qt"]
                it["j"] = j
                it["c0"] = 128 * j if j > 0 else 0
            nper = len(items) // NH
            assert nper >= 10
            fox_loads(0)

            def fx_z(it):
                i, h, qt, kb, c0 = it["i"], it["h"], it["qt"], it["kb"], it["c0"]
                sl = h % 2
                if it["hfirst"] and h + 1 < NH:
                    pass
                b = i % 3
                op("pe", lambda e: e.matmul(ps[b][:, c0:512], lhsT=KT[sl][0:67, kb * 128:(kb + 1) * 128],
                                            rhs=QT[sl][0:67, qt * 512 + c0:(qt + 1) * 512], start=True, stop=True),
                   reads=[f"KT{sl}", f"KTa{sl}", f"QT{sl}", f"QTa{sl}"], writes=[f"ps{b}"])

            def fx_p(it):
                i, h, qt, kb, c0 = it["i"], it["h"], it["qt"], it["kb"], it["c0"]
                sl = h % 2
                b = i % 3
                pb = i % 4
                op("act", lambda e: e.activation(out=Pb[pb][:, c0:512], in_=ps[b][:, c0:512], func=AF.Exp,
                                                 bias=bias_t[sl][:, qt, kb:kb + 1], scale=1.0),
                   reads=[f"ps{b}", f"bias{sl}"], writes=[f"Pb{pb}"])
                if it["j"] >= 0:
                    op("pool", lambda e: e.tensor_tensor(out=Pb[pb][:, c0:c0 + 128], in0=Pb[pb][:, c0:c0 + 128], in1=cb[:, UINC, :], op=ALU.mult),
                       reads=[f"Pb{pb}", "cb"], writes=[f"Pb{pb}"])

            def fx_pv(it):
                i, h, qt, kb, c0 = it["i"], it["h"], it["qt"], it["kb"], it["c0"]
                sl = h % 2
                pb = i % 4
                yb = 4 + it["tile"] % 2
                op("pe", lambda e: e.matmul(ps[yb][0:65, c0:512], lhsT=VV[sl][:, kb, 0:65], rhs=Pb[pb][:, c0:512],
                                            start=it["first"], stop=it["last"]),
                   reads=[f"VV{sl}", f"VVa{sl}", f"Pb{pb}"], writes=[f"ps{yb}"])

            def fx_fin1(it):
                if not it["last"]:
                    return
                t2 = it["tile"] % 2
                yb = 4 + t2
                op("dve", lambda e: e.reciprocal(out=rs[t2][64:65, :], in_=ps[yb][64:65, :]), reads=[f"ps{yb}"], writes=[f"rs{t2}"])
                op("dve", lambda e: e.tensor_copy(out=ysb[t2][:], in_=ps[yb][0:64, :]), reads=[f"ps{yb}"], writes=[f"ysb{t2}"])

            def fx_fin2(it):
                if not it["last"]:
                    return
                h, qt = it["h"], it["qt"]
                t2 = it["tile"] % 2
                op("pe", lambda e: e.matmul(ps[6][0:64, :], lhsT=cf[64:65, ONES, 0:64], rhs=rs[t2][64:65, :], start=True, stop=True),
                   reads=[f"rs{t2}", "cf"], writes=["ps6"])
                op("dve", lambda e: e.tensor_tensor(out=yo[t2][:], in0=ysb[t2][:], in1=ps[6][0:64, :], op=ALU.mult),
                   reads=[f"ysb{t2}", "ps6"], writes=[f"yo{t2}"])
                op("sp", lambda e: e.dma_start(out=yT_d[0, h * 64:(h + 1) * 64, qt * 512:(qt + 1) * 512], in_=yo[t2][:]),
                   reads=[f"yo{t2}"], writes=[f"yT_d0_{h}_{qt}"], dma=True)

            stages = [(fx_z, 0), (fx_p, 1), (fx_pv, 2), (fx_fin1, 3), (fx_fin2, 9)]
            n = len(items)
            for s in range(n + 10):
                for fn, sk in stages:
                    i = s - sk
                    if 0 <= i < n:
                        fn(items[i])
                if s < n and (s % nper) == 6:
                    h = items[s]["h"]
                    if h + 1 < NH:
                        fox_loads(h + 1)
            sy.barrier()

            op("dve", lambda e: e.memset(onesf[:], 1.0), writes=["onesf"])

            def sb_loads(hp):
                sl = hp % 2
                op("sp", lambda e: e.dma_start(out=KT[sl][:, :], in_=fm_d[12 + hp]), writes=[f"KT{sl}"], dma=True)
                op("sp", lambda e: e.dma_start(out=QT[sl][:, :], in_=fm_d[8 + hp]), writes=[f"QT{sl}"], dma=True)

            VS = [VV[0], VV[1], VV2[0], VV2[1]]

            def sb_vload(h):
                sl = h % 4
                for k0 in range(0, NB, 8):
                    op("sp", lambda e, k0=k0: e.dma_start(out=VS[sl][:, k0:k0 + 8, 0:64],
                                                          in_=v_d[1, k0 * 128:(k0 + 8) * 128, h * 64:(h + 1) * 64].rearrange("(kb p) d -> p kb d", p=128)),
                       writes=[f"VS{sl}"], dma=True)
                    op("sp", lambda e, k0=k0: e.dma_start(out=dVV[sl][:, k0:k0 + 8, :],
                                                          in_=dv_d[k0 * 128:(k0 + 8) * 128, h * 64:(h + 1) * 64].rearrange("(kb p) d -> p kb d", p=128)),
                       writes=[f"dVV{sl}"], dma=True)

            items = []
            for hp in range(NH // 2):
                for qb in range(NB):
                    a, jq = qb // 4, qb % 4
                    for kt in range(a, -1, -1):
                        W = 128 * (jq + 1) if kt == a else 512
                        for hh in range(2):
                            items.append(dict(h=2 * hp + hh, hp=hp, hh=hh, qb=qb, a=a, jq=jq, kt=kt, W=W, first=(kt == a), last=(kt == 0)))
            for i, it in enumerate(items):
                it["i"] = i
            npair = len(items) // 2
            per_hp = npair // (NH // 2)
            assert per_hp >= 12
            sb_loads(0)
            sb_vload(0)
            sb_vload(1)

            def sb_z(it):
                i, hp, hh, qb, kt, W = it["i"], it["hp"], it["hh"], it["qb"], it["kt"], it["W"]
                sl = hp % 2
                p0_ = 64 * hh
                b = hh
                op("pe", lambda e: e.matmul(ps[b][:, 0:W], lhsT=QT[sl][p0_:p0_ + 64, qb * 128:(qb + 1) * 128],
                                            rhs=KT[sl][p0_:p0_ + 64, kt * 512: kt * 512 + W], start=True, stop=True),
                   reads=[f"KT{sl}", f"QT{sl}"], writes=[f"ps{b}"])

            def sb_w(it):
                i, W, jq, hh = it["i"], it["W"], it["jq"], it["hh"]
                b = hh
                wi = i % 6
                op("act", lambda e: e.activation(out=wbuf[wi][:, 0:W], in_=ps[b][:, 0:W], func=AF.Sigmoid, scale=-1.0),
                   reads=[f"ps{b}"], writes=[f"wbuf{wi}"])
                if it["first"]:
                    c0 = 128 * jq
                    op("pool", lambda e: e.tensor_tensor(out=wbuf[wi][:, c0:c0 + 128], in0=wbuf[wi][:, c0:c0 + 128], in1=cf[:, LSTR, :], op=ALU.mult),
                       reads=[f"wbuf{wi}", "cf"], writes=[f"wbuf{wi}"])
                    op("pool", lambda e: e.tensor_tensor(out=wbuf[wi][:, c0:c0 + 128], in0=wbuf[wi][:, c0:c0 + 128], in1=cf[:, UINC, :], op=ALU.add),
                       reads=[f"wbuf{wi}", "cf"], writes=[f"wbuf{wi}"])

            def rev(ap_, n):
                return bass.AP(ap_.tensor, ap_.offset + (n - 1), [[ap_.ap[0][0], 128], [-1, n]])

            def sb_scan(it):
                i, W, jq = it["i"], it["W"], it["jq"]
                wi = i % 6
                gi = i % 8
                if it["first"]:
                    init = 1.0
                    rd = [f"wbuf{wi}", "onesf"]
                else:
                    init = Gs[(i - 2) % 8][:, 0:1]
                    rd = [f"wbuf{wi}", "onesf", f"Gs{(i - 2) % 8}"]
                op("dve", lambda e: e.tensor_tensor_scan(out=rev(Gs[gi][:, 0:W], W), data0=rev(wbuf[wi][:, 0:W], W), data1=onesf[:, 0:W],
                                                         initial=init, op0=ALU.mult, op1=ALU.mult),
                   reads=rd, writes=[f"Gs{gi}"])
                if it["first"]:
                    c0 = 128 * jq
                    op("pool", lambda e: e.tensor_tensor(out=Gs[gi][:, c0:c0 + 128], in0=Gs[gi][:, c0:c0 + 128], in1=cb[:, LINC, :], op=ALU.mult),
                       reads=[f"Gs{gi}", "cb"], writes=[f"Gs{gi}"])

            def sb_tr(it):
                i, W = it["i"], it["W"]
                gi = i % 8
                ti = i % 4

                def tr(e):
                    for c in range(W // 128):
                        ins = e.matmul(ps[2 + ti][:, c * 128:(c + 1) * 128], lhsT=Gs[gi][:, c * 128:(c + 1) * 128], rhs=cb[:, IDENT, :],
                                       start=True, stop=True)
                    return ins
                op("pe", tr, reads=[f"Gs{gi}", "cb"], writes=[f"pst{ti}"])

            def sb_ev(it):
                i, W = it["i"], it["W"]
                ti = i % 4
                g6 = i % 6
                op("act", lambda e: e.activation(out=GT[g6][:, 0:W], in_=ps[2 + ti][:, 0:W], func=AF.Identity), reads=[f"pst{ti}"], writes=[f"GT{g6}"])

            def sb_pv(it):
                i, h, hp, hh, qb, kt, W, jq, a = it["i"], it["h"], it["hp"], it["hh"], it["qb"], it["kt"], it["W"], it["jq"], it["a"]
                sl = h % 4
                g6 = i % 6
                yb = 6 + (hp * NT + a) % 2
                ycol = slice(jq * 128, (jq + 1) * 128)
                yrow = slice(64 * hh, 64 * hh + 64)

                def mm(e):
                    if it["first"]:
                        e.matmul(ps[yb][yrow, ycol], lhsT=VS[sl][:, qb, 0:64], rhs=cb[:, IDENT, :], start=True, stop=False)
                    nsub = W // 128
                    for c in range(nsub):
                        ins = e.matmul(ps[yb][yrow, ycol], lhsT=dVV[sl][:, kt * 4 + c, :], rhs=GT[g6][:, c * 128:(c + 1) * 128],
                                       start=False, stop=(it["last"] and c == nsub - 1))
                    return ins
                op("pe", mm, reads=[f"VS{sl}", f"dVV{sl}", f"GT{g6}", "cb"], writes=[f"ps{yb}"])

            def sb_fin(it):
                if not (it["last"] and it["jq"] == 3 and it["hh"] == 1):
                    return
                hp, a = it["hp"], it["a"]
                t2 = (hp * NT + a) % 2
                yb = 6 + t2
                op("act", lambda e: e.activation(out=yo2[t2][:], in_=ps[yb][:, :], func=AF.Identity), reads=[f"ps{yb}"], writes=[f"yo2_{t2}"])
                op("sp", lambda e: e.dma_start(out=yT_d[1, hp * 128:(hp + 1) * 128, a * 512:(a + 1) * 512], in_=yo2[t2][:]),
                   reads=[f"yo2_{t2}"], writes=[f"yT_d1_{hp}_{a}"], dma=True)

            stages = [(sb_w, 1), (sb_scan, 2), (sb_tr, 3), (sb_ev, 4), (sb_pv, 5), (sb_fin, 6), (sb_z, 0)]
            for k in range(npair + 7):
                if k < npair and k % per_hp == 9 and k // per_hp + 1 < NH // 2:
                    sb_loads(k // per_hp + 1)
                    sb_vload(2 * (k // per_hp + 1))
                    sb_vload(2 * (k // per_hp + 1) + 1)
                for fn, sk in stages:
                    m = k - sk
                    if 0 <= m < npair:
                        fn(items[2 * m])
                        fn(items[2 * m + 1])
            sy.barrier()

        p012.close()

        def ln_tail(R, rk, g_t, b_t, st6, mv, sc2, g_eng="pool", b_eng="pool", sfx=""):
            ops = []
            for hh in range(2):
                ops.append(lambda hh=hh: op("dve", lambda e: e.bn_stats(out=st6[:, hh * 6:(hh + 1) * 6], in_=R[:, hh * 512:(hh + 1) * 512]),
                                            reads=[rk], writes=["st6" + sfx]))
            ops.append(lambda: op("dve", lambda e: e.bn_aggr(out=mv[:], in_=st6[:]), reads=["st6" + sfx], writes=["mv" + sfx]))
            ops.append(lambda: op("dve", lambda e: e.tensor_scalar_add(out=sc2[:, 0:1], in0=mv[:, 1:2], scalar1=EPS / (ALPHA * ALPHA)), reads=["mv" + sfx], writes=["sc2" + sfx]))
            ops.append(lambda: op("pool", lambda e: e.tensor_tensor(out=sc2[:, 0:1], in0=sc2[:, 0:1], in1=mhalf[:, 0:1], op=ALU.pow),
                                  reads=["sc2" + sfx, "mhalf"], writes=["sc2" + sfx]))
            ops.append(lambda: op("dve", lambda e: e.scalar_tensor_tensor(out=sc2[:, 1:2], in0=mv[:, 0:1], scalar=-1.0, in1=sc2[:, 0:1],
                                                                          op0=ALU.mult, op1=ALU.mult), reads=["mv" + sfx, "sc2" + sfx], writes=["sc2b" + sfx]))
            ops.append(lambda: op("act", lambda e: e.activation(out=R[:], in_=R[:], func=AF.Identity, bias=sc2[:, 1:2], scale=sc2[:, 0:1]),
                                  reads=[rk, "sc2" + sfx, "sc2b" + sfx], writes=[rk]))
            for q in range(4):
                qs = slice(q * 256, (q + 1) * 256)
                ops.append(lambda qs=qs: op(g_eng, lambda e: e.tensor_tensor(out=R[:, qs], in0=R[:, qs], in1=g_t[:, qs], op=ALU.mult),
                                            reads=[rk, "lnt"], writes=[rk]))
            for q in range(4):
                qs = slice(q * 256, (q + 1) * 256)
                ops.append(lambda qs=qs: op(b_eng, lambda e: e.tensor_tensor(out=R[:, qs], in0=R[:, qs], in1=b_t[:, qs], op=ALU.add),
                                            reads=[rk, "lnt"], writes=[rk]))
            return ops

        deferred = []

        def pop_deferred(n):
            for _ in range(n):
                if deferred:
                    deferred.pop(0)()

        with ExitStack() as p3:
            P = p3.enter_context
            wfb = P(nc.sbuf_tensor("wfb", [128, 4, D], BF16))
            wsbb = P(nc.sbuf_tensor("wsbb", [128, 4, D], BF16))
            wob = P(nc.sbuf_tensor("wob", [128, 8, D], BF16))
            lnt = P(nc.sbuf_tensor("lnt_sb", [128, 2, D], F32))
            gt1 = P(nc.sbuf_tensor("gtab1", [128, D], F32))
            yf = [P(nc.sbuf_tensor(f"yf{i}", [128, 4, 512], BF16)) for i in range(2)]
            ysl = [P(nc.sbuf_tensor(f"ysl{i}", [128, 4, 512], BF16)) for i in range(2)]
            gt = [P(nc.sbuf_tensor(f"gt{i}", [128, 16, 512], BF16)) for i in range(2)]
            xt = [P(nc.sbuf_tensor(f"xt3_{i}", [128, 4, D], F32)) for i in range(2)]
            m1 = [P(nc.sbuf_tensor(f"m1_{i}", [128, 512], F32)) for i in range(2)]
            m2 = [P(nc.sbuf_tensor(f"m2_{i}", [128, 512], F32)) for i in range(2)]
            mg = [P(nc.sbuf_tensor(f"mg{i}", [128, 8, 512], BF16)) for i in range(2)]
            Rb = [P(nc.sbuf_tensor(f"Rb{i}", [128, D], F32)) for i in range(4)]
            st6 = [P(nc.sbuf_tensor(f"st6_{i}", [128, 12], F32)) for i in range(4)]
            mv = [P(nc.sbuf_tensor(f"mv_{i}", [128, 2], F32)) for i in range(4)]
            sc2 = [P(nc.sbuf_tensor(f"sc2_{i}", [128, 2], F32)) for i in range(4)]
            for c in range(4):
                op("pool", lambda e, c=c: e.dma_start(out=wfb[:, c, :], in_=wfox_d[c * 128:(c + 1) * 128, :]), writes=["wfb"], dma=True)
                op("pool", lambda e, c=c: e.dma_start(out=wsbb[:, c, :], in_=wsb_d[c * 128:(c + 1) * 128, :]), writes=["wsbb"], dma=True)
            for c in range(8):
                op("pool", lambda e, c=c: e.dma_start(out=wob[:, c, :], in_=wo_d[c * 128:(c + 1) * 128, :]), writes=["wob"], dma=True)
            op("sp", lambda e: e.dma_start(out=lnt[:], in_=lnt_d[0:2].rearrange("c p n -> p c n")), writes=["lnt"], dma=True)
            op("sp", lambda e: e.dma_start(out=gt1[:], in_=gt_d[0]), writes=["gt1"], dma=True)

            def loadA(tt):
                sl = tt % 2
                tsl = slice(tt * 512, (tt + 1) * 512)
                op("sp", lambda e: e.dma_start(out=yf[sl][:], in_=yT_d[0, :, tsl].rearrange("(c p) t -> p c t", p=128)), writes=[f"yf{sl}"], dma=True)
                op("sp", lambda e: e.dma_start(out=ysl[sl][:], in_=yT_d[1, :, tsl].rearrange("(c p) t -> p c t", p=128)), writes=[f"ysl{sl}"], dma=True)
                op("sp", lambda e: e.dma_start(out=gt[sl][:], in_=fm_d[16:32, :, tsl].rearrange("c p t -> p c t")), writes=[f"gt{sl}"], dma=True)

            def loadX(tt):
                sl = tt % 2
                tsl = slice(tt * 512, (tt + 1) * 512)
                op("sp", lambda e: e.dma_start(out=xt[sl][:], in_=x_d[tsl, :].rearrange("(j p) d -> p j d", p=128)), writes=[f"xt3_{sl}"], dma=True)

            c3 = {"cnt": 0}

            def OC(tt):
                sl = tt % 2
                MG = mg[sl]
                for oc in range(8):
                    cnt = c3["cnt"]
                    ba, bb = (2 * cnt) % 4, (2 * cnt + 1) % 4
                    mi = cnt % 2
                    c3["cnt"] += 1

                    def mmab(e, oc=oc, ba=ba, bb=bb, sl=sl):
                        for c in range(4):
                            e.matmul(ps[ba][:], lhsT=wfb[:, c, oc * 128:(oc + 1) * 128], rhs=yf[sl][:, c, :], start=(c == 0), stop=(c == 3))
                        for c in range(4):
                            ins = e.matmul(ps[bb][:], lhsT=wsbb[:, c, oc * 128:(oc + 1) * 128], rhs=ysl[sl][:, c, :], start=(c == 0), stop=(c == 3))
                        return ins
                    op("pe", mmab, reads=["wfb", "wsbb", f"yf{sl}", f"ysl{sl}"], writes=[f"ps{ba}", f"ps{bb}"])
                    op("dve", lambda e, oc=oc, ba=ba, mi=mi, sl=sl: e.tensor_tensor(out=m1[mi][:], in0=ps[ba][:], in1=gt[sl][:, oc, :], op=ALU.mult),
                       reads=[f"ps{ba}", f"gt{sl}"], writes=[f"m1_{mi}"])
                    op("dve", lambda e, oc=oc, bb=bb, mi=mi, sl=sl: e.tensor_tensor(out=m2[mi][:], in0=ps[bb][:], in1=gt[sl][:, 8 + oc, :], op=ALU.mult),
                       reads=[f"ps{bb}", f"gt{sl}"], writes=[f"m2_{mi}"])
                    op("pool", lambda e, oc=oc, mi=mi, MG=MG: e.tensor_tensor(out=MG[:, oc, :], in0=m1[mi][:], in1=m2[mi][:], op=ALU.add),
                       reads=[f"m1_{mi}", f"m2_{mi}"], writes=[f"mg{sl}"])
                    pop_deferred(9)

            def MO(tt):
                sl = tt % 2
                MG = mg[sl]
                chains = []
                for j in range(4):
                    blk = tt * 4 + j
                    R = Rb[j]
                    rk = f"Rb{j}"
                    for hh in range(2):
                        b = 4 + (2 * blk + hh) % 4

                        def mmo(e, j=j, hh=hh, b=b, MG=MG):
                            for oc in range(8):
                                ins = e.matmul(ps[b][:], lhsT=MG[:, oc, j * 128:(j + 1) * 128], rhs=wob[:, oc, hh * 512:(hh + 1) * 512],
                                               start=(oc == 0), stop=(oc == 7))
                            return ins
                        op("pe", mmo, reads=[f"mg{sl}", "wob"], writes=[f"ps{b}"])
                        op("dve", lambda e, hh=hh, b=b, R=R: e.tensor_tensor(out=R[:, hh * 512:(hh + 1) * 512], in0=ps[b][:],
                                                                              in1=gt1[:, hh * 512:(hh + 1) * 512], op=ALU.mult),
                           reads=[f"ps{b}", "gt1"], writes=[rk])
                    op("pool", lambda e, j=j, R=R, sl=sl: e.tensor_tensor(out=R[:], in0=R[:], in1=xt[sl][:, j, :], op=ALU.add),
                       reads=[f"xt3_{sl}", rk], writes=[rk])
                    ch = ln_tail(R, rk, lnt[:, 0, :], lnt[:, 1, :], st6[j], mv[j], sc2[j], g_eng="dve", sfx=f"_{j}")
                    ch.append(lambda blk=blk, R=R, rk=rk: op("sp", lambda e: e.dma_start(out=x1_d[blk * 128:(blk + 1) * 128, :], in_=R[:]),
                                                             reads=[rk], writes=[f"x1_d{blk}"], dma=True))
                    chains.append(ch)
                for k in range(max(len(c) for c in chains)):
                    for c in chains:
                        if k < len(c):
                            deferred.append(c[k])

            loadA(0)
            loadX(0)
            if NT > 1:
                loadA(1)
            OC(0)
            for tt in range(NT):
                if tt + 2 < NT:
                    loadA(tt + 2)
                if tt + 1 < NT:
                    loadX(tt + 1)
                    OC(tt + 1)
                else:
                    pop_deferred(len(deferred))
                MO(tt)
            pop_deferred(len(deferred))
            sy.barrier()

        TW = 256
        NT2 = S // TW
        JB = TW // 128
        with ExitStack() as p4:
            P = p4.enter_context
            wupb = P(nc.sbuf_tensor("wupb", [128, 8, 2 * DFF], BF16))
            wdb = P(nc.sbuf_tensor("wdb", [128, NAC, D], BF16))
            lnt = P(nc.sbuf_tensor("lnt2", [128, 2, D], F32))
            gt2 = P(nc.sbuf_tensor("gtab2", [128, D], F32))
            convp = P(nc.sbuf_tensor("convp_sb", [128, NFC, 4], F32))
            halo = [P(nc.sbuf_tensor(f"halo{i}", [128, NFC, 2], F32)) for i in range(2)]
            xt = [P(nc.sbuf_tensor(f"xt4_{i}", [128, JB, D], F32)) for i in range(2)]
            uT = [P(nc.sbuf_tensor(f"u2T{i}", [128, 8, TW], BF16)) for i in range(2)]
            hb = [P(nc.sbuf_tensor(f"hb{i}", [128, TW + 2], F32)) for i in range(2)]
            cv = [P(nc.sbuf_tensor(f"cv{i}", [128, TW], F32)) for i in range(3)]
            sg = [P(nc.sbuf_tensor(f"sg{i}", [128, TW], F32)) for i in range(2)]
            act = [P(nc.sbuf_tensor(f"act{i}", [128, NAC, TW], BF16)) for i in range(2)]
            Rb = [P(nc.sbuf_tensor(f"Rb4_{i}", [128, D], F32)) for i in range(2)]
            st6 = P(nc.sbuf_tensor("st6b", [128, 12], F32))
            mv = P(nc.sbuf_tensor("mvb", [128, 2], F32))
            sc2 = P(nc.sbuf_tensor("sc2b_", [128, 2], F32))
            for c in range(8):
                op("pool", lambda e, c=c: e.dma_start(out=wupb[:, c, :], in_=wup_d[c * 128:(c + 1) * 128, :]), writes=["wupb"], dma=True)
            for c in range(NAC):
                op("pool", lambda e, c=c: e.dma_start(out=wdb[:, c, :], in_=wdown_d[c * 128:(c + 1) * 128, :]), writes=["wdb"], dma=True)
            op("sp", lambda e: e.dma_start(out=lnt[:], in_=lnt_d[2:4].rearrange("c p n -> p c n")), writes=["lnt"], dma=True)
            op("sp", lambda e: e.dma_start(out=gt2[:], in_=gt_d[1]), writes=["gt2"], dma=True)
            op("sp", lambda e: e.dma_start(out=convp[:], in_=convp_d), writes=["convp"], dma=True)
            op("dve", lambda e: e.memset(halo[1][:], 0.0), writes=["halo1"])

            def loads4(tt):
                sl = tt % 2
                op("sp", lambda e: e.dma_start(out=xt[sl][:], in_=x1_d[tt * TW:(tt + 1) * TW, :].rearrange("(j p) d -> p j d", p=128)),
                   writes=[f"xt4_{sl}"], dma=True)
            cnts = {"pc": 0, "hc": 0}

            def T4(tt):
                sl = tt % 2
                X, U = xt[sl], uT[sl]
                for kc in range(8):
                    b = cnts["pc"] % 4
                    cnts["pc"] += 1

                    def tr(e, kc=kc, b=b, X=X):
                        for j in range(JB):
                            ins = e.transpose(out=ps[b][:, j * 128:(j + 1) * 128], in_=X[:, j, kc * 128:(kc + 1) * 128], identity=identf[:])
                        return ins
                    op("pe", tr, reads=[f"xt4_{sl}", "identf"], writes=[f"ps{b}"])
                    op("act", lambda e, kc=kc, b=b, U=U: e.activation(out=U[:, kc, :], in_=ps[b][:, 0:TW], func=AF.Identity,
                                                                      bias=modp[:, 2, kc:kc + 1], scale=modp[:, 3, kc:kc + 1]),
                       reads=[f"ps{b}", "modp"], writes=[f"u2T{sl}"])

            def U4(tt):
                sl = tt % 2
                U, A = uT[sl], act[sl]
                uk = f"u2T{sl}"
                base = cnts["hc"]
                cnts["hc"] += 2 * NAC

                def stA(c):
                    i, half = c // 2, c % 2
                    fc = i + half * NAC
                    b = cnts["pc"] % 4
                    cnts["pc"] += 1
                    hi = (base + c) % 2
                    ci = (base + c) % 3
                    H, CV = hb[hi], cv[ci]

                    def mmu(e):
                        for kc in range(8):
                            ins = e.matmul(ps[b][:, 0:TW], lhsT=wupb[:, kc, fc * 128:(fc + 1) * 128], rhs=U[:, kc, :], start=(kc == 0), stop=(kc == 7))
                        return ins
                    op("pe", mmu, reads=["wupb", uk], writes=[f"ps{b}"])
                    op("act", lambda e: e.activation(out=H[:, 2:TW + 2], in_=ps[b][:, 0:TW], func=AF.Identity), reads=[f"ps{b}"], writes=[f"hb{hi}"])
                    op("act", lambda e: e.activation(out=CV[:], in_=ps[b][:, 0:TW], func=AF.Identity, bias=convp[:, fc, 3:4], scale=convp[:, fc, 2:3]),
                       reads=[f"ps{b}", "convp"], writes=[f"cv{ci}"])
                    op("act", lambda e: e.activation(out=halo[sl][:, fc, :], in_=ps[b][:, TW - 2:TW], func=AF.Identity),
                       reads=[f"ps{b}"], writes=[f"halo{sl}"])
                    op("pool", lambda e: e.tensor_copy(out=H[:, 0:2], in_=halo[1 - sl][:, fc, :]), reads=[f"halo{1 - sl}"], writes=[f"hbh{hi}"])

                def stB(c):
                    i, half = c // 2, c % 2
                    fc = i + half * NAC
                    hi = (base + c) % 2
                    ci = (base + c) % 3
                    H, CV = hb[hi], cv[ci]
                    op("dve", lambda e: e.scalar_tensor_tensor(out=CV[:], in0=H[:, 1:TW + 1], scalar=convp[:, fc, 1:2], in1=CV[:],
                                                               op0=ALU.mult, op1=ALU.add),
                       reads=[f"hb{hi}", f"hbh{hi}", "convp", f"cv{ci}"], writes=[f"cv{ci}"])
                    op("dve", lambda e: e.scalar_tensor_tensor(out=CV[:], in0=H[:, 0:TW], scalar=convp[:, fc, 0:1], in1=CV[:],
                                                               op0=ALU.mult, op1=ALU.add),
                       reads=[f"hb{hi}", f"hbh{hi}", "convp", f"cv{ci}"], writes=[f"cv{ci}"])

                def stC(c):
                    i, half = c // 2, c % 2
                    ci = (base + c) % 3
                    CV = cv[ci]
                    SG = sg[i % 2]
                    if half == 0:
                        op("act", lambda e: e.activation(out=SG[:], in_=CV[:], func=AF.Silu), reads=[f"cv{ci}"], writes=[f"sg{i % 2}"])
                    else:
                        op("pool", lambda e: e.tensor_tensor(out=A[:, i, :], in0=SG[:], in1=CV[:], op=ALU.mult),
                           reads=[f"cv{ci}", f"sg{i % 2}"], writes=[f"act{sl}"])

                nchunk = 2 * NAC
                for st_ in range(nchunk + 2):
                    if st_ < nchunk:
                        stA(st_)
                    if 0 <= st_ - 1 < nchunk:
                        stB(st_ - 1)
                    if 0 <= st_ - 2 < nchunk:
                        stC(st_ - 2)
                    pop_deferred(1)

            def D4(tt):
                sl = tt % 2
                X, A = xt[sl], act[sl]
                for j in range(JB):
                    blk = tt * JB + j
                    R = Rb[blk % 2]
                    rk = f"Rb{blk % 2}"
                    for hh in range(2):
                        b = 4 + (2 * blk + hh) % 4

                        def mmd2(e, j=j, hh=hh, b=b, A=A):
                            for i in range(NAC):
                                ins = e.matmul(ps[b][:], lhsT=A[:, i, j * 128:(j + 1) * 128], rhs=wdb[:, i, hh * 512:(hh + 1) * 512],
                                               start=(i == 0), stop=(i == NAC - 1))
                            return ins
                        op("pe", mmd2, reads=[f"act{sl}", "wdb"], writes=[f"ps{b}"])
                        op("dve", lambda e, hh=hh, b=b, R=R: e.tensor_tensor(out=R[:, hh * 512:(hh + 1) * 512], in0=ps[b][:],
                                                                              in1=gt2[:, hh * 512:(hh + 1) * 512], op=ALU.mult),
                           reads=[f"ps{b}", "gt2"], writes=[rk])
                    op("pool", lambda e, j=j, R=R, X=X: e.tensor_tensor(out=R[:], in0=R[:], in1=X[:, j, :], op=ALU.add),
                       reads=[f"xt4_{sl}", rk], writes=[rk])
                    deferred.extend(ln_tail(R, rk, lnt[:, 0, :], lnt[:, 1, :], st6, mv, sc2, g_eng="dve", b_eng="dve"))
                    deferred.append(lambda blk=blk, R=R, rk=rk: op("sp", lambda e: e.dma_start(out=out_d[blk * 128:(blk + 1) * 128, :], in_=R[:]),
                                                                   reads=[rk], writes=[f"out_d{blk}"], dma=True))

            loads4(0)
            T4(0)
            for tt in range(NT2):
                if tt + 1 < NT2:
                    loads4(tt + 1)
                U4(tt)
                if tt + 1 < NT2:
                    T4(tt + 1)
                D4(tt)
            pop_deferred(len(deferred))
            sy.barrier()
    return nc


def _consts():
    r = np.arange(128)[:, None]
    c = np.arange(128)[None, :]
    dm = (r == c - 1).astype(np.float32) - (r == c).astype(np.float32)
    eb = ((r == 127) & (c == 0)).astype(np.float32)
    six = np.stack([(r == c), (r <= c), (r >= c), (r > c), (r < c), np.ones((128, 128), bool)]).astype(np.float32)
    return np.concatenate([six, dm[None], eb[None]], axis=0)


def prep_shared(w_ada, b_ada, w_in, b_forget, w_fox_proj, w_sb_proj, w_o, ln1_g, ln1_b, w_up, conv_w, conv_b, w_down, ln2_g, ln2_b):
    f = lambda a: np.ascontiguousarray(np.asarray(a, dtype=np.float32))
    w_in = f(w_in)
    w_in_r = np.concatenate([w_in[:, 0:512], w_in[:, 512:1024], w_in[:, 1544:2056], w_in[:, 2056:2568], w_in[:, 3080:4104],
                             w_in[:, 4104:5128], w_in[:, 1024:1536], w_in[:, 2568:3080], w_in[:, 1536:1544]], axis=1)
    convp = np.stack([f(conv_w)[0], f(conv_w)[1], f(conv_w)[2], f(conv_b)], axis=1).reshape(NFC, 128, 4).transpose(1, 0, 2)
    lnt = np.stack([np.broadcast_to(f(v)[None, :], (128, D)) for v in (ln1_g, ln1_b, ln2_g, ln2_b)])
    return {
        "w_ada": f(w_ada), "b_ada": f(b_ada).reshape(12, 1, 512), "w_in": f(w_in_r),
        "bft": f(np.broadcast_to(np.tile(f(b_forget), 4)[None, :], (128, 32))),
        "w_fox": f(w_fox_proj), "w_sb": f(w_sb_proj), "w_o": f(w_o), "lnt": f(lnt), "w_up": f(w_up),
        "convp": f(convp), "w_down": f(w_down), "consts": _consts(),
    }


def kernel(x, c, w_ada, b_ada, w_in, b_forget, w_fox_proj, w_sb_proj, w_o, ln1_g, ln1_b, w_up, conv_w, conv_b, w_down, ln2_g, ln2_b):
    x = np.asarray(x, dtype=np.float32)
    c = np.asarray(c, dtype=np.float32)
    B, S, _ = x.shape
    shared = prep_shared(w_ada, b_ada, w_in, b_forget, w_fox_proj, w_sb_proj, w_o, ln1_g, ln1_b, w_up, conv_w, conv_b, w_down, ln2_g, ln2_b)
    nc = build_nc(S)
    in_maps = []
    for b in range(B):
        m = dict(shared)
        m["x"] = np.ascontiguousarray(x[b])
        m["cT"] = np.ascontiguousarray(c[b].reshape(8, 128).T)
        in_maps.append(m)
    res = run_bass_kernel_spmd(nc, in_maps, core_ids=list(range(B)))
    return np.stack([r["out"] for r in res.results], axis=0).astype(np.float32)
```

```python
import numpy as np
from contextlib import ExitStack
import concourse.bass as bass
import concourse.mybir as mybir
from concourse.bass_utils import run_bass_kernel_spmd

F32 = mybir.dt.float32
BF16 = mybir.dt.bfloat16
AF = mybir.ActivationFunctionType
ALU = mybir.AluOpType
AX = mybir.AxisListType

D = 1024
DFF = 2816
NH = 8
DH = 64
NFC = 2 * DFF // 128
NAC = DFF // 128
ALPHA = 2.0 ** 0.25
EPS = 1e-5


class Sync:
    def __init__(self, nc, st):
        self.nc = nc
        self.st = st
        self.nsem = 0
        self.engs = {"sp": nc.sync, "act": nc.scalar, "dve": nc.vector, "pool": nc.gpsimd, "pe": nc.tensor}
        self.cs = {e: [self._newsem(), 0] for e in self.engs}
        self.waited = {e: {} for e in self.engs}
        self.lastw = {}
        self.readers = {}
        self.NDS = 8
        self.dsem = {e: [self._newsem() for _ in range(self.NDS)] for e in ("sp", "pool")}
        self.dcnt = {e: [0] * self.NDS for e in ("sp", "pool")}
        self.dn = {e: 0 for e in ("sp", "pool")}

    def _newsem(self):
        self.nsem += 1
        return self.st.enter_context(self.nc.semaphore(f"sy{self.nsem}"))

    def _wait(self, eng, sem, val):
        w = self.waited[eng]
        if w.get(sem, 0) >= val:
            return
        self.engs[eng].wait_ge(sem, val)
        w[sem] = val

    def op(self, eng, fn, reads=(), writes=(), dma=False):
        deps = []
        for k in reads:
            lw = self.lastw.get(k)
            if lw is not None:
                if lw[2] != eng or lw[3] or eng in ("act", "dve", "pool"):
                    deps.append(lw)
        for k in writes:
            lw = self.lastw.get(k)
            if lw is not None and (lw[2] != eng or lw[3] or eng in ("act", "dve", "pool")):
                deps.append(lw)
            for r in self.readers.get(k, {}).values():
                if r[2] != eng or r[3] or eng in ("act", "dve", "pool"):
                    deps.append(r)
        for d in deps:
            self._wait(eng, d[0], d[1])
        e = self.engs[eng]
        if dma:
            i = self.dn[eng] % self.NDS
            self.dn[eng] += 1
            sem = self.dsem[eng][i]
            if self.dcnt[eng][i] > 0:
                self._wait(eng, sem, self.dcnt[eng][i])
            ins = fn(e)
            self.dcnt[eng][i] += 16
            ins.then_inc(sem, 16)
            ref = (sem, self.dcnt[eng][i], eng, True)
            rkey = ("dma", eng, i)
        else:
            c = self.cs[eng]
            if c[1] >= 30000:
                c[0] = self._newsem()
                c[1] = 0
            ins = fn(e)
            c[1] += 1
            ins.then_inc(c[0], 1)
            ref = (c[0], c[1], eng, False)
            rkey = eng
        for k in reads:
            self.readers.setdefault(k, {})[rkey] = ref
        for k in writes:
            self.lastw[k] = ref
            self.readers[k] = {}
        return ref

    def barrier(self):
        refs = [(c[0], c[1]) for c in self.cs.values() if c[1] > 0]
        for e in self.dsem:
            for i in range(self.NDS):
                if self.dcnt[e][i] > 0:
                    refs.append((self.dsem[e][i], self.dcnt[e][i]))
        for eng in self.engs:
            for (s, v) in refs:
                if s is self.cs[eng][0]:
                    continue
                self._wait(eng, s, v)
        self.lastw.clear()
        self.readers.clear()


def build_nc(S, dbg=False):
    assert S % 512 == 0
    NT = S // 512
    NB = S // 128
    nc = bass.Bass("TRN2", target_bir_lowering=False)
    okind = "ExternalOutput" if dbg else "Internal"

    def din(name, shape, dt=F32):
        return nc.dram_tensor(name, list(shape), dt, kind="ExternalInput").ap()

    x_d = din("x", [S, D])
    cT_d = din("cT", [128, 8])
    wada_d = din("w_ada", [D, 6 * D])
    bada_d = din("b_ada", [12, 1, 512])
    win_d = din("w_in", [D, 5128])
    bft_d = din("bft", [128, 32])
    wfox_d = din("w_fox", [512, D])
    wsb_d = din("w_sb", [512, D])
    wo_d = din("w_o", [D, D])
    lnt_d = din("lnt", [4, 128, D])
    wup_d = din("w_up", [D, 2 * DFF])
    convp_d = din("convp", [128, NFC, 4])
    wdown_d = din("w_down", [DFF, D])
    consts_d = din("consts", [8, 128, 128])
    out_d = nc.dram_tensor("out", [S, D], F32, kind="ExternalOutput").ap()

    fm_d = nc.dram_tensor("fm_s", [32, 128, S], BF16, kind=okind).ap()
    v_d = nc.dram_tensor("v_s", [2, S, 512], BF16, kind=okind).ap()
    caug_d = nc.dram_tensor("caug_s", [8, 3, S], BF16, kind=okind).ap()
    yT_d = nc.dram_tensor("yT_s", [2, 512, S], BF16, kind=okind).ap()
    x1_d = nc.dram_tensor("x1_s", [S, D], F32, kind=okind).ap()
    dv_d = nc.dram_tensor("dv_s", [S, 512], BF16, kind=okind).ap()
    gt_d = nc.dram_tensor("gt_s", [2, 128, D], F32).ap()

    with ExitStack() as st:
        E = st.enter_context
        sy = Sync(nc, st)
        op = sy.op
        ps = [E(nc.psum_tensor(f"psb{i}", [128, 512], F32)) for i in range(8)]

        identf = E(nc.sbuf_tensor("identf", [128, 128], F32))
        mhalf = E(nc.sbuf_tensor("mhalf", [128, 1], F32))
        modp = E(nc.sbuf_tensor("modp", [128, 4, 8], F32))
        p012 = ExitStack()
        E2 = p012.enter_context
        cf = E2(nc.sbuf_tensor("cf", [128, 8, 128], F32))
        cb = E2(nc.sbuf_tensor("cb", [128, 8, 128], BF16))
        ctok = E2(nc.sbuf_tensor("ctok", [128, NB, 8], F32))
        cql = E2(nc.sbuf_tensor("cql", [128, NT + 1, 8], F32))
        IDENT, UINC, LINC, LSTR, USTR, ONES, DM, EB = range(8)

        op("sp", lambda e: e.dma_start(out=identf[:], in_=consts_d[0]), writes=["identf"], dma=True)
        op("dve", lambda e: e.memset(mhalf[:], -0.5), writes=["mhalf"])
        op("sp", lambda e: e.dma_start(out=cf[:], in_=consts_d.rearrange("c p n -> p c n")), writes=["cf"], dma=True)
        op("pool", lambda e: e.dma_start(out=cb[:], in_=consts_d.rearrange("c p n -> p c n")), writes=["cb"], dma=True)

        with ExitStack() as p0:
            P = p0.enter_context
            cts = P(nc.sbuf_tensor("cts", [128, 8], F32))
            modtab = P(nc.sbuf_tensor("modtab", [128, 6 * D], F32))
            crep = P(nc.sbuf_tensor("crep", [128, 8, 128], F32))
            wa = [P(nc.sbuf_tensor(f"wa{i}", [128, 8, 512], F32)) for i in range(2)]
            bad = [P(nc.sbuf_tensor(f"bad{i}", [1, 512], F32)) for i in range(2)]
            dtmp = P(nc.sbuf_tensor("dtmp", [128, 8, 128], F32))
            op("sp", lambda e: e.dma_start(out=cts[:], in_=cT_d), writes=["cts"], dma=True)
            for kc in range(8):
                op("dve", lambda e, kc=kc: e.tensor_copy(out=crep[:, kc, :], in_=cts[:, kc:kc + 1].to_broadcast([128, 128])),
                   reads=["cts"], writes=[f"crep{kc}"])
            for g in range(12):
                sl = g % 2
                op("sp", lambda e, g=g, sl=sl: e.dma_start(
                    out=wa[sl][:], in_=wada_d[:, g * 512:(g + 1) * 512].rearrange("(kc p) n -> p kc n", p=128)),
                   writes=[f"wa{sl}"], dma=True)
                op("sp", lambda e, g=g, sl=sl: e.dma_start(out=bad[sl][:], in_=bada_d[g]), writes=[f"bad{sl}"], dma=True)
                bank = ps[g % 2]

                def mm(e, g=g, sl=sl, bank=bank):
                    for kc in range(8):
                        e.matmul(bank[:], lhsT=crep[:, kc, :], rhs=wa[sl][:, kc, :], start=(kc == 0), stop=False)
                    return e.matmul(bank[:], lhsT=cf[0:1, ONES, :], rhs=bad[sl][0:1, :], start=False, stop=True)
                op("pe", mm, reads=[f"wa{sl}", f"bad{sl}", "cf"] + [f"crep{k}" for k in range(8)], writes=[f"ps{g % 2}"])
                addone = g in (2, 3, 8, 9)
                if g in (4, 5, 10, 11):
                    op("dve", lambda e, g=g, bank=bank: e.tensor_scalar(out=modtab[:, g * 512:(g + 1) * 512], in0=bank[:], scalar1=1.0,
                                                                        scalar2=1.0 / ALPHA, op0=ALU.add, op1=ALU.mult),
                       reads=[f"ps{g % 2}"], writes=[f"modtab{g}"])
                elif addone:
                    op("dve", lambda e, g=g, bank=bank: e.tensor_scalar_add(out=modtab[:, g * 512:(g + 1) * 512], in0=bank[:], scalar1=1.0),
                       reads=[f"ps{g % 2}"], writes=[f"modtab{g}"])
                else:
                    op("act", lambda e, g=g, bank=bank: e.activation(out=modtab[:, g * 512:(g + 1) * 512], in_=bank[:], func=AF.Identity),
                       reads=[f"ps{g % 2}"], writes=[f"modtab{g}"])
            for vi, base in enumerate((0, 1024, 3072, 4096)):
                for k in range(8):
                    op("dve", lambda e, base=base, k=k: e.tensor_tensor(
                        out=dtmp[:, k, :], in0=modtab[:, base + k * 128: base + (k + 1) * 128], in1=cf[:, IDENT, :], op=ALU.mult),
                       reads=[f"modtab{(base + k * 128) // 512}", "cf"], writes=["dtmp"])
                op("dve", lambda e, vi=vi: e.tensor_reduce(out=modp[:, vi, :], in_=dtmp[:], axis=AX.X, op=ALU.add),
                   reads=["dtmp"], writes=["modp"])
            op("sp", lambda e: e.dma_start(out=gt_d[0], in_=modtab[:, 2048:3072]), reads=["modtab4", "modtab5"], writes=["gt_d0"], dma=True)
            op("sp", lambda e: e.dma_start(out=gt_d[1], in_=modtab[:, 5120:6144]), reads=["modtab10", "modtab11"], writes=["gt_d1"], dma=True)
            sy.barrier()

        with ExitStack() as p1:
            P = p1.enter_context
            winb = P(nc.sbuf_tensor("winb", [128, 8, 5128], BF16))
            xt = [P(nc.sbuf_tensor(f"xt{i}", [128, 4, D], F32)) for i in range(2)]
            uT = [P(nc.sbuf_tensor(f"uT{i}", [128, 8, 512], BF16)) for i in range(2)]
            fst = [P(nc.sbuf_tensor(f"fst{i}", [128, 4, 512], BF16)) for i in range(3)]
            vst = [P(nc.sbuf_tensor(f"vst{i}", [128, 4, 1024], BF16)) for i in range(2)]
            dvst = [P(nc.sbuf_tensor(f"dvst{i}", [128, 4, 512], BF16)) for i in range(2)]
            bft = P(nc.sbuf_tensor("bft_sb", [128, 32], F32))
            fb = P(nc.sbuf_tensor("fb", [128, 32], F32))
            ef = P(nc.sbuf_tensor("ef", [128, 32], F32))
            lf = [P(nc.sbuf_tensor(f"lf{i}", [128, 32], F32)) for i in range(2)]
            d32 = P(nc.sbuf_tensor("d32", [8, 512], F32))
            r1 = P(nc.sbuf_tensor("r1", [8, 512], F32))
            r2 = P(nc.sbuf_tensor("r2", [8, 512], F32))
            cst = [P(nc.sbuf_tensor(f"cst{i}", [8, 3, 512], BF16)) for i in range(2)]

            for kc in range(8):
                op("pool", lambda e, kc=kc: e.dma_start(out=winb[:, kc, :], in_=win_d[kc * 128:(kc + 1) * 128, :]),
                   writes=[f"winb{kc}"], dma=True)
            WIN = [f"winb{k}" for k in range(8)]
            op("sp", lambda e: e.dma_start(out=bft[:], in_=bft_d), writes=["bft"], dma=True)
            op("dve", lambda e: e.memset(cql[:, 0, :], 0.0), writes=["cql"])

            def load_x(tt):
                op("sp", lambda e: e.dma_start(out=xt[tt % 2][:], in_=x_d[tt * 512:(tt + 1) * 512, :].rearrange("(j p) d -> p j d", p=128)),
                   writes=[f"xt{tt % 2}"], dma=True)
            load_x(0)
            pcnt = [0]

            def nbank():
                pcnt[0] += 1
                return pcnt[0] % 6
            fcnt = [0]
            for tt in range(NT):
                if tt + 1 < NT:
                    load_x(tt + 1)
                X = xt[tt % 2]
                U = uT[tt % 2]
                ukeys = [f"uT{tt % 2}_{k}" for k in range(8)]
                for kc in range(8):
                    b = nbank()

                    def tr(e, kc=kc, b=b, X=X):
                        for j in range(4):
                            ins = e.transpose(out=ps[b][:, j * 128:(j + 1) * 128], in_=X[:, j, kc * 128:(kc + 1) * 128], identity=cf[:, IDENT, :])
                        return ins
                    op("pe", tr, reads=[f"xt{tt % 2}", "cf"], writes=[f"ps{b}"])
                    if kc % 2 == 0:
                        op("act", lambda e, kc=kc, b=b, U=U: e.activation(out=U[:, kc, :], in_=ps[b][:], func=AF.Identity,
                                                                          bias=modp[:, 0, kc:kc + 1], scale=modp[:, 1, kc:kc + 1]),
                           reads=[f"ps{b}", "modp"], writes=[ukeys[kc]])
                    else:
                        op("dve", lambda e, kc=kc, b=b, U=U: e.tensor_scalar(out=U[:, kc, :], in0=ps[b][:], scalar1=modp[:, 1, kc:kc + 1],
                                                                             scalar2=modp[:, 0, kc:kc + 1], op0=ALU.mult, op1=ALU.add),
                           reads=[f"ps{b}", "modp"], writes=[ukeys[kc]])
                for oc in range(32):
                    b = nbank()

                    def mm(e, oc=oc, b=b, U=U):
                        for kc in range(8):
                            ins = e.matmul(ps[b][:], lhsT=winb[:, kc, oc * 128:(oc + 1) * 128], rhs=U[:, kc, :], start=(kc == 0), stop=(kc == 7))
                        return ins
                    op("pe", mm, reads=WIN + ukeys, writes=[f"ps{b}"])
                    fs = fcnt[0] % 3
                    dst = fst[fs][:, oc % 4, :]
                    if oc < 16 and (oc // 4) % 2 == 0:
                        op("dve", lambda e, b=b, dst=dst: e.tensor_scalar_mul(out=dst, in0=ps[b][:], scalar1=0.125),
                           reads=[f"ps{b}"], writes=[f"fst{fs}"])
                    elif oc < 16:
                        op("dve", lambda e, b=b, dst=dst: e.tensor_copy(out=dst, in_=ps[b][:]), reads=[f"ps{b}"], writes=[f"fst{fs}"])
                    else:
                        op("act", lambda e, b=b, dst=dst: e.activation(out=dst, in_=ps[b][:], func=AF.Sigmoid),
                           reads=[f"ps{b}"], writes=[f"fst{fs}"])
                    if oc % 4 == 3:
                        oc0 = oc - 3
                        op("sp", lambda e, oc0=oc0, fs=fs, tt=tt: e.dma_start(
                            out=fm_d[oc0:oc0 + 4, :, tt * 512:(tt + 1) * 512].rearrange("c p t -> p c t"), in_=fst[fs][:]),
                           reads=[f"fst{fs}"], writes=[f"fm_d{oc0}_{tt}"], dma=True)
                        fcnt[0] += 1
                VS = vst[tt % 2]
                for j in range(4):
                    for ab in range(2):
                        b = nbank()

                        def mmv(e, j=j, ab=ab, b=b, U=U):
                            for kc in range(8):
                                ins = e.matmul(ps[b][:], lhsT=U[:, kc, j * 128:(j + 1) * 128],
                                               rhs=winb[:, kc, 4096 + ab * 512: 4096 + (ab + 1) * 512], start=(kc == 0), stop=(kc == 7))
                            return ins
                        op("pe", mmv, reads=WIN + ukeys, writes=[f"ps{b}"])
                        dst = VS[:, j, ab * 512:(ab + 1) * 512]
                        if ab == 0:
                            op("act", lambda e, b=b, dst=dst: e.activation(out=dst, in_=ps[b][:], func=AF.Identity),
                               reads=[f"ps{b}"], writes=[f"vst{tt % 2}"])
                        else:
                            op("dve", lambda e, b=b, dst=dst: e.tensor_copy(out=dst, in_=ps[b][:]), reads=[f"ps{b}"], writes=[f"vst{tt % 2}"])
                for ab in range(2):
                    op("sp", lambda e, ab=ab, tt=tt, VS=VS: e.dma_start(
                        out=v_d[ab, tt * 512:(tt + 1) * 512, :].rearrange("(j p) d -> p j d", p=128), in_=VS[:, :, ab * 512:(ab + 1) * 512]),
                       reads=[f"vst{tt % 2}"], writes=[f"v_d{ab}_{tt}"], dma=True)
                DVS = dvst[tt % 2]
                for j in range(4):
                    b = nbank()
                    blk = tt * 4 + j
                    prev = VS[:, j - 1, 512:1024] if j > 0 else vst[(tt - 1) % 2][:, 3, 512:1024]

                    def mmdv(e, j=j, b=b, blk=blk, prev=prev, VS=VS):
                        ins = e.matmul(ps[b][:], lhsT=cb[:, DM, :], rhs=VS[:, j, 512:1024], start=True, stop=(blk == 0))
                        if blk > 0:
                            ins = e.matmul(ps[b][:], lhsT=cb[:, EB, :], rhs=prev, start=False, stop=True)
                        return ins
                    op("pe", mmdv, reads=[f"vst{tt % 2}", f"vst{(tt - 1) % 2}", "cb"], writes=[f"ps{b}"])
                    op("act", lambda e, j=j, b=b, DVS=DVS: e.activation(out=DVS[:, j, :], in_=ps[b][:], func=AF.Identity),
                       reads=[f"ps{b}"], writes=[f"dvst{tt % 2}"])
                op("sp", lambda e, tt=tt, DVS=DVS: e.dma_start(out=dv_d[tt * 512:(tt + 1) * 512, :].rearrange("(j p) d -> p j d", p=128), in_=DVS[:]),
                   reads=[f"dvst{tt % 2}"], writes=[f"dv_d{tt}"], dma=True)
                LF = lf[tt % 2]

                def mmf(e, U=U):
                    for j in range(4):
                        for kc in range(8):
                            ins = e.matmul(ps[6][:, j * 8:(j + 1) * 8], lhsT=U[:, kc, j * 128:(j + 1) * 128], rhs=winb[:, kc, 5120:5128],
                                           start=(kc == 0), stop=(kc == 7))
                    return ins
                op("pe", mmf, reads=WIN + ukeys, writes=["ps6"])
                op("dve", lambda e: e.tensor_tensor(out=fb[:], in0=ps[6][:, 0:32], in1=bft[:], op=ALU.add), reads=["ps6", "bft"], writes=["fb"])
                op("act", lambda e: e.activation(out=ef[:], in_=fb[:], func=AF.Exp, scale=-1.0), reads=["fb"], writes=["ef"])
                op("act", lambda e, LF=LF: e.activation(out=LF[:], in_=ef[:], func=AF.Ln, bias=1.0, scale=1.0), reads=["ef"], writes=[f"lf{tt % 2}"])

                def mmc(e, LF=LF):
                    for j in range(4):
                        ins = e.matmul(ps[6][:, 64 + j * 8: 64 + (j + 1) * 8], lhsT=cf[:, UINC, :], rhs=LF[:, j * 8:(j + 1) * 8], start=True, stop=(j == 0))
                        for jp in range(j):
                            ins = e.matmul(ps[6][:, 64 + j * 8: 64 + (j + 1) * 8], lhsT=cf[:, ONES, :], rhs=LF[:, jp * 8:(jp + 1) * 8],
                                           start=False, stop=(jp == j - 1))
                    for jp in range(4):
                        ins = e.matmul(ps[6][:, 96:104], lhsT=cf[:, ONES, :], rhs=LF[:, jp * 8:(jp + 1) * 8], start=(jp == 0), stop=(jp == 3))
                    return ins
                op("pe", mmc, reads=[f"lf{tt % 2}", "cf"], writes=["ps6"])
                for j in range(4):
                    op("dve", lambda e, j=j, tt=tt: e.tensor_tensor(out=ctok[:, tt * 4 + j, :], in0=ps[6][:, 64 + j * 8: 64 + (j + 1) * 8],
                                                                    in1=cql[:, tt, :], op=ALU.add),
                       reads=["ps6", "cql"], writes=["ctok"])
                op("dve", lambda e, tt=tt: e.tensor_tensor(out=cql[:, tt + 1, :], in0=ps[6][:, 96:104], in1=cql[:, tt, :], op=ALU.add),
                   reads=["ps6", "cql"], writes=["cql"])

                def mmd(e, LF=LF):
                    for j in range(4):
                        ins = e.matmul(ps[7][0:8, j * 128:(j + 1) * 128], lhsT=LF[:, j * 8:(j + 1) * 8], rhs=cf[:, LSTR, :], start=True, stop=(j == 3))
                        for jp in range(j + 1, 4):
                            ins = e.matmul(ps[7][0:8, j * 128:(j + 1) * 128], lhsT=LF[:, jp * 8:(jp + 1) * 8], rhs=cf[:, ONES, :],
                                           start=False, stop=(jp == 3))
                    return ins
                op("pe", mmd, reads=[f"lf{tt % 2}", "cf"], writes=["ps7"])
                CS = cst[tt % 2]
                ck = f"cst{tt % 2}"
                op("dve", lambda e: e.tensor_copy(out=d32[:], in_=ps[7][0:8, :]), reads=["ps7"], writes=["d32"])
                op("dve", lambda e, CS=CS: e.tensor_copy(out=CS[:, 0, :], in_=d32[:]), reads=["d32"], writes=[ck])
                op("dve", lambda e, CS=CS: e.tensor_tensor(out=r1[:], in0=d32[:], in1=CS[:, 0, :], op=ALU.subtract), reads=["d32", ck], writes=["r1"])
                op("dve", lambda e, CS=CS: e.tensor_copy(out=CS[:, 1, :], in_=r1[:]), reads=["r1"], writes=[ck])
                op("dve", lambda e, CS=CS: e.tensor_tensor(out=r2[:], in0=r1[:], in1=CS[:, 1, :], op=ALU.subtract), reads=["r1", ck], writes=["r2"])
                op("dve", lambda e, CS=CS: e.tensor_copy(out=CS[:, 2, :], in_=r2[:]), reads=["r2"], writes=[ck])
                op("sp", lambda e, CS=CS, tt=tt: e.dma_start(out=caug_d[:, :, tt * 512:(tt + 1) * 512], in_=CS[:]),
                   reads=[ck], writes=[f"caug_d{tt}"], dma=True)
            sy.barrier()

        with ExitStack() as p2:
            P = p2.enter_context
            KT = [P(nc.sbuf_tensor(f"KT{i}", [128, S], BF16)) for i in range(2)]
            QT = [P(nc.sbuf_tensor(f"QT{i}", [128, S], BF16)) for i in range(2)]
            VV = [P(nc.sbuf_tensor(f"VV{i}", [128, NB, 65], BF16)) for i in range(2)]
            bias_t = [P(nc.sbuf_tensor(f"bias{i}", [128, NT, NB], F32)) for i in range(2)]
            zeros = P(nc.sbuf_tensor("zeros", [128, 64], BF16))
            Pb = [P(nc.sbuf_tensor(f"Pb{i}", [128, 512], BF16)) for i in range(4)]
            dVV = [P(nc.sbuf_tensor(f"dVV{i}", [128, NB, 64], BF16)) for i in range(4)]
            VV2 = [P(nc.sbuf_tensor(f"VVb{i}", [128, NB, 64], BF16)) for i in range(2)]
            wbuf = [P(nc.sbuf_tensor(f"wbuf{i}", [128, 512], F32)) for i in range(6)]
            Gs = [P(nc.sbuf_tensor(f"Gs{i}", [128, 512], BF16)) for i in range(8)]
            GT = [P(nc.sbuf_tensor(f"GT{i}", [128, 512], BF16)) for i in range(6)]
            onesf = P(nc.sbuf_tensor("onesf", [128, 512], F32))
            rs = [P(nc.sbuf_tensor(f"rs{i}", [65, 512], F32)) for i in range(2)]
            ysb = [P(nc.sbuf_tensor(f"ysb{i}", [64, 512], F32)) for i in range(2)]
            yo = [P(nc.sbuf_tensor(f"yo{i}", [64, 512], BF16)) for i in range(2)]
            yo2 = [P(nc.sbuf_tensor(f"yo2_{i}", [128, 512], BF16)) for i in range(2)]

            op("dve", lambda e: e.memset(zeros[:], 0.0), writes=["zeros"])
            for i in range(2):
                op("dve", lambda e, i=i: e.memset(KT[i][64:67, :], 1.0), writes=[f"KTa{i}"])
                op("dve", lambda e, i=i: e.memset(VV[i][:, :, 64:65], 1.0), writes=[f"VVa{i}"])

            def fox_loads(h):
                sl = h % 2
                hp, hf = h // 2, h % 2
                op("sp", lambda e: e.dma_start(out=KT[sl][0:64, :], in_=fm_d[4 + hp, hf * 64:(hf + 1) * 64, :]), writes=[f"KT{sl}"], dma=True)
                op("sp", lambda e: e.dma_start(out=QT[sl][0:64, :], in_=fm_d[0 + hp, hf * 64:(hf + 1) * 64, :]), writes=[f"QT{sl}"], dma=True)
                op("sp", lambda e: e.dma_start(out=QT[sl][64:67, :], in_=caug_d[h]), writes=[f"QTa{sl}"], dma=True)
                for k0 in range(0, NB, 8):
                    op("sp", lambda e, k0=k0: e.dma_start(out=VV[sl][:, k0:k0 + 8, 0:64],
                                                          in_=v_d[0, k0 * 128:(k0 + 8) * 128, h * 64:(h + 1) * 64].rearrange("(kb p) d -> p kb d", p=128)),
                       writes=[f"VV{sl}"], dma=True)
                for qt in range(NT):
                    op("dve", lambda e, qt=qt: e.tensor_scalar(out=bias_t[sl][:, qt, :], in0=ctok[:, :, h], scalar1=cql[:, qt + 1, h:h + 1],
                                                               scalar2=None, op0=ALU.subtract),
                       reads=["ctok", "cql"], writes=[f"bias{sl}"])

            items = []
            for h in range(NH):
                for qt in range(NT):
                    n = 4 * qt + 4
                    for kb in range(n):
                        items.append(dict(h=h, qt=qt, kb=kb, first=(kb == 0), last=(kb == n - 1), hfirst=(qt == 0 and kb == 0)))
            for i, it in enumerate(items):
                it["i"] = i
                it["tile"] = it["h"] * NT + it["qt"]
                j = it["kb"] - 4 * it["qt"]
                it["j"] = j
                it["c0"] = 128 * j if j > 0 else 0
            nper = len(items) // NH
            assert nper >= 10
            fox_loads(0)

            def fx_z(it):
                i, h, qt, kb, c0 = it["i"], it["h"], it["qt"], it["kb"], it["c0"]
                sl = h % 2
                if it["hfirst"] and h + 1 < NH:
                    pass
                b = i % 3
                op("pe", lambda e: e.matmul(ps[b][:, c0:512], lhsT=KT[sl][0:67, kb * 128:(kb + 1) * 128],
                                            rhs=QT[sl][0:67, qt * 512 + c0:(qt + 1) * 512], start=True, stop=True),
                   reads=[f"KT{sl}", f"KTa{sl}", f"QT{sl}", f"QTa{sl}"], writes=[f"ps{b}"])

            def fx_p(it):
                i, h, qt, kb, c0 = it["i"], it["h"], it["qt"], it["kb"], it["c0"]
                sl = h % 2
                b = i % 3
                pb = i % 4
                op("act", lambda e: e.activation(out=Pb[pb][:, c0:512], in_=ps[b][:, c0:512], func=AF.Exp,
                                                 bias=bias_t[sl][:, qt, kb:kb + 1], scale=1.0),
                   reads=[f"ps{b}", f"bias{sl}"], writes=[f"Pb{pb}"])
                if it["j"] >= 0:
                    op("pool", lambda e: e.tensor_tensor(out=Pb[pb][:, c0:c0 + 128], in0=Pb[pb][:, c0:c0 + 128], in1=cb[:, UINC, :], op=ALU.mult),
                       reads=[f"Pb{pb}", "cb"], writes=[f"Pb{pb}"])

            def fx_pv(it):
                i, h, qt, kb, c0 = it["i"], it["h"], it["qt"], it["kb"], it["c0"]
                sl = h % 2
                pb = i % 4
                yb = 4 + it["tile"] % 2
                op("pe", lambda e: e.matmul(ps[yb][0:65, c0:512], lhsT=VV[sl][:, kb, 0:65], rhs=Pb[pb][:, c0:512],
                                            start=it["first"], stop=it["last"]),
                   reads=[f"VV{sl}", f"VVa{sl}", f"Pb{pb}"], writes=[f"ps{yb}"])

            def fx_fin1(it):
                if not it["last"]:
                    return
                t2 = it["tile"] % 2
                yb = 4 + t2
                op("dve", lambda e: e.reciprocal(out=rs[t2][64:65, :], in_=ps[yb][64:65, :]), reads=[f"ps{yb}"], writes=[f"rs{t2}"])
                op("dve", lambda e: e.tensor_copy(out=ysb[t2][:], in_=ps[yb][0:64, :]), reads=[f"ps{yb}"], writes=[f"ysb{t2}"])

            def fx_fin2(it):
                if not it["last"]:
                    return
                h, qt = it["h"], it["qt"]
                t2 = it["tile"] % 2
                op("pe", lambda e: e.matmul(ps[6][0:64, :], lhsT=cf[64:65, ONES, 0:64], rhs=rs[t2][64:65, :], start=True, stop=True),
                   reads=[f"rs{t2}", "cf"], writes=["ps6"])
                op("dve", lambda e: e.tensor_tensor(out=yo[t2][:], in0=ysb[t2][:], in1=ps[6][0:64, :], op=ALU.mult),
                   reads=[f"ysb{t2}", "ps6"], writes=[f"yo{t2}"])
                op("sp", lambda e: e.dma_start(out=yT_d[0, h * 64:(h + 1) * 64, qt * 512:(qt + 1) * 512], in_=yo[t2][:]),
                   reads=[f"yo{t2}"], writes=[f"yT_d0_{h}_{qt}"], dma=True)

            stages = [(fx_z, 0), (fx_p, 1), (fx_pv, 2), (fx_fin1, 3), (fx_fin2, 9)]
            n = len(items)
            for s in range(n + 10):
                for fn, sk in stages:
                    i = s - sk
                    if 0 <= i < n:
                        fn(items[i])
                if s < n and (s % nper) == 6:
                    h = items[s]["h"]
                    if h + 1 < NH:
                        fox_loads(h + 1)
            sy.barrier()

            op("dve", lambda e: e.memset(onesf[:], 1.0), writes=["onesf"])

            def sb_loads(hp):
                sl = hp % 2
                op("sp", lambda e: e.dma_start(out=KT[sl][:, :], in_=fm_d[12 + hp]), writes=[f"KT{sl}"], dma=True)
                op("sp", lambda e: e.dma_start(out=QT[sl][:, :], in_=fm_d[8 + hp]), writes=[f"QT{sl}"], dma=True)

            VS = [VV[0], VV[1], VV2[0], VV2[1]]

            def sb_vload(h):
                sl = h % 4
                for k0 in range(0, NB, 8):
                    op("sp", lambda e, k0=k0: e.dma_start(out=VS[sl][:, k0:k0 + 8, 0:64],
                                                          in_=v_d[1, k0 * 128:(k0 + 8) * 128, h * 64:(h + 1) * 64].rearrange("(kb p) d -> p kb d", p=128)),
                       writes=[f"VS{sl}"], dma=True)
                    op("sp", lambda e, k0=k0: e.dma_start(out=dVV[sl][:, k0:k0 + 8, :],
                                                          in_=dv_d[k0 * 128:(k0 + 8) * 128, h * 64:(h + 1) * 64].rearrange("(kb p) d -> p kb d", p=128)),
                       writes=[f"dVV{sl}"], dma=True)

            items = []
            for hp in range(NH // 2):
                for qb in range(NB):
                    a, jq = qb // 4, qb % 4
                    for kt in range(a, -1, -1):
                        W = 128 * (jq + 1) if kt == a else 512
                        for hh in range(2):
                            items.append(dict(h=2 * hp + hh, hp=hp, hh=hh, qb=qb, a=a, jq=jq, kt=kt, W=W, first=(kt == a), last=(kt == 0)))
            for i, it in enumerate(items):
                it["i"] = i
            npair = len(items) // 2
            per_hp = npair // (NH // 2)
            assert per_hp >= 12
            sb_loads(0)
            sb_vload(0)
            sb_vload(1)

            def sb_z(it):
                i, hp, hh, qb, kt, W = it["i"], it["hp"], it["hh"], it["qb"], it["kt"], it["W"]
                sl = hp % 2
                p0_ = 64 * hh
                b = hh
                op("pe", lambda e: e.matmul(ps[b][:, 0:W], lhsT=QT[sl][p0_:p0_ + 64, qb * 128:(qb + 1) * 128],
                                            rhs=KT[sl][p0_:p0_ + 64, kt * 512: kt * 512 + W], start=True, stop=True),
                   reads=[f"KT{sl}", f"QT{sl}"], writes=[f"ps{b}"])

            def sb_w(it):
                i, W, jq, hh = it["i"], it["W"], it["jq"], it["hh"]
                b = hh
                wi = i % 6
                op("act", lambda e: e.activation(out=wbuf[wi][:, 0:W], in_=ps[b][:, 0:W], func=AF.Sigmoid, scale=-1.0),
                   reads=[f"ps{b}"], writes=[f"wbuf{wi}"])
                if it["first"]:
                    c0 = 128 * jq
                    op("pool", lambda e: e.tensor_tensor(out=wbuf[wi][:, c0:c0 + 128], in0=wbuf[wi][:, c0:c0 + 128], in1=cf[:, LSTR, :], op=ALU.mult),
                       reads=[f"wbuf{wi}", "cf"], writes=[f"wbuf{wi}"])
                    op("pool", lambda e: e.tensor_tensor(out=wbuf[wi][:, c0:c0 + 128], in0=wbuf[wi][:, c0:c0 + 128], in1=cf[:, UINC, :], op=ALU.add),
                       reads=[f"wbuf{wi}", "cf"], writes=[f"wbuf{wi}"])

            def rev(ap_, n):
                return bass.AP(ap_.tensor, ap_.offset + (n - 1), [[ap_.ap[0][0], 128], [-1, n]])

            def sb_scan(it):
                i, W, jq = it["i"], it["W"], it["jq"]
                wi = i % 6
                gi = i % 8
                if it["first"]:
                    init = 1.0
                    rd = [f"wbuf{wi}", "onesf"]
                else:
                    init = Gs[(i - 2) % 8][:, 0:1]
                    rd = [f"wbuf{wi}", "onesf", f"Gs{(i - 2) % 8}"]
                op("dve", lambda e: e.tensor_tensor_scan(out=rev(Gs[gi][:, 0:W], W), data0=rev(wbuf[wi][:, 0:W], W), data1=onesf[:, 0:W],
                                                         initial=init, op0=ALU.mult, op1=ALU.mult),
                   reads=rd, writes=[f"Gs{gi}"])
                if it["first"]:
                    c0 = 128 * jq
                    op("pool", lambda e: e.tensor_tensor(out=Gs[gi][:, c0:c0 + 128], in0=Gs[gi][:, c0:c0 + 128], in1=cb[:, LINC, :], op=ALU.mult),
                       reads=[f"Gs{gi}", "cb"], writes=[f"Gs{gi}"])

            def sb_tr(it):
                i, W = it["i"], it["W"]
                gi = i % 8
                ti = i % 4

                def tr(e):
                    for c in range(W // 128):
                        ins = e.matmul(ps[2 + ti][:, c * 128:(c + 1) * 128], lhsT=Gs[gi][:, c * 128:(c + 1) * 128], rhs=cb[:, IDENT, :],
                                       start=True, stop=True)
                    return ins
                op("pe", tr, reads=[f"Gs{gi}", "cb"], writes=[f"pst{ti}"])

            def sb_ev(it):
                i, W = it["i"], it["W"]
                ti = i % 4
                g6 = i % 6
                op("act", lambda e: e.activation(out=GT[g6][:, 0:W], in_=ps[2 + ti][:, 0:W], func=AF.Identity), reads=[f"pst{ti}"], writes=[f"GT{g6}"])

            def sb_pv(it):
                i, h, hp, hh, qb, kt, W, jq, a = it["i"], it["h"], it["hp"], it["hh"], it["qb"], it["kt"], it["W"], it["jq"], it["a"]
                sl = h % 4
                g6 = i % 6
                yb = 6 + (hp * NT + a) % 2
                ycol = slice(jq * 128, (jq + 1) * 128)
                yrow = slice(64 * hh, 64 * hh + 64)

                def mm(e):
                    if it["first"]:
                        e.matmul(ps[yb][yrow, ycol], lhsT=VS[sl][:, qb, 0:64], rhs=cb[:, IDENT, :], start=True, stop=False)
                    nsub = W // 128
                    for c in range(nsub):
                        ins = e.matmul(ps[yb][yrow, ycol], lhsT=dVV[sl][:, kt * 4 + c, :], rhs=GT[g6][:, c * 128:(c + 1) * 128],
                                       start=False, stop=(it["last"] and c == nsub - 1))
                    return ins
                op("pe", mm, reads=[f"VS{sl}", f"dVV{sl}", f"GT{g6}", "cb"], writes=[f"ps{yb}"])

            def sb_fin(it):
                if not (it["last"] and it["jq"] == 3 and it["hh"] == 1):
                    return
                hp, a = it["hp"], it["a"]
                t2 = (hp * NT + a) % 2
                yb = 6 + t2
                op("act", lambda e: e.activation(out=yo2[t2][:], in_=ps[yb][:, :], func=AF.Identity), reads=[f"ps{yb}"], writes=[f"yo2_{t2}"])
                op("sp", lambda e: e.dma_start(out=yT_d[1, hp * 128:(hp + 1) * 128, a * 512:(a + 1) * 512], in_=yo2[t2][:]),
                   reads=[f"yo2_{t2}"], writes=[f"yT_d1_{hp}_{a}"], dma=True)

            stages = [(sb_w, 1), (sb_scan, 2), (sb_tr, 3), (sb_ev, 4), (sb_pv, 5), (sb_fin, 6), (sb_z, 0)]
            for k in range(npair + 7):
                if k < npair and k % per_hp == 9 and k // per_hp + 1 < NH // 2:
                    sb_loads(k // per_hp + 1)
                    sb_vload(2 * (k // per_hp + 1))
                    sb_vload(2 * (k // per_hp + 1) + 1)
                for fn, sk in stages:
                    m = k - sk
                    if 0 <= m < npair:
                        fn(items[2 * m])
                        fn(items[2 * m + 1])
            sy.barrier()

        p012.close()

        def ln_tail(R, rk, g_t, b_t, st6, mv, sc2, g_eng="pool", b_eng="pool", sfx=""):
            ops = []
            for hh in range(2):
                ops.append(lambda hh=hh: op("dve", lambda e: e.bn_stats(out=st6[:, hh * 6:(hh + 1) * 6], in_=R[:, hh * 512:(hh + 1) * 512]),
                                            reads=[rk], writes=["st6" + sfx]))
            ops.append(lambda: op("dve", lambda e: e.bn_aggr(out=mv[:], in_=st6[:]), reads=["st6" + sfx], writes=["mv" + sfx]))
            ops.append(lambda: op("dve", lambda e: e.tensor_scalar_add(out=sc2[:, 0:1], in0=mv[:, 1:2], scalar1=EPS / (ALPHA * ALPHA)), reads=["mv" + sfx], writes=["sc2" + sfx]))
            ops.append(lambda: op("pool", lambda e: e.tensor_tensor(out=sc2[:, 0:1], in0=sc2[:, 0:1], in1=mhalf[:, 0:1], op=ALU.pow),
                                  reads=["sc2" + sfx, "mhalf"], writes=["sc2" + sfx]))
            ops.append(lambda: op("dve", lambda e: e.scalar_tensor_tensor(out=sc2[:, 1:2], in0=mv[:, 0:1], scalar=-1.0, in1=sc2[:, 0:1],
                                                                          op0=ALU.mult, op1=ALU.mult), reads=["mv" + sfx, "sc2" + sfx], writes=["sc2b" + sfx]))
            ops.append(lambda: op("act", lambda e: e.activation(out=R[:], in_=R[:], func=AF.Identity, bias=sc2[:, 1:2], scale=sc2[:, 0:1]),
                                  reads=[rk, "sc2" + sfx, "sc2b" + sfx], writes=[rk]))
            for q in range(4):
                qs = slice(q * 256, (q + 1) * 256)
                ops.append(lambda qs=qs: op(g_eng, lambda e: e.tensor_tensor(out=R[:, qs], in0=R[:, qs], in1=g_t[:, qs], op=ALU.mult),
                                            reads=[rk, "lnt"], writes=[rk]))
            for q in range(4):
                qs = slice(q * 256, (q + 1) * 256)
                ops.append(lambda qs=qs: op(b_eng, lambda e: e.tensor_tensor(out=R[:, qs], in0=R[:, qs], in1=b_t[:, qs], op=ALU.add),
                                            reads=[rk, "lnt"], writes=[rk]))
            return ops

        deferred = []

        def pop_deferred(n):
            for _ in range(n):
                if deferred:
                    deferred.pop(0)()

        with ExitStack() as p3:
            P = p3.enter_context
            wfb = P(nc.sbuf_tensor("wfb", [128, 4, D], BF16))
            wsbb = P(nc.sbuf_tensor("wsbb", [128, 4, D], BF16))
            wob = P(nc.sbuf_tensor("wob", [128, 8, D], BF16))
            lnt = P(nc.sbuf_tensor("lnt_sb", [128, 2, D], F32))
            gt1 = P(nc.sbuf_tensor("gtab1", [128, D], F32))
            yf = [P(nc.sbuf_tensor(f"yf{i}", [128, 4, 512], BF16)) for i in range(2)]
            ysl = [P(nc.sbuf_tensor(f"ysl{i}", [128, 4, 512], BF16)) for i in range(2)]
            gt = [P(nc.sbuf_tensor(f"gt{i}", [128, 16, 512], BF16)) for i in range(2)]
            xt = [P(nc.sbuf_tensor(f"xt3_{i}", [128, 4, D], F32)) for i in range(2)]
            m1 = [P(nc.sbuf_tensor(f"m1_{i}", [128, 512], F32)) for i in range(2)]
            m2 = [P(nc.sbuf_tensor(f"m2_{i}", [128, 512], F32)) for i in range(2)]
            mg = [P(nc.sbuf_tensor(f"mg{i}", [128, 8, 512], BF16)) for i in range(2)]
            Rb = [P(nc.sbuf_tensor(f"Rb{i}", [128, D], F32)) for i in range(4)]
            st6 = [P(nc.sbuf_tensor(f"st6_{i}", [128, 12], F32)) for i in range(4)]
            mv = [P(nc.sbuf_tensor(f"mv_{i}", [128, 2], F32)) for i in range(4)]
            sc2 = [P(nc.sbuf_tensor(f"sc2_{i}", [128, 2], F32)) for i in range(4)]
            for c in range(4):
                op("pool", lambda e, c=c: e.dma_start(out=wfb[:, c, :], in_=wfox_d[c * 128:(c + 1) * 128, :]), writes=["wfb"], dma=True)
                op("pool", lambda e, c=c: e.dma_start(out=wsbb[:, c, :], in_=wsb_d[c * 128:(c + 1) * 128, :]), writes=["wsbb"], dma=True)
            for c in range(8):
                op("pool", lambda e, c=c: e.dma_start(out=wob[:, c, :], in_=wo_d[c * 128:(c + 1) * 128, :]), writes=["wob"], dma=True)
            op("sp", lambda e: e.dma_start(out=lnt[:], in_=lnt_d[0:2].rearrange("c p n -> p c n")), writes=["lnt"], dma=True)
            op("sp", lambda e: e.dma_start(out=gt1[:], in_=gt_d[0]), writes=["gt1"], dma=True)

            def loadA(tt):
                sl = tt % 2
                tsl = slice(tt * 512, (tt + 1) * 512)
                op("sp", lambda e: e.dma_start(out=yf[sl][:], in_=yT_d[0, :, tsl].rearrange("(c p) t -> p c t", p=128)), writes=[f"yf{sl}"], dma=True)
                op("sp", lambda e: e.dma_start(out=ysl[sl][:], in_=yT_d[1, :, tsl].rearrange("(c p) t -> p c t", p=128)), writes=[f"ysl{sl}"], dma=True)
                op("sp", lambda e: e.dma_start(out=gt[sl][:], in_=fm_d[16:32, :, tsl].rearrange("c p t -> p c t")), writes=[f"gt{sl}"], dma=True)

            def loadX(tt):
                sl = tt % 2
                tsl = slice(tt * 512, (tt + 1) * 512)
                op("sp", lambda e: e.dma_start(out=xt[sl][:], in_=x_d[tsl, :].rearrange("(j p) d -> p j d", p=128)), writes=[f"xt3_{sl}"], dma=True)

            c3 = {"cnt": 0}

            def OC(tt):
                sl = tt % 2
                MG = mg[sl]
                for oc in range(8):
                    cnt = c3["cnt"]
                    ba, bb = (2 * cnt) % 4, (2 * cnt + 1) % 4
                    mi = cnt % 2
                    c3["cnt"] += 1

                    def mmab(e, oc=oc, ba=ba, bb=bb, sl=sl):
                        for c in range(4):
                            e.matmul(ps[ba][:], lhsT=wfb[:, c, oc * 128:(oc + 1) * 128], rhs=yf[sl][:, c, :], start=(c == 0), stop=(c == 3))
                        for c in range(4):
                            ins = e.matmul(ps[bb][:], lhsT=wsbb[:, c, oc * 128:(oc + 1) * 128], rhs=ysl[sl][:, c, :], start=(c == 0), stop=(c == 3))
                        return ins
                    op("pe", mmab, reads=["wfb", "wsbb", f"yf{sl}", f"ysl{sl}"], writes=[f"ps{ba}", f"ps{bb}"])
                    op("dve", lambda e, oc=oc, ba=ba, mi=mi, sl=sl: e.tensor_tensor(out=m1[mi][:], in0=ps[ba][:], in1=gt[sl][:, oc, :], op=ALU.mult),
                       reads=[f"ps{ba}", f"gt{sl}"], writes=[f"m1_{mi}"])
                    op("dve", lambda e, oc=oc, bb=bb, mi=mi, sl=sl: e.tensor_tensor(out=m2[mi][:], in0=ps[bb][:], in1=gt[sl][:, 8 + oc, :], op=ALU.mult),
                       reads=[f"ps{bb}", f"gt{sl}"], writes=[f"m2_{mi}"])
                    op("pool", lambda e, oc=oc, mi=mi, MG=MG: e.tensor_tensor(out=MG[:, oc, :], in0=m1[mi][:], in1=m2[mi][:], op=ALU.add),
                       reads=[f"m1_{mi}", f"m2_{mi}"], writes=[f"mg{sl}"])
                    pop_deferred(9)

            def MO(tt):
                sl = tt % 2
                MG = mg[sl]
                chains = []
                for j in range(4):
                    blk = tt * 4 + j
                    R = Rb[j]
                    rk = f"Rb{j}"
                    for hh in range(2):
                        b = 4 + (2 * blk + hh) % 4

                        def mmo(e, j=j, hh=hh, b=b, MG=MG):
                            for oc in range(8):
                                ins = e.matmul(ps[b][:], lhsT=MG[:, oc, j * 128:(j + 1) * 128], rhs=wob[:, oc, hh * 512:(hh + 1) * 512],
                                               start=(oc == 0), stop=(oc == 7))
                            return ins
                        op("pe", mmo, reads=[f"mg{sl}", "wob"], writes=[f"ps{b}"])
                        op("dve", lambda e, hh=hh, b=b, R=R: e.tensor_tensor(out=R[:, hh * 512:(hh + 1) * 512], in0=ps[b][:],
                                                                              in1=gt1[:, hh * 512:(hh + 1) * 512], op=ALU.mult),
                           reads=[f"ps{b}", "gt1"], writes=[rk])
                    op("pool", lambda e, j=j, R=R, sl=sl: e.tensor_tensor(out=R[:], in0=R[:], in1=xt[sl][:, j, :], op=ALU.add),
                       reads=[f"xt3_{sl}", rk], writes=[rk])
                    ch = ln_tail(R, rk, lnt[:, 0, :], lnt[:, 1, :], st6[j], mv[j], sc2[j], g_eng="dve", sfx=f"_{j}")
                    ch.append(lambda blk=blk, R=R, rk=rk: op("sp", lambda e: e.dma_start(out=x1_d[blk * 128:(blk + 1) * 128, :], in_=R[:]),
                                                             reads=[rk], writes=[f"x1_d{blk}"], dma=True))
                    chains.append(ch)
                for k in range(max(len(c) for c in chains)):
                    for c in chains:
                        if k < len(c):
                            deferred.append(c[k])

            loadA(0)
            loadX(0)
            if NT > 1:
                loadA(1)
            OC(0)
            for tt in range(NT):
                if tt + 2 < NT:
                    loadA(tt + 2)
                if tt + 1 < NT:
                    loadX(tt + 1)
                    OC(tt + 1)
                else:
                    pop_deferred(len(deferred))
                MO(tt)
            pop_deferred(len(deferred))
            sy.barrier()

        TW = 256
        NT2 = S // TW
        JB = TW // 128
        with ExitStack() as p4:
            P = p4.enter_context
            wupb = P(nc.sbuf_tensor("wupb", [128, 8, 2 * DFF], BF16))
            wdb = P(nc.sbuf_tensor("wdb", [128, NAC, D], BF16))
            lnt = P(nc.sbuf_tensor("lnt2", [128, 2, D], F32))
            gt2 = P(nc.sbuf_tensor("gtab2", [128, D], F32))
            convp = P(nc.sbuf_tensor("convp_sb", [128, NFC, 4], F32))
            halo = [P(nc.sbuf_tensor(f"halo{i}", [128, NFC, 2], F32)) for i in range(2)]
            xt = [P(nc.sbuf_tensor(f"xt4_{i}", [128, JB, D], F32)) for i in range(2)]
            uT = [P(nc.sbuf_tensor(f"u2T{i}", [128, 8, TW], BF16)) for i in range(2)]
            hb = [P(nc.sbuf_tensor(f"hb{i}", [128, TW + 2], F32)) for i in range(2)]
            cv = [P(nc.sbuf_tensor(f"cv{i}", [128, TW], F32)) for i in range(3)]
            sg = [P(nc.sbuf_tensor(f"sg{i}", [128, TW], F32)) for i in range(2)]
            act = [P(nc.sbuf_tensor(f"act{i}", [128, NAC, TW], BF16)) for i in range(2)]
            Rb = [P(nc.sbuf_tensor(f"Rb4_{i}", [128, D], F32)) for i in range(2)]
            st6 = P(nc.sbuf_tensor("st6b", [128, 12], F32))
            mv = P(nc.sbuf_tensor("mvb", [128, 2], F32))
            sc2 = P(nc.sbuf_tensor("sc2b_", [128, 2], F32))
            HF = (NAC // 2) * 128
            for hv, key in ((0, "wupbA"), (1, "wupbB")):
                for c in range(8):
                    op("pool", lambda e, c=c, hv=hv: e.dma_start(
                        out=wupb[:, c, :].rearrange("p (g n) -> p g n", g=2)[:, :, hv * HF:(hv + 1) * HF],
                        in_=wup_d[c * 128:(c + 1) * 128, :].rearrange("p (g n) -> p g n", g=2)[:, :, hv * HF:(hv + 1) * HF]),
                       writes=[key], dma=True)
            pend_wd = [(lambda c=c: op("pool", lambda e: e.dma_start(out=wdb[:, c, :], in_=wdown_d[c * 128:(c + 1) * 128, :]),
                                       writes=["wdb"], dma=True)) for c in range(NAC)]
            op("sp", lambda e: e.dma_start(out=lnt[:], in_=lnt_d[2:4].rearrange("c p n -> p c n")), writes=["lnt"], dma=True)
            op("sp", lambda e: e.dma_start(out=gt2[:], in_=gt_d[1]), writes=["gt2"], dma=True)
            op("sp", lambda e: e.dma_start(out=convp[:], in_=convp_d), writes=["convp"], dma=True)
            op("dve", lambda e: e.memset(halo[1][:], 0.0), writes=["halo1"])

            def loads4(tt):
                sl = tt % 2
                op("sp", lambda e: e.dma_start(out=xt[sl][:], in_=x1_d[tt * TW:(tt + 1) * TW, :].rearrange("(j p) d -> p j d", p=128)),
                   writes=[f"xt4_{sl}"], dma=True)
            cnts = {"pc": 0, "hc": 0}

            def T4(tt):
                sl = tt % 2
                X, U = xt[sl], uT[sl]
                for kc in range(8):
                    b = cnts["pc"] % 4
                    cnts["pc"] += 1

                    def tr(e, kc=kc, b=b, X=X):
                        for j in range(JB):
                            ins = e.transpose(out=ps[b][:, j * 128:(j + 1) * 128], in_=X[:, j, kc * 128:(kc + 1) * 128], identity=identf[:])
                        return ins
                    op("pe", tr, reads=[f"xt4_{sl}", "identf"], writes=[f"ps{b}"])
                    op("act", lambda e, kc=kc, b=b, U=U: e.activation(out=U[:, kc, :], in_=ps[b][:, 0:TW], func=AF.Identity,
                                                                      bias=modp[:, 2, kc:kc + 1], scale=modp[:, 3, kc:kc + 1]),
                       reads=[f"ps{b}", "modp"], writes=[f"u2T{sl}"])

            def U4(tt):
                sl = tt % 2
                U, A = uT[sl], act[sl]
                uk = f"u2T{sl}"
                base = cnts["hc"]
                cnts["hc"] += 2 * NAC

                def stA(c):
                    i, half = c // 2, c % 2
                    fc = i + half * NAC
                    b = cnts["pc"] % 4
                    cnts["pc"] += 1
                    hi = (base + c) % 2
                    ci = (base + c) % 3
                    H, CV = hb[hi], cv[ci]

                    def mmu(e):
                        for kc in range(8):
                            ins = e.matmul(ps[b][:, 0:TW], lhsT=wupb[:, kc, fc * 128:(fc + 1) * 128], rhs=U[:, kc, :], start=(kc == 0), stop=(kc == 7))
                        return ins
                    op("pe", mmu, reads=["wupbA" if i < NAC // 2 else "wupbB", uk], writes=[f"ps{b}"])
                    op("act", lambda e: e.activation(out=H[:, 2:TW + 2], in_=ps[b][:, 0:TW], func=AF.Identity), reads=[f"ps{b}"], writes=[f"hb{hi}"])
                    op("act", lambda e: e.activation(out=CV[:], in_=ps[b][:, 0:TW], func=AF.Identity, bias=convp[:, fc, 3:4], scale=convp[:, fc, 2:3]),
                       reads=[f"ps{b}", "convp"], writes=[f"cv{ci}"])
                    op("act", lambda e: e.activation(out=halo[sl][:, fc, :], in_=ps[b][:, TW - 2:TW], func=AF.Identity),
                       reads=[f"ps{b}"], writes=[f"halo{sl}"])
                    op("pool", lambda e: e.tensor_copy(out=H[:, 0:2], in_=halo[1 - sl][:, fc, :]), reads=[f"halo{1 - sl}"], writes=[f"hbh{hi}"])

                def stB(c):
                    i, half = c // 2, c % 2
                    fc = i + half * NAC
                    hi = (base + c) % 2
                    ci = (base + c) % 3
                    H, CV = hb[hi], cv[ci]
                    op("dve", lambda e: e.scalar_tensor_tensor(out=CV[:], in0=H[:, 1:TW + 1], scalar=convp[:, fc, 1:2], in1=CV[:],
                                                               op0=ALU.mult, op1=ALU.add),
                       reads=[f"hb{hi}", f"hbh{hi}", "convp", f"cv{ci}"], writes=[f"cv{ci}"])
                    op("dve", lambda e: e.scalar_tensor_tensor(out=CV[:], in0=H[:, 0:TW], scalar=convp[:, fc, 0:1], in1=CV[:],
                                                               op0=ALU.mult, op1=ALU.add),
                       reads=[f"hb{hi}", f"hbh{hi}", "convp", f"cv{ci}"], writes=[f"cv{ci}"])

                def stC(c):
                    i, half = c // 2, c % 2
                    ci = (base + c) % 3
                    CV = cv[ci]
                    SG = sg[i % 2]
                    if half == 0:
                        op("act", lambda e: e.activation(out=SG[:], in_=CV[:], func=AF.Silu), reads=[f"cv{ci}"], writes=[f"sg{i % 2}"])
                    else:
                        op("pool", lambda e: e.tensor_tensor(out=A[:, i, :], in0=SG[:], in1=CV[:], op=ALU.mult),
                           reads=[f"cv{ci}", f"sg{i % 2}"], writes=[f"act{sl}"])

                nchunk = 2 * NAC
                for st_ in range(nchunk + 2):
                    if st_ < nchunk:
                        stA(st_)
                    if 0 <= st_ - 1 < nchunk:
                        stB(st_ - 1)
                    if 0 <= st_ - 2 < nchunk:
                        stC(st_ - 2)
                    pop_deferred(1)
                    if pend_wd and st_ % 2 == 1:
                        pend_wd.pop(0)()
                while pend_wd:
                    pend_wd.pop(0)()

            def D4(tt):
                sl = tt % 2
                X, A = xt[sl], act[sl]
                for j in range(JB):
                    blk = tt * JB + j
                    R = Rb[blk % 2]
                    rk = f"Rb{blk % 2}"
                    for hh in range(2):
                        b = 4 + (2 * blk + hh) % 4

                        def mmd2(e, j=j, hh=hh, b=b, A=A):
                            for i in range(NAC):
                                ins = e.matmul(ps[b][:], lhsT=A[:, i, j * 128:(j + 1) * 128], rhs=wdb[:, i, hh * 512:(hh + 1) * 512],
                                               start=(i == 0), stop=(i == NAC - 1))
                            return ins
                        op("pe", mmd2, reads=[f"act{sl}", "wdb"], writes=[f"ps{b}"])
                        op("dve", lambda e, hh=hh, b=b, R=R: e.tensor_tensor(out=R[:, hh * 512:(hh + 1) * 512], in0=ps[b][:],
                                                                              in1=gt2[:, hh * 512:(hh + 1) * 512], op=ALU.mult),
                           reads=[f"ps{b}", "gt2"], writes=[rk])
                    op("pool", lambda e, j=j, R=R, X=X: e.tensor_tensor(out=R[:], in0=R[:], in1=X[:, j, :], op=ALU.add),
                       reads=[f"xt4_{sl}", rk], writes=[rk])
                    deferred.extend(ln_tail(R, rk, lnt[:, 0, :], lnt[:, 1, :], st6, mv, sc2, g_eng="dve", b_eng="dve"))
                    deferred.append(lambda blk=blk, R=R, rk=rk: op("sp", lambda e: e.dma_start(out=out_d[blk * 128:(blk + 1) * 128, :], in_=R[:]),
                                                                   reads=[rk], writes=[f"out_d{blk}"], dma=True))

            loads4(0)
            T4(0)
            for tt in range(NT2):
                if tt + 1 < NT2:
                    loads4(tt + 1)
                U4(tt)
                if tt + 1 < NT2:
                    T4(tt + 1)
                D4(tt)
            pop_deferred(len(deferred))
            sy.barrier()
    return nc


def _consts():
    r = np.arange(128)[:, None]
    c = np.arange(128)[None, :]
    dm = (r == c - 1).astype(np.float32) - (r == c).astype(np.float32)
    eb = ((r == 127) & (c == 0)).astype(np.float32)
    six = np.stack([(r == c), (r <= c), (r >= c), (r > c), (r < c), np.ones((128, 128), bool)]).astype(np.float32)
    return np.concatenate([six, dm[None], eb[None]], axis=0)


def prep_shared(w_ada, b_ada, w_in, b_forget, w_fox_proj, w_sb_proj, w_o, ln1_g, ln1_b, w_up, conv_w, conv_b, w_down, ln2_g, ln2_b):
    f = lambda a: np.ascontiguousarray(np.asarray(a, dtype=np.float32))
    w_in = f(w_in)
    w_in_r = np.concatenate([w_in[:, 0:512], w_in[:, 512:1024], w_in[:, 1544:2056], w_in[:, 2056:2568], w_in[:, 3080:4104],
                             w_in[:, 4104:5128], w_in[:, 1024:1536], w_in[:, 2568:3080], w_in[:, 1536:1544]], axis=1)
    convp = np.stack([f(conv_w)[0], f(conv_w)[1], f(conv_w)[2], f(conv_b)], axis=1).reshape(NFC, 128, 4).transpose(1, 0, 2)
    lnt = np.stack([np.broadcast_to(f(v)[None, :], (128, D)) for v in (ln1_g, ln1_b, ln2_g, ln2_b)])
    return {
        "w_ada": f(w_ada), "b_ada": f(b_ada).reshape(12, 1, 512), "w_in": f(w_in_r),
        "bft": f(np.broadcast_to(np.tile(f(b_forget), 4)[None, :], (128, 32))),
        "w_fox": f(w_fox_proj), "w_sb": f(w_sb_proj), "w_o": f(w_o), "lnt": f(lnt), "w_up": f(w_up),
        "convp": f(convp), "w_down": f(w_down), "consts": _consts(),
    }


def kernel(x, c, w_ada, b_ada, w_in, b_forget, w_fox_proj, w_sb_proj, w_o, ln1_g, ln1_b, w_up, conv_w, conv_b, w_down, ln2_g, ln2_b):
    x = np.asarray(x, dtype=np.float32)
    c = np.asarray(c, dtype=np.float32)
    B, S, _ = x.shape
    shared = prep_shared(w_ada, b_ada, w_in, b_forget, w_fox_proj, w_sb_proj, w_o, ln1_g, ln1_b, w_up, conv_w, conv_b, w_down, ln2_g, ln2_b)
    nc = build_nc(S)
    in_maps = []
    for b in range(B):
        m = dict(shared)
        m["x"] = np.ascontiguousarray(x[b])
        m["cT"] = np.ascontiguousarray(c[b].reshape(8, 128).T)
        in_maps.append(m)
    res = run_bass_kernel_spmd(nc, in_maps, core_ids=list(range(B)))
    return np.stack([r["out"] for r in res.results], axis=0).astype(np.float32)
```
